# Optimizing a Trainium2 kernel written in Bass

```python
import math
import jax, jax.numpy as jnp
from jax import lax
import numpy as np

D_MODEL = 2048
BATCH = 4
SEQ = 4096
DEPTH = 4

D_FF = 5632
NORM_EPS = 1e-5
NEG_INF = -1e30
A_HEADS = 16
A_KV_HEADS = 4
HEAD_DIM = 64
WINDOW = 128
ATTN_BLOCK = 128
ROPE_THETA = 500000.0
ROPE_DIM = HEAD_DIM // 4
A_WIDTH = A_HEADS * HEAD_DIM
KV_WIDTH = A_KV_HEADS * HEAD_DIM
S5_WIDTH = D_MODEL // 2
S5_GROUP = 16
S5_GROUPS = S5_WIDTH // S5_GROUP
S5_STATE = 64
EVEN_IN = A_WIDTH + 2 * KV_WIDTH + S5_WIDTH
EVEN_OUT = A_WIDTH + S5_WIDTH
M_INNER = 2 * D_MODEL
M_HEAD_DIM = 64
M_HEADS = M_INNER // M_HEAD_DIM
M_GROUPS = 8
M_STATE = 128
M_CONV = 4
M_CHUNK = 128
M_CONV_DIM = M_INNER + 2 * M_GROUPS * M_STATE
M_IN = M_INNER + M_CONV_DIM + M_HEADS
N_EVEN = (DEPTH + 1) // 2
N_ODD = DEPTH // 2

kernel_name = "hybrid_swa_s5_ssd_macaron"


def rms_norm(x, g):
    xf = x.astype(jnp.float32)
    xf = xf * lax.rsqrt(jnp.mean(xf * xf, axis=-1, keepdims=True) + NORM_EPS)
    return xf.astype(x.dtype) * g


def swiglu_ffn(x, w_gate, w_up, w_down):
    return (jax.nn.silu(x @ w_gate) * (x @ w_up)) @ w_down


def partial_rotary(t, positions):
    half = ROPE_DIM // 2
    inv_freq = jnp.exp(-math.log(ROPE_THETA) * jnp.arange(half, dtype=jnp.float32) * (2.0 / ROPE_DIM))
    ang = positions.astype(jnp.float32)[:, :, None] * inv_freq
    cos = jnp.cos(ang)[:, :, None, :]
    sin = jnp.sin(ang)[:, :, None, :]
    tr = t[..., :ROPE_DIM].astype(jnp.float32)
    t1, t2 = tr[..., :half], tr[..., half:]
    rot = jnp.concatenate([t1 * cos - t2 * sin, t2 * cos + t1 * sin], axis=-1).astype(t.dtype)
    return jnp.concatenate([rot, t[..., ROPE_DIM:]], axis=-1)


def sliding_window_attention(q, k, v, sinks):
    b, s, _, hd = q.shape
    nb = s // ATTN_BLOCK
    grp = A_HEADS // A_KV_HEADS
    qb = q.reshape(b, nb, ATTN_BLOCK, A_KV_HEADS, grp, hd)

    def banded(t):
        tp = jnp.pad(t, ((0, 0), (ATTN_BLOCK, 0), (0, 0), (0, 0)))
        prev = tp[:, :s].reshape(b, nb, ATTN_BLOCK, A_KV_HEADS, hd)
        cur = t.reshape(b, nb, ATTN_BLOCK, A_KV_HEADS, hd)
        return jnp.concatenate([prev, cur], axis=2)

    kb, vb = banded(k), banded(v)
    scores = jnp.einsum('bnqhgd,bnchd->bnhgqc', qb, kb).astype(jnp.float32) * (1.0 / math.sqrt(hd))
    qpos = jnp.arange(ATTN_BLOCK)[:, None] + ATTN_BLOCK
    cpos = jnp.arange(2 * ATTN_BLOCK)[None, :]
    rel = qpos - cpos
    band = (rel >= 0) & (rel < WINDOW)
    blk_start = jnp.arange(nb)[:, None, None] * ATTN_BLOCK
    valid = band[None] & (blk_start + cpos[None] - ATTN_BLOCK >= 0)
    scores = jnp.where(valid[None, :, None, None], scores, NEG_INF)
    sink = sinks.astype(jnp.float32).reshape(A_KV_HEADS, grp)[None, None, :, :, None, None]
    sink = jnp.broadcast_to(sink, scores.shape[:-1] + (1,))
    probs = jax.nn.softmax(jnp.concatenate([scores, sink], axis=-1), axis=-1)[..., :-1]
    out = jnp.einsum('bnhgqc,bnchd->bnqhgd', probs.astype(v.dtype), vb)
    return out.reshape(b, s, A_HEADS * hd)


def s5_mixer(u, a_re, a_im, log_dt, b_re, b_im, c_re, c_im, d_skip, w_glu, b_glu):
    f32 = jnp.float32
    bsz, s, _ = u.shape
    uf = u.astype(f32).reshape(bsz, s, S5_GROUPS, S5_GROUP)
    are, aim = a_re.astype(f32), a_im.astype(f32)
    dt = jnp.exp(log_dt.astype(f32))[:, None]
    mag = jnp.exp(are * dt)
    abar_re, abar_im = mag * jnp.cos(aim * dt), mag * jnp.sin(aim * dt)
    nr, ni = abar_re - 1.0, abar_im
    den = are * are + aim * aim
    coef_re = (nr * are + ni * aim) / den
    coef_im = (ni * are - nr * aim) / den
    bre, bim = b_re.astype(f32), b_im.astype(f32)
    bbar_re = coef_re[..., None] * bre - coef_im[..., None] * bim
    bbar_im = coef_re[..., None] * bim + coef_im[..., None] * bre
    bu_re = jnp.einsum('gpc,bsgc->bsgp', bbar_re, uf)
    bu_im = jnp.einsum('gpc,bsgc->bsgp', bbar_im, uf)
    a_re_t = jnp.broadcast_to(abar_re, (s, S5_GROUPS, S5_STATE))
    a_im_t = jnp.broadcast_to(abar_im, (s, S5_GROUPS, S5_STATE))

    def combine(left, right):
        a1r, a1i, b1r, b1i = left
        a2r, a2i, b2r, b2i = right
        return (a1r * a2r - a1i * a2i, a1r * a2i + a1i * a2r,
                a2r * b1r - a2i * b1i + b2r, a2r * b1i + a2i * b1r + b2i)

    def scan_one(br, bi):
        _, _, hr, hi = lax.associative_scan(combine, (a_re_t, a_im_t, br, bi), axis=0)
        return hr, hi

    h_re, h_im = jax.vmap(scan_one)(bu_re, bu_im)
    y = (jnp.einsum('gcp,bsgp->bsgc', c_re.astype(f32), h_re)
         - jnp.einsum('gcp,bsgp->bsgc', c_im.astype(f32), h_im))
    y = (y + d_skip.astype(f32) * uf).reshape(bsz, s, S5_WIDTH)
    g = jax.nn.gelu(y)
    out = g * jax.nn.sigmoid(g @ w_glu.astype(f32) + b_glu.astype(f32))
    return out.astype(u.dtype)


def attn_s5_mixer(h, positions, w_in, sinks, a_re, a_im, log_dt, b_re, b_im, c_re, c_im,
                  d_skip, w_glu, b_glu, w_out):
    b, s, _ = h.shape
    proj = h @ w_in
    q = proj[..., :A_WIDTH].reshape(b, s, A_HEADS, HEAD_DIM)
    k = proj[..., A_WIDTH:A_WIDTH + KV_WIDTH].reshape(b, s, A_KV_HEADS, HEAD_DIM)
    v = proj[..., A_WIDTH + KV_WIDTH:A_WIDTH + 2 * KV_WIDTH].reshape(b, s, A_KV_HEADS, HEAD_DIM)
    u = proj[..., A_WIDTH + 2 * KV_WIDTH:]
    q = partial_rotary(q, positions)
    k = partial_rotary(k, positions)
    attn = sliding_window_attention(q, k, v, sinks)
    ssm = s5_mixer(u, a_re, a_im, log_dt, b_re, b_im, c_re, c_im, d_skip, w_glu, b_glu)
    return jnp.concatenate([attn, ssm], axis=-1) @ w_out


def ssd_chunked(x, dt, a, bm, cm):
    bsz, s, _, p = x.shape
    nc = s // M_CHUNK
    hpg = M_HEADS // M_GROUPS
    xc = (x * dt[..., None]).reshape(bsz, nc, M_CHUNK, M_GROUPS, hpg, p)
    bc = bm.reshape(bsz, nc, M_CHUNK, M_GROUPS, M_STATE)
    cc = cm.reshape(bsz, nc, M_CHUNK, M_GROUPS, M_STATE)
    adt = (a * dt).reshape(bsz, nc, M_CHUNK, M_GROUPS, hpg).transpose(0, 1, 3, 4, 2)
    a_cum = jnp.cumsum(adt, axis=-1)
    seg = a_cum[..., :, None] - a_cum[..., None, :]
    causal = jnp.tril(jnp.ones((M_CHUNK, M_CHUNK), dtype=bool))
    decay = jnp.exp(jnp.where(causal, seg, -jnp.inf))
    cb = jnp.einsum('bclgn,bcsgn->bcgls', cc, bc)
    y_diag = jnp.einsum('bcgls,bcgjls,bcsgjp->bclgjp', cb, decay, xc)
    decay_states = jnp.exp(a_cum[..., -1:] - a_cum)
    states = jnp.einsum('bclgn,bcgjl,bclgjp->bcgjpn', bc, decay_states, xc)
    chunk_decay = jnp.exp(a_cum[..., -1])

    def step(carry, inp):
        st, dec = inp
        return carry * dec[..., None, None] + st, carry

    init = jnp.zeros((bsz, M_GROUPS, hpg, p, M_STATE), jnp.float32)
    _, prev = lax.scan(step, init, (jnp.moveaxis(states, 1, 0), jnp.moveaxis(chunk_decay, 1, 0)))
    prev = jnp.moveaxis(prev, 0, 1)
    y_off = jnp.einsum('bclgn,bcgjpn,bcgjl->bclgjp', cc, prev, jnp.exp(a_cum))
    return (y_diag + y_off).reshape(bsz, s, M_HEADS, p)


def mamba2_mixer(h, w_in, conv_w, conv_b, dt_bias, a_log, d_skip, norm_g, w_out):
    f32 = jnp.float32
    bsz, s, _ = h.shape
    zxbcdt = h @ w_in
    z = zxbcdt[..., :M_INNER]
    xbc = zxbcdt[..., M_INNER:M_INNER + M_CONV_DIM]
    dt_raw = zxbcdt[..., M_INNER + M_CONV_DIM:]
    xpad = jnp.pad(xbc, ((0, 0), (M_CONV - 1, 0), (0, 0)))
    conv = conv_b
    for tap in range(M_CONV):
        conv = conv + conv_w[tap] * xpad[:, tap:tap + s]
    xbc = jax.nn.silu(conv)
    xs = xbc[..., :M_INNER].reshape(bsz, s, M_HEADS, M_HEAD_DIM).astype(f32)
    bm = xbc[..., M_INNER:M_INNER + M_GROUPS * M_STATE].reshape(bsz, s, M_GROUPS, M_STATE).astype(f32)
    cm = xbc[..., M_INNER + M_GROUPS * M_STATE:].reshape(bsz, s, M_GROUPS, M_STATE).astype(f32)
    dt = jax.nn.softplus(dt_raw.astype(f32) + dt_bias.astype(f32))
    a = -jnp.exp(a_log.astype(f32))
    y = ssd_chunked(xs, dt, a, bm, cm) + d_skip.astype(f32)[:, None] * xs
    y = y.reshape(bsz, s, M_INNER) * jax.nn.silu(z.astype(f32))
    y = y.reshape(bsz, s, M_GROUPS, M_INNER // M_GROUPS)
    y = y * lax.rsqrt(jnp.mean(y * y, axis=-1, keepdims=True) + NORM_EPS)
    y = y.reshape(bsz, s, M_INNER).astype(h.dtype) * norm_g
    return y @ w_out


def setup_inputs(seed: int = 0) -> dict:
    key = jax.random.key(seed)
    ks = iter(jax.random.split(key, 48))
    f32 = jnp.float32

    def nrm(shape, scale):
        return jax.random.normal(next(ks), shape, f32) * scale

    def gain(shape):
        return 1.0 + nrm(shape, 0.02)

    x = jax.random.normal(next(ks), (BATCH, SEQ, D_MODEL), f32)
    positions = jnp.broadcast_to(jnp.arange(SEQ, dtype=jnp.int32)[None, :], (BATCH, SEQ))
    a_im0 = jnp.pi * jnp.arange(S5_STATE, dtype=f32)
    dt0 = jnp.exp(jax.random.uniform(next(ks), (N_ODD, M_HEADS), f32, math.log(1e-3), math.log(1e-1)))
    return {
        "x": x,
        "positions": positions,
        "norm_ffn1": gain((DEPTH, D_MODEL)),
        "ffn1_gate": nrm((DEPTH, D_MODEL, D_FF), D_MODEL ** -0.5),
        "ffn1_up": nrm((DEPTH, D_MODEL, D_FF), D_MODEL ** -0.5),
        "ffn1_down": nrm((DEPTH, D_FF, D_MODEL), D_FF ** -0.5),
        "norm_mix": gain((DEPTH, D_MODEL)),
        "norm_ffn2": gain((DEPTH, D_MODEL)),
        "ffn2_gate": nrm((DEPTH, D_MODEL, D_FF), D_MODEL ** -0.5),
        "ffn2_up": nrm((DEPTH, D_MODEL, D_FF), D_MODEL ** -0.5),
        "ffn2_down": nrm((DEPTH, D_FF, D_MODEL), D_FF ** -0.5),
        "ev_w_in": nrm((N_EVEN, D_MODEL, EVEN_IN), D_MODEL ** -0.5),
        "ev_sinks": nrm((N_EVEN, A_HEADS), 0.5),
        "s5_a_re": -0.5 + nrm((N_EVEN, S5_GROUPS, S5_STATE), 0.01),
        "s5_a_im": a_im0 + nrm((N_EVEN, S5_GROUPS, S5_STATE), 0.01),
        "s5_log_dt": jax.random.uniform(next(ks), (N_EVEN, S5_GROUPS), f32, math.log(1e-3), math.log(1e-1)),
        "s5_b_re": nrm((N_EVEN, S5_GROUPS, S5_STATE, S5_GROUP), (2 * S5_GROUP) ** -0.5),
        "s5_b_im": nrm((N_EVEN, S5_GROUPS, S5_STATE, S5_GROUP), (2 * S5_GROUP) ** -0.5),
        "s5_c_re": nrm((N_EVEN, S5_GROUPS, S5_GROUP, S5_STATE), S5_STATE ** -0.5),
        "s5_c_im": nrm((N_EVEN, S5_GROUPS, S5_GROUP, S5_STATE), S5_STATE ** -0.5),
        "s5_d": nrm((N_EVEN, S5_GROUPS, S5_GROUP), 1.0),
        "s5_w_glu": nrm((N_EVEN, S5_WIDTH, S5_WIDTH), S5_WIDTH ** -0.5),
        "s5_b_glu": nrm((N_EVEN, S5_WIDTH), 0.01),
        "ev_w_out": nrm((N_EVEN, EVEN_OUT, D_MODEL), EVEN_OUT ** -0.5),
        "m_w_in": nrm((N_ODD, D_MODEL, M_IN), D_MODEL ** -0.5),
        "m_conv_w": nrm((N_ODD, M_CONV, M_CONV_DIM), M_CONV ** -0.5),
        "m_conv_b": nrm((N_ODD, M_CONV_DIM), 0.01),
        "m_dt_bias": dt0 + jnp.log(-jnp.expm1(-dt0)),
        "m_a_log": jnp.log(jax.random.uniform(next(ks), (N_ODD, M_HEADS), f32, 1.0, 16.0)),
        "m_d": gain((N_ODD, M_HEADS)),
        "m_norm": gain((N_ODD, M_INNER)),
        "m_w_out": nrm((N_ODD, M_INNER, D_MODEL), M_INNER ** -0.5),
        "final_norm": gain((D_MODEL,)),
    }


def reference(x, positions, norm_ffn1, ffn1_gate, ffn1_up, ffn1_down, norm_mix,
              norm_ffn2, ffn2_gate, ffn2_up, ffn2_down,
              ev_w_in, ev_sinks, s5_a_re, s5_a_im, s5_log_dt, s5_b_re, s5_b_im,
              s5_c_re, s5_c_im, s5_d, s5_w_glu, s5_b_glu, ev_w_out,
              m_w_in, m_conv_w, m_conv_b, m_dt_bias, m_a_log, m_d, m_norm, m_w_out,
              final_norm):
    for layer in range(DEPTH):
        x = x + 0.5 * swiglu_ffn(rms_norm(x, norm_ffn1[layer]), ffn1_gate[layer], ffn1_up[layer], ffn1_down[layer])
        hn = rms_norm(x, norm_mix[layer])
        if layer % 2 == 0:
            e = layer // 2
            mix = attn_s5_mixer(hn, positions, ev_w_in[e], ev_sinks[e], s5_a_re[e], s5_a_im[e],
                                s5_log_dt[e], s5_b_re[e], s5_b_im[e], s5_c_re[e], s5_c_im[e],
                                s5_d[e], s5_w_glu[e], s5_b_glu[e], ev_w_out[e])
        else:
            o = layer // 2
            mix = mamba2_mixer(hn, m_w_in[o], m_conv_w[o], m_conv_b[o], m_dt_bias[o], m_a_log[o],
                               m_d[o], m_norm[o], m_w_out[o])
        x = x + mix
        x = x + 0.5 * swiglu_ffn(rms_norm(x, norm_ffn2[layer]), ffn2_gate[layer], ffn2_up[layer], ffn2_down[layer])
    return rms_norm(x, final_norm)
```

```python
import contextlib
import math
import numpy as np
import concourse.bass as bass
import concourse.mybir as mybir
from concourse.bass_utils import run_bass_kernel_spmd

F32 = mybir.dt.float32
BF16 = mybir.dt.bfloat16
I32 = mybir.dt.int32
U8 = mybir.dt.uint8
AF = mybir.ActivationFunctionType
ALU = mybir.AluOpType

D = 2048
DFF = 5632
DEPTH = 4
EPS = 1e-5
NKC = D // 128
NFC = DFF // 128
TT = 512

ENGS = ("pe", "act", "dve", "pool", "sp")
DMA_K = 8
SEM_PHASE = 30000


class Sched:
    def __init__(self, nc):
        self.nc = nc
        self.prog = {e: [] for e in ENGS}
        self.last_w = {}
        self.readers = {}
        self.ndma = {e: 0 for e in ENGS}
        self.bar_deps = set()
        self.bar_pending = {e: False for e in ENGS}
        self.lastc = {}
        self.lastd = {}

    def barrier(self):
        deps = set()
        for e in ENGS:
            if self.lastc.get(e) is not None:
                deps.add((e, self.lastc[e]))
            for i in self.lastd.get(e, []):
                deps.add((e, i))
        self.bar_deps = deps
        self.bar_pending = {e: True for e in ENGS}
        self.last_w = {}
        self.readers = {}

    def _record(self, eng, kind, method, args, kwargs, reads, writes):
        px = [r for r in reads if isinstance(r, str) and r.startswith("psum")]
        if px:
            reads = [r for r in reads if r not in px]
            writes = list(writes) + px
        idx = len(self.prog[eng])
        deps = set()
        if self.bar_pending[eng]:
            deps |= self.bar_deps
            self.bar_pending[eng] = False
        for r in reads:
            w = self.last_w.get(r)
            if w is not None:
                deps.add(w)
        for r in writes:
            w = self.last_w.get(r)
            if w is not None:
                deps.add(w)
            rd = self.readers.get(r)
            if rd:
                for e2, i2 in rd.items():
                    deps.add((e2, i2))
        pe_sync = kwargs.pop("pe_sync", False) if kind == "c" else False
        if eng == "pe" and kind == "c" and not pe_sync:
            deps = {d for d in deps if not (d[0] == "pe" and self.prog["pe"][d[1]]["kind"] == "c")}
        deps.discard((eng, idx))
        rec = dict(kind=kind, method=method, args=args, kwargs=kwargs, deps=deps,
                   need_sig=False, dma_n=None)
        if kind == "d":
            rec["dma_n"] = self.ndma[eng]
            self.ndma[eng] += 1
        self.prog[eng].append(rec)
        if kind == "c":
            self.lastc[eng] = idx
        else:
            self.lastd[eng] = (self.lastd.get(eng, []) + [idx])[-DMA_K:]
        for r in reads:
            self.readers.setdefault(r, {})[eng] = idx
        for r in writes:
            self.last_w[r] = (eng, idx)
            self.readers[r] = {}
        return (eng, idx)

    def op(self, eng, method, *args, reads=(), writes=(), **kwargs):
        return self._record(eng, "c", method, args, kwargs, reads, writes)

    def dma(self, eng, out, in_, reads=(), writes=(), **kwargs):
        return self._record(eng, "d", "dma_start", (), dict(out=out, in_=in_, **kwargs), reads, writes)

    def emit(self, final_waits=()):
        nc = self.nc
        prog = self.prog
        for e in ENGS:
            for rec in prog[e]:
                for (e2, i2) in rec["deps"]:
                    prog[e2][i2]["need_sig"] = True
        for fw in final_waits:
            prog[fw[0]][fw[1]]["need_sig"] = True
        nphase = {}
        for e in ENGS:
            c = 0
            for rec in prog[e]:
                if rec["kind"] == "c" and rec["need_sig"]:
                    rec["sig"] = c
                    c += 1
            nphase[e] = (c + SEM_PHASE - 1) // SEM_PHASE
        with contextlib.ExitStack() as st:
            csems = {e: [st.enter_context(nc.semaphore(f"c_{e}_{p}")) for p in range(nphase[e])]
                     for e in ENGS}
            dsems = {e: ([st.enter_context(nc.semaphore(f"d_{e}_{k}")) for k in range(DMA_K)]
                         if self.ndma[e] else []) for e in ENGS}

            def sigof(e2, i2):
                rec = prog[e2][i2]
                if rec["kind"] == "c":
                    s = rec["sig"]
                    return csems[e2][s // SEM_PHASE], (s % SEM_PHASE) + 1, (e2, "c", s // SEM_PHASE)
                n = rec["dma_n"]
                return dsems[e2][n % DMA_K], 16 * (n // DMA_K + 1), (e2, "d", n % DMA_K)

            block = st.enter_context(nc.Block())
            engobj = dict(pe="tensor", act="scalar", dve="vector", pool="gpsimd", sp="sync")

            def make(e):
                def body(eng):
                    seen = {}
                    for rec in prog[e]:
                        waits = [sigof(e2, i2) for (e2, i2) in rec["deps"]]
                        if rec["kind"] == "d":
                            n = rec["dma_n"]
                            if n >= DMA_K:
                                waits.append((dsems[e][n % DMA_K], 16 * (n // DMA_K), (e, "d", n % DMA_K)))
                        best = {}
                        for sem, val, key in waits:
                            if seen.get(key, 0) >= val:
                                continue
                            if key not in best or best[key][1] < val:
                                best[key] = (sem, val)
                        for key, (sem, val) in best.items():
                            eng.wait_ge(sem, val)
                            seen[key] = val
                        ins = getattr(eng, rec["method"])(*rec["args"], **rec["kwargs"])
                        if rec["kind"] == "d":
                            n = rec["dma_n"]
                            ins.then_inc(dsems[e][n % DMA_K], 16)
                        elif rec["need_sig"]:
                            s = rec["sig"]
                            ins.then_inc(csems[e][s // SEM_PHASE], 1)
                    if e == "sp":
                        for fw in final_waits:
                            sem, val, key = sigof(*fw)
                            if seen.get(key, 0) < val:
                                eng.wait_ge(sem, val)
                                seen[key] = val
                return body

            for e in ENGS:
                if prog[e] or e == "sp":
                    getattr(block, engobj[e])(make(e))


class Tile:
    def __init__(self, ap, res):
        self.ap = ap
        self.res = res

    def __getitem__(self, k):
        return self.ap[k]


class Arena:
    def __init__(self, base_ap, nbytes):
        self.base = base_ap
        self.nbytes = nbytes
        self.off = 0
        self.gen = 0
        self.floor = 0

    def set_floor(self):
        self.floor = self.off

    def reset(self):
        self.off = self.floor
        self.gen += 1

    def alloc(self, name, cols, dtype, shape=None):
        esz = {F32: 4, BF16: 2, I32: 4}[dtype]
        nb = (cols * esz + 63) // 64 * 64
        assert self.off + nb <= self.nbytes, (name, self.off, nb, self.nbytes)
        ap = self.base[:, self.off:self.off + cols * esz].bitcast(dtype)
        self.off += nb
        return Tile(ap, f"{name}#{self.gen}")


class Ctx:
    pass


def norm_tile(C, src_rows, gbc, xnT, tagbase):
    S = C.S
    xnT3 = xnT.ap.rearrange("p (k t) -> p k t", k=NKC)
    for tq in range(4):
        i = C.cnt_norm
        C.cnt_norm += 1
        xt = C.xt[i % 2]
        xn = C.xn[i % 2]
        ss = C.ss[i % 2]
        S.dma("sp", xt.ap, src_rows[tq * 128:(tq + 1) * 128, :], reads=[tagbase], writes=[xt.res])
        S.op("act", "activation", out=xn.ap, in_=xt.ap, func=AF.Square, accum_out=ss.ap[:, 0:1],
             reads=[xt.res], writes=[xn.res, ss.res])
        S.op("act", "activation", out=ss.ap[:, 1:2], in_=ss.ap[:, 0:1], func=AF.Sqrt, scale=1.0 / D,
             bias=C.eps.ap[:, 0:1], reads=[ss.res], writes=[ss.res])
        S.op("dve", "reciprocal", out=ss.ap[:, 2:3], in_=ss.ap[:, 1:2], reads=[ss.res], writes=[ss.res])
        S.op("dve", "scalar_tensor_tensor", out=xn.ap, in0=xt.ap, scalar=ss.ap[:, 2:3], in1=gbc.ap,
             op0=ALU.mult, op1=ALU.mult, reads=[xt.res, ss.res, gbc.res], writes=[xn.res])
        for half in range(2):
            bank = C.next_bank()
            pb = C.psb[bank]
            for j in range(8):
                dc = half * 8 + j
                S.op("pe", "transpose", out=pb[:, j * 128:(j + 1) * 128], in_=xn.ap[:, dc * 128:(dc + 1) * 128],
                     identity=C.ident.ap, reads=[xn.res, C.ident.res], writes=[C.pres[bank]])
            eng = "act" if half == 0 else "dve"
            dst = xnT3[:, half * 8:(half + 1) * 8, tq * 128:(tq + 1) * 128]
            srcv = pb.rearrange("p (j t) -> p j t", j=8)
            if eng == "act":
                S.op("act", "activation", out=dst, in_=srcv, func=AF.Copy, reads=[C.pres[bank]], writes=[xnT.res])
            else:
                S.op("dve", "tensor_copy", out=dst, in_=srcv, reads=[C.pres[bank]], writes=[xnT.res])


def load_bcast_row(C, tile, src_row_ap, n):
    C.S.dma("sp", tile.ap[:, 0:n], src_row_ap.partition_broadcast(128), writes=[tile.res])


def ffn_stage(C, src, dst, gain_row, wg, wu, wd, tag):
    S, A = C.S, C.A
    S.barrier()
    A.reset()
    T = C.T
    gbc = A.alloc("gbc", D, F32)
    load_bcast_row(C, gbc, gain_row, D)
    C.xt = [A.alloc(f"xt{i}", D, F32) for i in range(2)]
    C.xn = [A.alloc(f"xn{i}", D, BF16) for i in range(2)]
    C.ss = [A.alloc(f"ss{i}", 4, F32) for i in range(2)]
    xnTs = [A.alloc(f"xnT{i}", NKC * TT, BF16) for i in range(2)]
    hT = A.alloc("hT", NFC * TT, BF16)
    hT3 = hT.ap.rearrange("p (f t) -> p f t", f=NFC)
    wgt = [A.alloc(f"wg{i}", NKC * 256, BF16) for i in range(2)]
    wut = [A.alloc(f"wu{i}", NKC * 256, BF16) for i in range(2)]
    wdt = [A.alloc(f"wd{i}", 4 * 512, BF16) for i in range(3)]
    sg = [A.alloc(f"sg{i}", TT, F32) for i in range(2)]
    xs = [A.alloc(f"xs{i}", 512, F32) for i in range(4)]
    wgv = wg.rearrange("(kc p) f -> p kc f", p=128)
    wuv = wu.rearrange("(kc p) f -> p kc f", p=128)
    wdv = wd.rearrange("(fc p) d -> p fc d", p=128)
    nw = 0
    nd = 0
    nx = 0
    for tt in range(T // TT):
        rows = slice(tt * TT, (tt + 1) * TT)
        xnT = xnTs[tt % 2]
        xnT3 = xnT.ap.rearrange("p (k t) -> p k t", k=NKC)
        norm_tile(C, src[rows, :], gbc, xnT, (tag, "x", tt) if src is dst else ("xin", tt))
        for fg in range(NFC // 2):
            wgb, wub = wgt[nw % 2], wut[nw % 2]
            nw += 1
            S.dma("pool", wgb.ap.rearrange("p (k f) -> p k f", k=NKC), wgv[:, :, fg * 256:(fg + 1) * 256],
                  writes=[wgb.res])
            S.dma("pool", wub.ap.rearrange("p (k f) -> p k f", k=NKC), wuv[:, :, fg * 256:(fg + 1) * 256],
                  writes=[wub.res])
            wg3 = wgb.ap.rearrange("p (k f) -> p k f", k=NKC)
            wu3 = wub.ap.rearrange("p (k f) -> p k f", k=NKC)
            for j in range(2):
                fc = fg * 2 + j
                bg = C.next_bank()
                bu = C.next_bank()
                for kc in range(NKC):
                    S.op("pe", "matmul", C.ps[bg], lhsT=wg3[:, kc, j * 128:(j + 1) * 128], rhs=xnT3[:, kc, :],
                         start=(kc == 0), stop=(kc == NKC - 1), reads=[wgb.res, xnT.res], writes=[C.pres[bg]])
                for kc in range(NKC):
                    S.op("pe", "matmul", C.ps[bu], lhsT=wu3[:, kc, j * 128:(j + 1) * 128], rhs=xnT3[:, kc, :],
                         start=(kc == 0), stop=(kc == NKC - 1), reads=[wub.res, xnT.res], writes=[C.pres[bu]])
                sgt = sg[fc % 2]
                S.op("act", "activation", out=sgt.ap, in_=C.ps[bg], func=AF.Silu, reads=[C.pres[bg]], writes=[sgt.res])
                S.op("dve", "tensor_tensor", out=hT3[:, fc, :], in0=sgt.ap, in1=C.ps[bu], op=ALU.mult,
                     reads=[sgt.res, C.pres[bu]], writes=[(hT.res, fc)])
        for q in range(4):
            banks = [C.next_bank() for _ in range(4)]
            for fcg in range(NFC // 4):
                wdb = wdt[nd % 3]
                nd += 1
                S.dma("pool", wdb.ap.rearrange("p (f d) -> p f d", f=4),
                      wdv[:, fcg * 4:(fcg + 1) * 4, q * 512:(q + 1) * 512], writes=[wdb.res])
                wd3 = wdb.ap.rearrange("p (f d) -> p f d", f=4)
                for j in range(4):
                    fc = fcg * 4 + j
                    for tq in range(4):
                        S.op("pe", "matmul", C.ps[banks[tq]], lhsT=hT3[:, fc, tq * 128:(tq + 1) * 128], rhs=wd3[:, j, :],
                             start=(fc == 0), stop=(fc == NFC - 1), reads=[(hT.res, fc), wdb.res],
                             writes=[C.pres[banks[tq]]])
            for tq in range(4):
                xb = xs[nx % 4]
                nx += 1
                r0 = tt * TT + tq * 128
                S.dma("sp", xb.ap, src[r0:r0 + 128, q * 512:(q + 1) * 512],
                      reads=[(tag, "x", tt) if src is dst else ("xin", tt)], writes=[xb.res])
                S.op("dve", "scalar_tensor_tensor", out=xb.ap, in0=C.ps[banks[tq]], scalar=0.5, in1=xb.ap,
                     op0=ALU.mult, op1=ALU.add, reads=[C.pres[banks[tq]], xb.res], writes=[xb.res])
                C.last_out = S.dma("sp", dst[r0:r0 + 128, q * 512:(q + 1) * 512], xb.ap, reads=[xb.res],
                                   writes=[(tag, "xo", tt, tq, q)])


def final_norm_stage(C, src, dst, gain_row):
    S, A = C.S, C.A
    S.barrier()
    A.reset()
    gbc = A.alloc("gbc", D, F32)
    load_bcast_row(C, gbc, gain_row, D)
    xt = [A.alloc(f"xt{i}", D, F32) for i in range(3)]
    junk = A.alloc("junk", D, BF16)
    ss = [A.alloc(f"ss{i}", 4, F32) for i in range(3)]
    outs = []
    for i in range(C.T // 128):
        x, s_ = xt[i % 3], ss[i % 3]
        S.dma("sp", x.ap, src[i * 128:(i + 1) * 128, :], writes=[x.res])
        S.op("act", "activation", out=junk.ap, in_=x.ap, func=AF.Square, accum_out=s_.ap[:, 0:1],
             reads=[x.res], writes=[junk.res, s_.res])
        S.op("act", "activation", out=s_.ap[:, 1:2], in_=s_.ap[:, 0:1], func=AF.Sqrt, scale=1.0 / D,
             bias=C.eps.ap[:, 0:1], reads=[s_.res], writes=[s_.res])
        S.op("dve", "reciprocal", out=s_.ap[:, 2:3], in_=s_.ap[:, 1:2], reads=[s_.res], writes=[s_.res])
        S.op("dve", "scalar_tensor_tensor", out=x.ap, in0=x.ap, scalar=s_.ap[:, 2:3], in1=gbc.ap,
             op0=ALU.mult, op1=ALU.mult, reads=[x.res, s_.res, gbc.res], writes=[x.res])
        outs.append(S.dma("sp", dst[i * 128:(i + 1) * 128, :], x.ap, reads=[x.res]))
    return outs


PI = math.pi
TWO_PI_HI = 6.28125
TWO_PI_LO = 2.0 * math.pi - 6.28125


def wrap_pi(S, t, m, shape_ap=None):
    S.op("dve", "tensor_scalar", out=m.ap, in0=t.ap, scalar1=PI, scalar2=-2.0 * PI, op0=ALU.is_gt, op1=ALU.mult,
         reads=[t.res], writes=[m.res])
    S.op("dve", "tensor_tensor", out=t.ap, in0=t.ap, in1=m.ap, op=ALU.add, reads=[t.res, m.res], writes=[t.res])
    S.op("dve", "tensor_scalar", out=m.ap, in0=t.ap, scalar1=-PI, scalar2=2.0 * PI, op0=ALU.is_lt, op1=ALU.mult,
         reads=[t.res], writes=[m.res])
    S.op("dve", "tensor_tensor", out=t.ap, in0=t.ap, in1=m.ap, op=ALU.add, reads=[t.res, m.res], writes=[t.res])


def make_rot(names_banks):
    st = {"i": 0}

    def f():
        b = names_banks[st["i"] % len(names_banks)]
        st["i"] += 1
        return b
    return f


def even_stage(C, l, src, dst):
    e = l // 2
    S, A, W = C.S, C.A, C.W
    T = C.T
    NB = T // 128
    cf = C.cstf
    op, dma = S.op, S.dma
    tabs = C.tabs
    lhsd = C.lhsd
    smalld = C.smalld
    S.barrier()
    A.reset()

    def L(name, cols, dt=F32):
        return A.alloc(name, cols, dt)
    are, aim, ldt = L("are", 32), L("aim", 32), L("ldt", 32)
    for g2 in range(2):
        ps_ = slice(g2 * 64, (g2 + 1) * 64)
        dma("sp", are.ap[ps_, :], W["s5_a_re"][e].rearrange("(j g) p -> g p j", g=2)[g2], writes=[are.res],
            allow_slow_non_contiguous=True)
        dma("sp", aim.ap[ps_, :], W["s5_a_im"][e].rearrange("(j g) p -> g p j", g=2)[g2], writes=[aim.res],
            allow_slow_non_contiguous=True)
        dma("sp", ldt.ap[ps_, :], W["s5_log_dt"][e].rearrange("(j g) -> g j", g=2)[g2].partition_broadcast(64),
            writes=[ldt.res], allow_slow_non_contiguous=True)
    dt_, mag, th, m_, thc = L("dt", 32), L("mag", 32), L("th", 32), L("m", 32), L("thc", 32)
    cs, sn = L("cs", 32), L("sn", 32)
    op("act", "activation", out=dt_.ap, in_=ldt.ap, func=AF.Exp, reads=[ldt.res], writes=[dt_.res])
    op("dve", "tensor_tensor", out=mag.ap, in0=are.ap, in1=dt_.ap, op=ALU.mult, reads=[are.res, dt_.res], writes=[mag.res])
    op("act", "activation", out=mag.ap, in_=mag.ap, func=AF.Exp, reads=[mag.res], writes=[mag.res])
    op("dve", "tensor_tensor", out=th.ap, in0=aim.ap, in1=dt_.ap, op=ALU.mult, reads=[aim.res, dt_.res], writes=[th.res])
    for _ in range(5):
        wrap_pi(S, th, m_)
    op("dve", "tensor_scalar", out=thc.ap, in0=th.ap, scalar1=PI / 2, scalar2=None, op0=ALU.add, reads=[th.res], writes=[thc.res])
    wrap_pi(S, thc, m_)
    op("act", "activation", out=sn.ap, in_=th.ap, func=AF.Sin, reads=[th.res], writes=[sn.res])
    op("act", "activation", out=cs.ap, in_=thc.ap, func=AF.Sin, reads=[thc.res], writes=[cs.res])
    abr, abi, den, cre, cim, t1_, t2_ = L("abr", 32), L("abi", 32), L("den", 32), L("cre", 32), L("cim", 32), L("t1_", 32), L("t2_", 32)
    TTm = lambda o, a, b, o_: op("dve", "tensor_tensor", out=o.ap, in0=a.ap, in1=b.ap, op=o_, reads=[a.res, b.res], writes=[o.res])
    TTm(abr, mag, cs, ALU.mult)
    TTm(abi, mag, sn, ALU.mult)
    op("dve", "tensor_scalar", out=t1_.ap, in0=abr.ap, scalar1=-1.0, scalar2=None, op0=ALU.add, reads=[abr.res], writes=[t1_.res])
    TTm(den, are, are, ALU.mult)
    TTm(t2_, aim, aim, ALU.mult)
    TTm(den, den, t2_, ALU.add)
    op("dve", "reciprocal", out=den.ap, in_=den.ap, reads=[den.res], writes=[den.res])
    TTm(cre, t1_, are, ALU.mult)
    TTm(t2_, abi, aim, ALU.mult)
    TTm(cre, cre, t2_, ALU.add)
    TTm(cre, cre, den, ALU.mult)
    TTm(cim, abi, are, ALU.mult)
    TTm(t2_, t1_, aim, ALU.mult)
    TTm(cim, cim, t2_, ALU.subtract)
    TTm(cim, cim, den, ALU.mult)
    dma("sp", smalld[:, 0:32], mag.ap, reads=[mag.res], writes=["smalld"])
    bre, bim, bbr, bbi, tb = L("bre", 512), L("bim", 512), L("bbr", 512), L("bbi", 512), L("tb", 512)
    v3 = lambda t: t.ap.rearrange("p (j c) -> p j c", j=32)
    for g2 in range(2):
        ps_ = slice(g2 * 64, (g2 + 1) * 64)
        dma("sp", v3(bre)[ps_], W["s5_b_re"][e].rearrange("(j g) p c -> g p j c", g=2)[g2], writes=[bre.res])
        dma("sp", v3(bim)[ps_], W["s5_b_im"][e].rearrange("(j g) p c -> g p j c", g=2)[g2], writes=[bim.res])
    bc3 = lambda t: t.ap.unsqueeze(2).broadcast_to([128, 32, 16])

    def TB(o, a, b_bc_tile, o_):
        op("dve", "tensor_tensor", out=v3(o), in0=v3(a), in1=bc3(b_bc_tile), op=o_, reads=[a.res, b_bc_tile.res], writes=[o.res])
    TB(bbr, bre, cre, ALU.mult)
    TB(tb, bim, cim, ALU.mult)
    TTm(bbr, bbr, tb, ALU.subtract)
    TB(bbi, bim, cre, ALU.mult)
    TB(tb, bre, cim, ALU.mult)
    TTm(bbi, bbi, tb, ALU.add)
    X = [L(f"X{i}", 128) for i in range(2)]
    lo = [L(f"lo{i}", 512, BF16) for i in range(2)]
    cnat = [L(f"cnat{i}", 64) for i in range(2)]
    identF = cf.ap[:, 0:128]
    prot = make_rot([0, 1, 2, 3, 4, 5, 6, 7])
    k = 0
    for ch in range(8):
        for ri, bb in enumerate((bbr, bbi)):
            Xt = X[k % 2]
            lot = lo[k % 2]
            k += 1
            X4 = Xt.ap.rearrange("p (j g c) -> p j g c", j=4, g=2)
            for g2 in range(2):
                op("dve", "tensor_scalar", out=X4[:, :, g2, :], in0=v3(bb)[:, 4 * ch:4 * ch + 4, :], scalar1=cf.ap[:, 130 + g2:131 + g2],
                   scalar2=None, op0=ALU.mult, reads=[bb.res, cf.res], writes=[Xt.res])
            bk = prot()
            op("pe", "transpose", out=C.ps[bk][:, 0:128], in_=Xt.ap, identity=identF, reads=[Xt.res, cf.res], writes=[C.pres[bk]])
            for jj in range(4):
                op("dve", "tensor_scalar", out=lot.ap[:, jj * 128:(jj + 1) * 128], in0=C.ps[bk][:, 0:128],
                   scalar1=cf.ap[:, 132 + jj:133 + jj], scalar2=None, op0=ALU.mult, reads=[C.pres[bk], cf.res], writes=[lot.res])
            dma("sp", lhsd[:, (ri * 32 + ch * 4) * 128:(ri * 32 + ch * 4 + 4) * 128], lot.ap, reads=[lot.res], writes=["lhsd"])
        for ri, (cw, sgn) in enumerate(((W["s5_c_re"], 1.0), (W["s5_c_im"], -1.0))):
            Xt = X[k % 2]
            lot = lo[k % 2]
            cn = cnat[k % 2]
            k += 1
            dma("sp", cn.ap, cw[e].rearrange("g c p -> (g c) p")[ch * 128:(ch + 1) * 128, :], writes=[cn.res])
            for g2 in range(2):
                op("dve", "tensor_scalar", out=Xt.ap[:, g2 * 64:(g2 + 1) * 64], in0=cn.ap, scalar1=cf.ap[:, 136 + g2:137 + g2],
                   scalar2=sgn, op0=ALU.mult, op1=ALU.mult, reads=[cn.res, cf.res], writes=[Xt.res])
            bk = prot()
            op("pe", "transpose", out=C.ps[bk][:, 0:128], in_=Xt.ap, identity=identF, reads=[Xt.res, cf.res], writes=[C.pres[bk]])
            op("dve", "memset", lot.ap, 0.0, writes=[lot.res])
            for jj in range(4):
                op("dve", "tensor_copy", out=lot.ap[:, jj * 128 + jj * 32:jj * 128 + jj * 32 + 32], in_=C.ps[bk][:, jj * 32:jj * 32 + 32],
                   reads=[C.pres[bk]], writes=[lot.res])
            dma("sp", lhsd[:, ((2 + ri) * 32 + ch * 4) * 128:((2 + ri) * 32 + ch * 4 + 4) * 128], lot.ap, reads=[lot.res], writes=["lhsd"])
    NP = 8
    Ct, St, Ut, Vt = L("Ct", NP * 512), L("St", NP * 512), L("Ut", NP * 256), L("Vt", NP * 256)
    C3 = Ct.ap.rearrange("p (j t) -> p j t", j=NP)
    S3 = St.ap.rearrange("p (j t) -> p j t", j=NP)
    U3 = Ut.ap.rearrange("p (j t) -> p j t", j=NP)
    V3 = Vt.ap.rearrange("p (j t) -> p j t", j=NP)
    for q8 in range(32 // NP):
        js = slice(q8 * NP, (q8 + 1) * NP)
        op("dve", "tensor_copy", out=C3[:, :, 0:1], in_=cs.ap[:, js].unsqueeze(2), reads=[cs.res], writes=[Ct.res])
        op("dve", "tensor_copy", out=S3[:, :, 0:1], in_=sn.ap[:, js].unsqueeze(2), reads=[sn.res], writes=[St.res])
        n = 1
        while n < 512:
            cn_b = C3[:, :, n - 1:n].broadcast_to([128, NP, n])
            sn_b = S3[:, :, n - 1:n].broadcast_to([128, NP, n])
            rw = [Ct.res, St.res]
            op("dve", "tensor_tensor", out=U3[:, :, 0:n], in0=S3[:, :, 0:n], in1=sn_b, op=ALU.mult, reads=rw, writes=[Ut.res])
            op("dve", "tensor_tensor", out=V3[:, :, 0:n], in0=C3[:, :, 0:n], in1=sn_b, op=ALU.mult, reads=rw, writes=[Vt.res])
            op("dve", "tensor_tensor", out=C3[:, :, n:2 * n], in0=C3[:, :, 0:n], in1=cn_b, op=ALU.mult, reads=rw, writes=[Ct.res])
            op("dve", "tensor_tensor", out=S3[:, :, n:2 * n], in0=S3[:, :, 0:n], in1=cn_b, op=ALU.mult, reads=rw, writes=[St.res])
            op("dve", "tensor_tensor", out=C3[:, :, n:2 * n], in0=C3[:, :, n:2 * n], in1=U3[:, :, 0:n], op=ALU.subtract,
               reads=[Ct.res, Ut.res], writes=[Ct.res])
            op("dve", "tensor_tensor", out=S3[:, :, n:2 * n], in0=S3[:, :, n:2 * n], in1=V3[:, :, 0:n], op=ALU.add,
               reads=[St.res, Vt.res], writes=[St.res])
            n *= 2
        dma("sp", tabs[js, 0].rearrange("j p t -> p j t"), C3, reads=[Ct.res], writes=["tabs"])
        dma("sp", tabs[js, 1].rearrange("j p t -> p j t"), S3, reads=[St.res], writes=["tabs"])
    if getattr(C, "stop_at", "") == "prep":
        return
    S.barrier()
    A.reset()
    gbc = L("gbc", D)
    load_bcast_row(C, gbc, W["norm_mix"][l], D)
    _xt = L("xt0", D)
    C.xt = [_xt, _xt]
    _xn = L("xn0", D, BF16)
    C.xn = [_xn, _xn]
    C.ss = [L(f"ss{i}", 4) for i in range(2)]
    hnT = L("hnT", NKC * TT, BF16)
    hnT3 = hnT.ap.rearrange("p (k t) -> p k t", k=NKC)
    wb = [L(f"wb{i}", NKC * 512, BF16) for i in range(2)]
    nwb = [0]

    def load_w(src_ap3, nk):
        t = wb[nwb[0] % 2]
        nwb[0] += 1
        v = t.ap[:, 0:nk * 512].rearrange("p (k f) -> p k f", k=nk)
        dma("pool", v, src_ap3, writes=[t.res])
        return t, v
    lhs = L("lhs", 4 * 32 * 128, BF16)
    dma("sp", lhs.ap, lhsd, reads=["lhsd"], writes=[lhs.res])
    lhs3 = lhs.ap.rearrange("p (m k) -> p m k", k=128)
    rho = L("rho", 32)
    dma("sp", rho.ap, smalld[:, 0:32], reads=["smalld"], writes=[rho.res])
    hst = L("hst", 64)
    op("dve", "memset", hst.ap, 0.0, writes=[hst.res])
    dcol = L("dcol", 8)
    dma("sp", dcol.ap, W["s5_d"][e].rearrange("g c -> (g c)").rearrange("(a p) -> p a", p=128), writes=[dcol.res],
        allow_slow_non_contiguous=True)
    bglu = L("bglu", 8)
    dma("sp", bglu.ap, W["s5_b_glu"][e].rearrange("(a p) -> p a", p=128), writes=[bglu.res], allow_slow_non_contiguous=True)
    dD = L("dD", 8 * 128, BF16)
    for ch in range(8):
        op("dve", "tensor_scalar", out=dD.ap[:, ch * 128:(ch + 1) * 128], in0=cf.ap[:, 0:128], scalar1=dcol.ap[:, ch:ch + 1],
           scalar2=None, op0=ALU.mult, reads=[cf.res, dcol.res], writes=[dD.res])
    if getattr(C, "stop_at", "") == "setup1":
        return
    esk = L("esk", 16)
    load_bcast_row(C, esk, W["ev_sinks"][e], 16)
    op("act", "activation", out=esk.ap, in_=esk.ap, func=AF.Exp, reads=[esk.res], writes=[esk.res])
    mcur, mprev = L("mcur", 128, BF16), L("mprev", 128, BF16)
    op("dve", "tensor_copy", out=mcur.ap, in_=cf.ap[:, 256:384], reads=[cf.res], writes=[mcur.res])
    op("dve", "tensor_copy", out=mprev.ap, in_=cf.ap[:, 384:512], reads=[cf.res], writes=[mprev.res])
    if getattr(C, "stop_at", "") == "setup2":
        return
    posi = L("posi", NB, I32)
    dma("sp", posi.ap, C.pos.rearrange("(b p) -> p b", p=128), writes=[posi.res], allow_slow_non_contiguous=True)
    posf = L("posf", NB)
    op("dve", "tensor_copy", out=posf.ap, in_=posi.ap, reads=[posi.res], writes=[posf.res])
    ang, qf, mm_ = L("ang", NB * 8), L("qf", NB * 8), L("mm_", NB * 8)
    angc = qf
    rsin, rcos = L("rsin", NB * 8), L("rcos", NB * 8)
    a3 = lambda t: t.ap.rearrange("p (b i) -> p b i", i=8)
    op("dve", "tensor_tensor", out=a3(ang), in0=posf.ap.unsqueeze(2).broadcast_to([128, NB, 8]),
       in1=cf.ap[:, 140:148].unsqueeze(1).broadcast_to([128, NB, 8]), op=ALU.mult, reads=[posf.res, cf.res], writes=[ang.res])
    op("dve", "tensor_scalar", out=qf.ap, in0=ang.ap, scalar1=1.0 / (2 * PI), scalar2=None, op0=ALU.mult, reads=[ang.res], writes=[qf.res])
    op("dve", "tensor_scalar", out=qf.ap, in0=qf.ap, scalar1=12582912.0, scalar2=None, op0=ALU.add, reads=[qf.res], writes=[qf.res])
    op("dve", "tensor_scalar", out=qf.ap, in0=qf.ap, scalar1=-12582912.0, scalar2=None, op0=ALU.add, reads=[qf.res], writes=[qf.res])
    op("dve", "scalar_tensor_tensor", out=ang.ap, in0=qf.ap, scalar=-TWO_PI_HI, in1=ang.ap, op0=ALU.mult, op1=ALU.add,
       reads=[qf.res, ang.res], writes=[ang.res])
    op("dve", "scalar_tensor_tensor", out=ang.ap, in0=qf.ap, scalar=-TWO_PI_LO, in1=ang.ap, op0=ALU.mult, op1=ALU.add,
       reads=[qf.res, ang.res], writes=[ang.res])
    wrap_pi(S, ang, mm_)
    wrap_pi(S, ang, mm_)
    op("dve", "tensor_scalar", out=angc.ap, in0=ang.ap, scalar1=PI / 2, scalar2=None, op0=ALU.add, reads=[ang.res], writes=[angc.res])
    wrap_pi(S, angc, mm_)
    for t_ in (ang, angc):
        op("dve", "tensor_scalar", out=t_.ap, in0=t_.ap, scalar1=-PI, scalar2=PI, op0=ALU.max, op1=ALU.min, reads=[t_.res], writes=[t_.res])
    op("act", "activation", out=rsin.ap, in_=ang.ap, func=AF.Sin, reads=[ang.res], writes=[rsin.res])
    op("act", "activation", out=rcos.ap, in_=angc.ap, func=AF.Sin, reads=[angc.res], writes=[rcos.res])
    qtok = L("qtok", 4 * 1024, BF16)
    qtok3 = qtok.ap.rearrange("p (b f) -> p b f", b=4)
    kdup = [L(f"kdup{i}", 512, BF16) for i in range(2)]
    vext = L("vext", 5 * 4 * 65, BF16)
    vext4 = vext.ap.rearrange("p (s k d) -> p s k d", s=5, k=4)
    op("dve", "memset", vext.ap, 1.0, writes=[vext.res])
    qT = L("qT", 8 * 512, BF16)
    qT3 = qT.ap.rearrange("p (c t) -> p c t", c=8)
    kT = L("kT", 4 * 640, BF16)
    kT3 = kT.ap.rearrange("p (k t) -> p k t", k=4)
    PTb = [L(f"PT{i}", 512, BF16) for i in range(2)] * 2
    Eb = [L(f"E{i}", 512, BF16) for i in range(2)]
    _atok = L("atok0", 1024, BF16)
    atok = [_atok, _atok]
    featT = L("featT", 16 * 512, BF16)
    featT3 = featT.ap.rearrange("p (c t) -> p c t", c=16)
    den4 = [L(f"den4{i}", 8) for i in range(2)]
    rt = [L(f"rt{i}", 2 * 64 * 2 + 16) for i in range(2)]
    uT = L("uT", 8 * 512, BF16)
    uT3 = uT.ap.rearrange("p (c t) -> p c t", c=8)
    gT = Tile(qtok.ap, qtok.res)
    gT3 = gT.ap.rearrange("p (c t) -> p c t", c=8)
    tabC = [L(f"tabC{i}", 512) for i in range(2)]
    tabS = [L(f"tabS{i}", 512) for i in range(2)]
    _s5 = [L(f"s5t_{k_}", 512) for k_ in range(6)]
    s5t = [_s5, _s5]
    _hb = [L(f"hbf_{k_}", 512, BF16) for k_ in range(2)]
    hbf = [_hb, _hb]
    sig = [L(f"sig{i}", 512, BF16) for i in range(2)]
    xs = [L(f"xs{i}", 512) for i in range(2)] * 2
    rotA = make_rot([0, 1, 2, 3, 4, 5, 6, 7])
    nE = [0]
    nPT = [0]
    nrt = [0]
    nxs = [0]

    def rotary(bank_ap, nh, bres, b):
        r_ = rt[nrt[0] % 2]
        nrt[0] += 1
        pv = bank_ap.rearrange("p (h d) -> p h d", d=64)
        cb_ = a3(rcos)[:, b, :].unsqueeze(1).broadcast_to([128, nh, 8])
        sb_ = a3(rsin)[:, b, :].unsqueeze(1).broadcast_to([128, nh, 8])
        ta = r_.ap[:, 0:nh * 8].rearrange("p (h i) -> p h i", i=8)
        tb_ = r_.ap[:, 64:64 + nh * 8].rearrange("p (h i) -> p h i", i=8)
        tc = r_.ap[:, 128:128 + nh * 8].rearrange("p (h i) -> p h i", i=8)
        td = r_.ap[:, 192:192 + nh * 8].rearrange("p (h i) -> p h i", i=8)
        rr = [bres, rcos.res, rsin.res]
        op("dve", "tensor_tensor", out=ta, in0=pv[:, :, 0:8], in1=cb_, op=ALU.mult, reads=rr, writes=[r_.res])
        op("dve", "tensor_tensor", out=tb_, in0=pv[:, :, 8:16], in1=sb_, op=ALU.mult, reads=rr, writes=[r_.res])
        op("dve", "tensor_tensor", out=tc, in0=pv[:, :, 8:16], in1=cb_, op=ALU.mult, reads=rr, writes=[r_.res])
        op("dve", "tensor_tensor", out=td, in0=pv[:, :, 0:8], in1=sb_, op=ALU.mult, reads=rr, writes=[r_.res])
        return ta, tb_, tc, td, r_

    if getattr(C, "stop_at", "") == "setup":
        return
    for tt in range(T // TT):
        rows = slice(tt * TT, (tt + 1) * TT)
        norm_tile(C, src[rows, :], gbc, hnT, ("mixx", tt))
        if getattr(C, "stop_at", "") == "p0":
            return
        w_in = W["ev_w_in"][e].rearrange("(kc p) f -> p kc f", p=128)
        for piece in range(3):
            if getattr(C, "stop_at", "") == "p%d" % (piece + 1) and piece > 0:
                return
            wt, wv = load_w(w_in[:, :, piece * 512:(piece + 1) * 512], NKC)
            for b in range(4):
                bk = rotA()
                for kc in range(NKC):
                    op("pe", "matmul", C.ps[bk], lhsT=hnT3[:, kc, b * 128:(b + 1) * 128], rhs=wv[:, kc, :], start=(kc == 0),
                       stop=(kc == NKC - 1), reads=[hnT.res, wt.res], writes=[C.pres[bk]])
                if piece < 2:
                    dst3 = qtok3[:, b, piece * 512:(piece + 1) * 512].rearrange("p (h d) -> p h d", d=64)
                    op("act", "activation", out=qtok3[:, b, piece * 512:(piece + 1) * 512], in_=C.ps[bk], func=AF.Copy,
                       reads=[C.pres[bk]], writes=[qtok.res])
                    ta, tb_, tc, td, r_ = rotary(C.ps[bk], 8, C.pres[bk], tt * 4 + b)
                    op("dve", "tensor_tensor", out=dst3[:, :, 0:8], in0=ta, in1=tb_, op=ALU.subtract, reads=[r_.res], writes=[qtok.res])
                    op("dve", "tensor_tensor", out=dst3[:, :, 8:16], in0=tc, in1=td, op=ALU.add, reads=[r_.res], writes=[qtok.res])
                else:
                    kd = kdup[b % 2]
                    kd4 = kd.ap.rearrange("p (k u d) -> p k u d", k=4, u=2)
                    kps = C.ps[bk][:, 0:256].rearrange("p (k d) -> p k d", d=64)
                    op("act", "activation", out=kd4[:, :, 0, :], in_=kps, func=AF.Copy, reads=[C.pres[bk]], writes=[kd.res])
                    ta, tb_, tc, td, r_ = rotary(C.ps[bk][:, 0:256], 4, C.pres[bk], tt * 4 + b)
                    op("dve", "tensor_tensor", out=kd4[:, :, 0, 0:8], in0=ta, in1=tb_, op=ALU.subtract, reads=[r_.res], writes=[kd.res])
                    op("dve", "tensor_tensor", out=kd4[:, :, 0, 8:16], in0=tc, in1=td, op=ALU.add, reads=[r_.res], writes=[kd.res])
                    op("dve", "tensor_copy", out=kd4[:, :, 1, :], in_=kd4[:, :, 0, :], reads=[kd.res], writes=[kd.res])
                    op("act", "activation", out=vext4[:, 1 + b, :, 0:64], in_=C.ps[bk][:, 256:512].rearrange("p (k d) -> p k d", d=64),
                       func=AF.Copy, reads=[C.pres[bk]], writes=[vext.res])
                    bk2 = rotA()
                    for kv in range(4):
                        op("pe", "transpose", out=C.psb[bk2][:, kv * 128:(kv + 1) * 128], in_=kd.ap[:, kv * 128:(kv + 1) * 128],
                           identity=C.ident.ap, reads=[kd.res, C.ident.res], writes=[C.pres[bk2]])
                    op("act", "activation", out=kT3[:, :, (1 + b) * 128:(2 + b) * 128],
                       in_=C.psb[bk2][:, 0:512].rearrange("p (k t) -> p k t", k=4), func=AF.Copy, reads=[C.pres[bk2]], writes=[kT.res])
        if getattr(C, "stop_at", "") == "p4":
            return
        for b in range(4):
            bk2 = rotA()
            for c in range(8):
                op("pe", "transpose", out=C.psb[bk2][:, c * 128:(c + 1) * 128], in_=qtok3[:, b, c * 128:(c + 1) * 128],
                   identity=C.ident.ap, reads=[qtok.res, C.ident.res], writes=[C.pres[bk2]])
            op("dve", "tensor_copy", out=qT3[:, :, b * 128:(b + 1) * 128], in_=C.psb[bk2].rearrange("p (c t) -> p c t", c=8),
               reads=[C.pres[bk2]], writes=[qT.res])
        if getattr(C, "stop_at", "") == "p5":
            return
        for piece in range(2):
            wt, wv = load_w(w_in[:, :, 1536 + piece * 512:1536 + (piece + 1) * 512], NKC)
            for j in range(4):
                uc = piece * 4 + j
                bk = rotA()
                for kc in range(NKC):
                    op("pe", "matmul", C.ps[bk], lhsT=wv[:, kc, j * 128:(j + 1) * 128], rhs=hnT3[:, kc, :], start=(kc == 0),
                       stop=(kc == NKC - 1), reads=[hnT.res, wt.res], writes=[C.pres[bk]])
                op("act", "activation", out=uT3[:, uc, :], in_=C.ps[bk], func=AF.Copy, reads=[C.pres[bk]], writes=[(uT.res, uc)])
        if getattr(C, "stop_at", "") == "proj":
            return
        for b in range(4):
            gb = tt * 4 + b
            at = atok[b % 2]
            at3 = at.ap.rearrange("p (h d) -> p h d", d=64)
            for kv in range(4):
                kbs = [0, 1] if gb > 0 else [1]
                pts = []
                for kb in kbs:
                    bk = rotA()
                    kcol = (b + kb) * 128
                    for hh in range(4):
                        h = 4 * kv + hh
                        c, s_ = h // 2, h % 2
                        ps_ = slice(s_ * 64, (s_ + 1) * 64)
                        op("pe", "matmul", C.ps[bk][:, hh * 128:(hh + 1) * 128], lhsT=kT3[ps_, kv, kcol:kcol + 128],
                           rhs=qT3[ps_, c, b * 128:(b + 1) * 128], start=True, stop=True, reads=[kT.res, qT.res], writes=[C.pres[bk]],
                           pe_sync=True)
                    Et = Eb[nE[0] % 2]
                    nE[0] += 1
                    op("act", "activation", out=Et.ap, in_=C.ps[bk], func=AF.Exp, scale=0.125, reads=[C.pres[bk]], writes=[Et.res])
                    Pt = PTb[nPT[0] % 4]
                    nPT[0] += 1
                    mk = mcur if kb == 1 else mprev
                    op("pool", "tensor_tensor", out=Pt.ap.rearrange("p (h q) -> p h q", h=4), in0=Et.ap.rearrange("p (h q) -> p h q", h=4),
                       in1=mk.ap.unsqueeze(1).broadcast_to([128, 4, 128]), op=ALU.mult, reads=[Et.res, mk.res], writes=[Pt.res])
                    pts.append((Pt, kb))
                if getattr(C, "stop_at", "") == "a1" and (b, kv) == (0, 0):
                    return
                bo = rotA()
                for hh in range(4):
                    for i_, (Pt, kb) in enumerate(pts):
                        op("pe", "matmul", C.ps[bo][:, hh * 65:(hh + 1) * 65], lhsT=Pt.ap[:, hh * 128:(hh + 1) * 128],
                           rhs=vext4[:, b + kb, kv, :], start=(i_ == 0), stop=(i_ == len(pts) - 1), reads=[Pt.res, vext.res],
                           writes=[C.pres[bo]])
                if getattr(C, "stop_at", "") == "a2" and (b, kv) == (0, 0):
                    return
                ov = C.ps[bo][:, 0:260].rearrange("p (h d) -> p h d", d=65)
                dn = den4[kv % 2]
                op("dve", "tensor_tensor", out=dn.ap[:, 0:4].unsqueeze(2), in0=ov[:, :, 64:65], in1=esk.ap[:, 4 * kv:4 * kv + 4].unsqueeze(2),
                   op=ALU.add, reads=[C.pres[bo], esk.res], writes=[dn.res])
                op("dve", "reciprocal", out=dn.ap[:, 4:8], in_=dn.ap[:, 0:4], reads=[dn.res], writes=[dn.res])
                op("dve", "tensor_tensor", out=at3[:, 4 * kv:4 * kv + 4, :], in0=ov[:, :, 0:64],
                   in1=dn.ap[:, 4:8].unsqueeze(2).broadcast_to([128, 4, 64]), op=ALU.mult, reads=[C.pres[bo], dn.res], writes=[at.res])
            if getattr(C, "stop_at", "") == "a3" and b == 0:
                return
            bk2 = rotA()
            for c in range(8):
                op("pe", "transpose", out=C.psb[bk2][:, c * 128:(c + 1) * 128], in_=at.ap[:, c * 128:(c + 1) * 128],
                   identity=C.ident.ap, reads=[at.res, C.ident.res], writes=[C.pres[bk2]])
            op("act", "activation", out=featT3[:, 0:8, b * 128:(b + 1) * 128], in_=C.psb[bk2].rearrange("p (c t) -> p c t", c=8),
               func=AF.Copy, reads=[C.pres[bk2]], writes=[(featT.res, "a", b)])
        op("act", "activation", out=kT3[:, :, 0:128], in_=kT3[:, :, 512:640], func=AF.Copy, reads=[kT.res], writes=[kT.res])
        op("dve", "tensor_copy", out=vext4[:, 0], in_=vext4[:, 4], reads=[vext.res], writes=[vext.res])
        if getattr(C, "stop_at", "") == "attn":
            return
        rotB = make_rot([0, 1, 2, 3])
        for ch in range(8):
            yb = 4 + (ch % 2)
            for jj in range(4):
                j = ch * 4 + jj
                i2 = j % 2
                tC, tS = tabC[i2], tabS[i2]
                dma("sp", tC.ap, tabs[j, 0], reads=["tabs"], writes=[tC.res])
                dma("sp", tS.ap, tabs[j, 1], reads=["tabs"], writes=[tS.res])
                bR, bI = rotB(), rotB()
                op("pe", "matmul", C.ps[bR], lhsT=lhs3[:, j, :], rhs=uT3[:, ch, :], start=True, stop=True,
                   reads=[lhs.res, (uT.res, ch)], writes=[C.pres[bR]])
                op("pe", "matmul", C.ps[bI], lhsT=lhs3[:, 32 + j, :], rhs=uT3[:, ch, :], start=True, stop=True,
                   reads=[lhs.res, (uT.res, ch)], writes=[C.pres[bI]])
                t = s5t[i2]
                op("dve", "tensor_tensor", out=t[0].ap, in0=C.ps[bR], in1=tC.ap, op=ALU.mult, reads=[C.pres[bR], tC.res], writes=[t[0].res])
                op("dve", "tensor_tensor", out=t[1].ap, in0=C.ps[bI], in1=tS.ap, op=ALU.mult, reads=[C.pres[bI], tS.res], writes=[t[1].res])
                op("dve", "tensor_tensor", out=t[2].ap, in0=C.ps[bI], in1=tC.ap, op=ALU.mult, reads=[C.pres[bI], tC.res], writes=[t[2].res])
                op("dve", "tensor_tensor", out=t[3].ap, in0=C.ps[bR], in1=tS.ap, op=ALU.mult, reads=[C.pres[bR], tS.res], writes=[t[3].res])
                op("pool", "tensor_tensor", out=t[0].ap, in0=t[0].ap, in1=t[1].ap, op=ALU.add, reads=[t[0].res, t[1].res], writes=[t[0].res])
                op("pool", "tensor_tensor", out=t[2].ap, in0=t[2].ap, in1=t[3].ap, op=ALU.subtract, reads=[t[2].res, t[3].res], writes=[t[2].res])
                rb = rho.ap[:, j:j + 1].broadcast_to([128, 512])
                op("dve", "tensor_tensor_scan", out=t[4].ap, data0=rb, data1=t[0].ap, initial=hst.ap[:, 2 * j:2 * j + 1], op0=ALU.mult,
                   op1=ALU.add, reads=[rho.res, t[0].res, (hst.res, j)], writes=[t[4].res])
                op("dve", "tensor_tensor_scan", out=t[5].ap, data0=rb, data1=t[2].ap, initial=hst.ap[:, 2 * j + 1:2 * j + 2], op0=ALU.mult,
                   op1=ALU.add, reads=[rho.res, t[2].res, (hst.res, j)], writes=[t[5].res])
                op("pool", "tensor_tensor", out=t[0].ap, in0=t[4].ap, in1=tC.ap, op=ALU.mult, reads=[t[4].res, tC.res], writes=[t[0].res])
                op("pool", "tensor_tensor", out=t[1].ap, in0=t[5].ap, in1=tS.ap, op=ALU.mult, reads=[t[5].res, tS.res], writes=[t[1].res])
                op("pool", "tensor_tensor", out=t[2].ap, in0=t[4].ap, in1=tS.ap, op=ALU.mult, reads=[t[4].res, tS.res], writes=[t[2].res])
                op("pool", "tensor_tensor", out=t[3].ap, in0=t[5].ap, in1=tC.ap, op=ALU.mult, reads=[t[5].res, tC.res], writes=[t[3].res])
                op("dve", "tensor_tensor", out=t[0].ap, in0=t[0].ap, in1=t[1].ap, op=ALU.subtract, reads=[t[0].res, t[1].res], writes=[t[0].res])
                op("dve", "tensor_tensor", out=t[2].ap, in0=t[2].ap, in1=t[3].ap, op=ALU.add, reads=[t[2].res, t[3].res], writes=[t[2].res])
                hr, hi = hbf[i2]
                op("act", "activation", out=hr.ap, in_=t[0].ap, func=AF.Copy, reads=[t[0].res], writes=[hr.res])
                op("act", "activation", out=hi.ap, in_=t[2].ap, func=AF.Copy, reads=[t[2].res], writes=[hi.res])
                op("act", "activation", out=hst.ap[:, 2 * j:2 * j + 1], in_=t[0].ap[:, 511:512], func=AF.Copy, reads=[t[0].res], writes=[(hst.res, j)])
                op("act", "activation", out=hst.ap[:, 2 * j + 1:2 * j + 2], in_=t[2].ap[:, 511:512], func=AF.Copy, reads=[t[2].res], writes=[(hst.res, j)])
                op("pe", "matmul", C.ps[yb], lhsT=lhs3[:, 64 + j, :], rhs=hr.ap, start=(jj == 0), stop=False,
                   reads=[lhs.res, hr.res], writes=[C.pres[yb]])
                op("pe", "matmul", C.ps[yb], lhsT=lhs3[:, 96 + j, :], rhs=hi.ap, start=False, stop=False,
                   reads=[lhs.res, hi.res], writes=[C.pres[yb]])
            op("pe", "matmul", C.ps[yb], lhsT=dD.ap[:, ch * 128:(ch + 1) * 128], rhs=uT3[:, ch, :], start=False, stop=True,
               reads=[dD.res, (uT.res, ch)], writes=[C.pres[yb]])
            op("act", "activation", out=gT3[:, ch, :], in_=C.ps[yb], func=AF.Gelu_apprx_tanh, reads=[C.pres[yb]], writes=[gT.res])
        if getattr(C, "stop_at", "") == "s5":
            return
        wglu = W["s5_w_glu"][e].rearrange("(kc p) f -> p kc f", p=128)
        for piece in range(2):
            wt, wv = load_w(wglu[:, :, piece * 512:(piece + 1) * 512], 8)
            for j in range(4):
                oc = piece * 4 + j
                bk = rotA()
                for kc in range(8):
                    op("pe", "matmul", C.ps[bk], lhsT=wv[:, kc, j * 128:(j + 1) * 128], rhs=gT3[:, kc, :], start=(kc == 0), stop=(kc == 7),
                       reads=[wt.res] + [gT.res], writes=[C.pres[bk]])
                sg_ = sig[oc % 2]
                op("act", "activation", out=sg_.ap, in_=C.ps[bk], func=AF.Sigmoid, bias=bglu.ap[:, oc:oc + 1], reads=[C.pres[bk], bglu.res],
                   writes=[sg_.res])
                op("dve", "tensor_tensor", out=featT3[:, 8 + oc, :], in0=gT3[:, oc, :], in1=sg_.ap, op=ALU.mult, reads=[gT.res, sg_.res],
                   writes=[(featT.res, "s", oc)])
        wout = W["ev_w_out"][e].rearrange("(kc p) f -> p kc f", p=128)
        frd = [(featT.res, "a", b_) for b_ in range(4)] + [(featT.res, "s", o_) for o_ in range(8)]
        for ds in range(4):
            wt, wv = load_w(wout[:, :, ds * 512:(ds + 1) * 512], NKC)
            for tq in range(4):
                bk = rotA()
                for kc in range(16):
                    op("pe", "matmul", C.ps[bk], lhsT=featT3[:, kc, tq * 128:(tq + 1) * 128], rhs=wv[:, kc, :], start=(kc == 0), stop=(kc == 15),
                       reads=[wt.res] + frd, writes=[C.pres[bk]])
                xb = xs[nxs[0] % 4]
                nxs[0] += 1
                r0 = tt * TT + tq * 128
                dma("sp", xb.ap, src[r0:r0 + 128, ds * 512:(ds + 1) * 512], writes=[xb.res])
                op("dve", "tensor_tensor", out=xb.ap, in0=C.ps[bk], in1=xb.ap, op=ALU.add, reads=[C.pres[bk], xb.res], writes=[xb.res])
                dma("sp", dst[r0:r0 + 128, ds * 512:(ds + 1) * 512], xb.ap, reads=[xb.res])


def odd_stage(C, l, src, dst):
    o = l // 2
    S, A, W = C.S, C.A, C.W
    T = C.T
    cf = C.cstf
    op, dma = S.op, S.dma
    S.barrier()
    A.reset()

    def L(name, cols, dt=F32):
        return A.alloc(name, cols, dt)
    gbc = L("gbc", D)
    load_bcast_row(C, gbc, W["norm_mix"][l], D)
    _xt = L("xt0", D)
    C.xt = [_xt, _xt]
    _xn = L("xn0", D, BF16)
    C.xn = [_xn, _xn]
    C.ss = [L(f"ss{i}", 4) for i in range(2)]
    hnT = L("hnT", NKC * TT, BF16)
    hnT3 = hnT.ap.rearrange("p (k t) -> p k t", k=NKC)
    wb = [L(f"wb{i}", NKC * 512, BF16) for i in range(2)]
    nwb = [0]

    def load_w(src_ap3, nk, ncol=512):
        t = wb[nwb[0] % 2]
        nwb[0] += 1
        v = t.ap[:, 0:nk * ncol].rearrange("p (k f) -> p k f", k=nk)
        dma("pool", v, src_ap3, writes=[t.res])
        return t, v
    abc, dtb, dbc = L("abc", 64), L("dtb", 64), L("dbc", 64)
    load_bcast_row(C, abc, W["m_a_log"][o], 64)
    load_bcast_row(C, dtb, W["m_dt_bias"][o], 64)
    load_bcast_row(C, dbc, W["m_d"][o], 64)
    op("act", "activation", out=abc.ap, in_=abc.ap, func=AF.Exp, reads=[abc.res], writes=[abc.res])
    op("dve", "tensor_scalar", out=abc.ap, in0=abc.ap, scalar1=-1.0, scalar2=None, op0=ALU.mult, reads=[abc.res], writes=[abc.res])
    ng = L("ng", 32)
    dma("sp", ng.ap, W["m_norm"][o].rearrange("(a p) -> p a", p=128), writes=[ng.res], allow_slow_non_contiguous=True)
    cw = L("cw", 48 * 4)
    cw3 = cw.ap.rearrange("p (a k) -> p a k", k=4)
    for k_ in range(4):
        dma("sp", cw3[:, :, k_], W["m_conv_w"][o][k_].rearrange("(a p) -> p a", p=128), writes=[cw.res], allow_slow_non_contiguous=True)
    cbias = L("cbias", 48)
    dma("sp", cbias.ap, W["m_conv_b"][o].rearrange("(a p) -> p a", p=128), writes=[cbias.res], allow_slow_non_contiguous=True)
    halo = L("halo", 48 * 3)
    halo3 = halo.ap.rearrange("p (a k) -> p a k", k=3)
    op("dve", "memset", halo.ap, 0.0, writes=[halo.res])
    state = L("state", 8 * 512)
    state3 = state.ap.rearrange("p (g f) -> p g f", g=8)
    op("dve", "memset", state.ap, 0.0, writes=[state.res])
    stbf = [L(f"stbf{i}", 512, BF16) for i in range(2)]
    trif = cf.ap[:, 256:384]
    onesf = cf.ap[:, 640:768]
    identF = cf.ap[:, 0:128]
    negm4 = L("negm4", 512)
    for i in range(4):
        op("dve", "tensor_copy", out=negm4.ap[:, i * 128:(i + 1) * 128], in_=cf.ap[:, 512:640], reads=[cf.res], writes=[negm4.res])
    cv = [L(f"cv{i}", 515) for i in range(2)]
    acc = [L(f"acc{i}", 512) for i in range(2)]
    BT, CT = L("BT", 8 * 512, BF16), L("CT", 8 * 512, BF16)
    BT3 = BT.ap.rearrange("p (g t) -> p g t", g=8)
    CT3 = CT.ap.rearrange("p (g t) -> p g t", g=8)
    Btok = L("Btok", 4 * 1024, BF16)
    Btok4 = Btok.ap.rearrange("p (c g n) -> p c g n", c=4, g=8)
    xTg = L("xTg", 4 * 512, BF16)
    xTg3 = xTg.ap.rearrange("p (j t) -> p j t", j=4)
    xtok = L("xtok", 4 * 512, BF16)
    xtok3 = xtok.ap.rearrange("p (c f) -> p c f", c=4)
    zs = L("zs", 4 * 512, BF16)
    zs3 = zs.ap.rearrange("p (c f) -> p c f", c=4)
    yT = L("yT", 32 * 512, BF16)
    yT3 = yT.ap.rearrange("p (a t) -> p a t", a=32)
    dtt, adt, acum, asum, ea, wdec, cdec = (L(n_, 256) for n_ in ("dtt", "adt", "acum", "asum", "ea", "wdec", "cdec"))
    v3 = lambda t: t.ap.rearrange("p (c h) -> p c h", c=4)
    R1 = [L(f"R1{i}", 512) for i in range(2)]
    sgt = R1
    Lb = L("Lb", 1024, BF16)
    Mb = L("Mb", 1024, BF16)
    Lb3 = Lb.ap.rearrange("p (h l) -> p h l", h=8)
    Mb3 = Mb.ap.rearrange("p (h l) -> p h l", h=8)
    cbt = L("cbt", 128, BF16)
    xdt, xw = L("xdt", 512, BF16), L("xw", 512, BF16)
    yo, yy = L("yo", 512), L("yy", 512)
    xd = yo
    ynb = L("ynb", 512, BF16)
    sq = ynb
    rs = L("rs", 4)
    xs = [L(f"xs{i}", 512) for i in range(2)] * 2
    nxs = [0]
    rot = make_rot([0, 1, 2, 3, 4, 5, 6, 7])
    w_in = W["m_w_in"][o].rearrange("(kc p) f -> p kc f", p=128)
    ncv = [0]

    def conv_chunk(bank, fcg, out_ap, out_res):
        i = ncv[0] % 2
        ncv[0] += 1
        c_, a_ = cv[i], acc[i]
        op("dve", "tensor_copy", out=c_.ap[:, 0:3], in_=halo3[:, fcg, :], reads=[(halo.res, fcg)], writes=[c_.res])
        op("act", "activation", out=c_.ap[:, 3:515], in_=C.ps[bank], func=AF.Copy, reads=[C.pres[bank]], writes=[c_.res])
        op("dve", "tensor_copy", out=halo3[:, fcg, :], in_=c_.ap[:, 512:515], reads=[c_.res], writes=[(halo.res, fcg)])
        op("dve", "tensor_scalar", out=a_.ap, in0=c_.ap[:, 0:512], scalar1=cw3[:, fcg, 0:1], scalar2=cbias.ap[:, fcg:fcg + 1],
           op0=ALU.mult, op1=ALU.add, reads=[c_.res, cw.res, cbias.res], writes=[a_.res])
        for k_ in range(1, 4):
            op("dve", "scalar_tensor_tensor", out=a_.ap, in0=c_.ap[:, k_:k_ + 512], scalar=cw3[:, fcg, k_:k_ + 1], in1=a_.ap,
               op0=ALU.mult, op1=ALU.add, reads=[c_.res, cw.res, a_.res], writes=[a_.res])
        op("act", "activation", out=out_ap, in_=a_.ap, func=AF.Silu, reads=[a_.res], writes=[out_res])

    for tt in range(T // TT):
        rows = slice(tt * TT, (tt + 1) * TT)
        norm_tile(C, src[rows, :], gbc, hnT, ("mixx", tt))
        for piece in range(4):
            wt, wv = load_w(w_in[:, :, 8192 + piece * 512:8192 + (piece + 1) * 512], NKC)
            for j in range(4):
                fcl = piece * 4 + j
                bk = rot()
                for kc in range(NKC):
                    op("pe", "matmul", C.ps[bk], lhsT=wv[:, kc, j * 128:(j + 1) * 128], rhs=hnT3[:, kc, :], start=(kc == 0),
                       stop=(kc == NKC - 1), reads=[hnT.res, wt.res], writes=[C.pres[bk]])
                if fcl < 8:
                    conv_chunk(bk, 32 + fcl, BT3[:, fcl, :], BT.res)
                else:
                    conv_chunk(bk, 32 + fcl, CT3[:, fcl - 8, :], CT.res)
        for c in range(4):
            bk = rot()
            for g in range(8):
                op("pe", "transpose", out=C.psb[bk][:, g * 128:(g + 1) * 128], in_=BT3[:, g, c * 128:(c + 1) * 128],
                   identity=C.ident.ap, reads=[BT.res, C.ident.res], writes=[C.pres[bk]])
            op("act", "activation", out=Btok4[:, c], in_=C.psb[bk].rearrange("p (g n) -> p g n", g=8), func=AF.Copy,
               reads=[C.pres[bk]], writes=[Btok.res])
        wt, wv = load_w(w_in[:, :, 10240:10304], NKC, 64)
        bk = rot()
        for c in range(4):
            for kc in range(NKC):
                op("pe", "matmul", C.ps[bk][:, c * 64:(c + 1) * 64], lhsT=hnT3[:, kc, c * 128:(c + 1) * 128], rhs=wv[:, kc, :],
                   start=(kc == 0), stop=(kc == NKC - 1), reads=[hnT.res, wt.res], writes=[C.pres[bk]])
        op("dve", "tensor_tensor", out=v3(dtt), in0=C.ps[bk][:, 0:256].rearrange("p (c h) -> p c h", c=4),
           in1=dtb.ap.unsqueeze(1).broadcast_to([128, 4, 64]), op=ALU.add, reads=[C.pres[bk], dtb.res], writes=[dtt.res])
        op("act", "activation", out=dtt.ap, in_=dtt.ap, func=AF.Exp, reads=[dtt.res], writes=[dtt.res])
        op("dve", "tensor_scalar", out=dtt.ap, in0=dtt.ap, scalar1=1.0, scalar2=None, op0=ALU.add, reads=[dtt.res], writes=[dtt.res])
        op("act", "activation", out=dtt.ap, in_=dtt.ap, func=AF.Ln, reads=[dtt.res], writes=[dtt.res])
        op("dve", "tensor_tensor", out=v3(adt), in0=v3(dtt), in1=abc.ap.unsqueeze(1).broadcast_to([128, 4, 64]), op=ALU.mult,
           reads=[dtt.res, abc.res], writes=[adt.res])
        bk = rot()
        op("pe", "matmul", C.ps[bk][:, 0:256], lhsT=trif, rhs=adt.ap, start=True, stop=True, reads=[cf.res, adt.res], writes=[C.pres[bk]])
        op("pe", "matmul", C.ps[bk][:, 256:512], lhsT=onesf, rhs=adt.ap, start=True, stop=True, reads=[cf.res, adt.res], writes=[C.pres[bk]])
        op("act", "activation", out=acum.ap, in_=C.ps[bk][:, 0:256], func=AF.Copy, reads=[C.pres[bk]], writes=[acum.res])
        op("act", "activation", out=asum.ap, in_=C.ps[bk][:, 256:512], func=AF.Copy, reads=[C.pres[bk]], writes=[asum.res])
        op("act", "activation", out=ea.ap, in_=acum.ap, func=AF.Exp, reads=[acum.res], writes=[ea.res])
        op("act", "activation", out=cdec.ap, in_=asum.ap, func=AF.Exp, reads=[asum.res], writes=[cdec.res])
        op("dve", "tensor_tensor", out=wdec.ap, in0=asum.ap, in1=acum.ap, op=ALU.subtract, reads=[asum.res, acum.res], writes=[wdec.res])
        op("act", "activation", out=wdec.ap, in_=wdec.ap, func=AF.Exp, reads=[wdec.res], writes=[wdec.res])
        op("dve", "tensor_tensor", out=wdec.ap, in0=wdec.ap, in1=dtt.ap, op=ALU.mult, reads=[wdec.res, dtt.res], writes=[wdec.res])
        for g in range(8):
            wt, wv = load_w(w_in[:, :, 4096 + g * 512:4096 + (g + 1) * 512], NKC)
            for j in range(4):
                bk = rot()
                for kc in range(NKC):
                    op("pe", "matmul", C.ps[bk], lhsT=wv[:, kc, j * 128:(j + 1) * 128], rhs=hnT3[:, kc, :], start=(kc == 0),
                       stop=(kc == NKC - 1), reads=[hnT.res, wt.res], writes=[C.pres[bk]])
                conv_chunk(bk, g * 4 + j, xTg3[:, j, :], xTg.res)
            for c in range(4):
                bk = rot()
                for j in range(4):
                    op("pe", "transpose", out=C.psb[bk][:, j * 128:(j + 1) * 128], in_=xTg3[:, j, c * 128:(c + 1) * 128],
                       identity=C.ident.ap, reads=[xTg.res, C.ident.res], writes=[C.pres[bk]])
                op("act", "activation", out=xtok3[:, c, :], in_=C.psb[bk][:, 0:512], func=AF.Copy, reads=[C.pres[bk]], writes=[xtok.res])
            wt, wv = load_w(w_in[:, :, g * 512:(g + 1) * 512], NKC)
            for c in range(4):
                bk = rot()
                for kc in range(NKC):
                    op("pe", "matmul", C.ps[bk], lhsT=hnT3[:, kc, c * 128:(c + 1) * 128], rhs=wv[:, kc, :], start=(kc == 0),
                       stop=(kc == NKC - 1), reads=[hnT.res, wt.res], writes=[C.pres[bk]])
                op("act", "activation", out=zs3[:, c, :], in_=C.ps[bk], func=AF.Silu, reads=[C.pres[bk]], writes=[zs.res])
            hs = slice(g * 8, (g + 1) * 8)
            for c in range(4):
                cs_ = slice(c * 128, (c + 1) * 128)
                for half in range(2):
                    h0 = g * 8 + half * 4
                    r1 = R1[half]
                    op("dve", "tensor_tensor", out=r1.ap.rearrange("p (h l) -> p h l", h=4), in0=trif.unsqueeze(1).broadcast_to([128, 4, 128]),
                       in1=v3(adt)[:, c, h0:h0 + 4].unsqueeze(2).broadcast_to([128, 4, 128]), op=ALU.mult, reads=[cf.res, adt.res], writes=[r1.res])
                    bk = rot()
                    op("pe", "matmul", C.ps[bk], lhsT=onesf, rhs=r1.ap, start=True, stop=False, reads=[cf.res, r1.res], writes=[C.pres[bk]])
                    op("pe", "matmul", C.ps[bk], lhsT=identF, rhs=negm4.ap, start=False, stop=True, reads=[cf.res, negm4.res], writes=[C.pres[bk]])
                    sg_ = sgt[half]
                    op("dve", "tensor_tensor", out=sg_.ap.rearrange("p (h l) -> p h l", h=4), in0=C.ps[bk].rearrange("p (h l) -> p h l", h=4),
                       in1=v3(acum)[:, c, h0:h0 + 4].unsqueeze(2).broadcast_to([128, 4, 128]), op=ALU.subtract,
                       reads=[C.pres[bk], acum.res], writes=[sg_.res])
                    op("act", "activation", out=Lb.ap[:, half * 512:(half + 1) * 512], in_=sg_.ap, func=AF.Exp, reads=[sg_.res], writes=[Lb.res])
                bk = rot()
                op("pe", "matmul", C.ps[bk][:, 0:128], lhsT=BT3[:, g, tt * 0 + c * 128:(c + 1) * 128], rhs=CT3[:, g, cs_], start=True, stop=True,
                   reads=[BT.res, CT.res], writes=[C.pres[bk]])
                op("act", "activation", out=cbt.ap, in_=C.ps[bk][:, 0:128], func=AF.Copy, reads=[C.pres[bk]], writes=[cbt.res])
                op("pool", "tensor_tensor", out=Mb3, in0=Lb3, in1=cbt.ap.unsqueeze(1).broadcast_to([128, 8, 128]), op=ALU.mult,
                   reads=[Lb.res, cbt.res], writes=[Mb.res])
                xv = xtok3[:, c, :].rearrange("p (h d) -> p h d", d=64)
                op("dve", "tensor_tensor", out=xdt.ap.rearrange("p (h d) -> p h d", d=64), in0=xv,
                   in1=v3(dtt)[:, c, hs].unsqueeze(2).broadcast_to([128, 8, 64]), op=ALU.mult, reads=[xtok.res, dtt.res], writes=[xdt.res])
                op("dve", "tensor_tensor", out=xw.ap.rearrange("p (h d) -> p h d", d=64), in0=xv,
                   in1=v3(wdec)[:, c, hs].unsqueeze(2).broadcast_to([128, 8, 64]), op=ALU.mult, reads=[xtok.res, wdec.res], writes=[xw.res])
                sb_ = stbf[c % 2]
                op("act", "activation", out=sb_.ap, in_=state3[:, g, :], func=AF.Copy, reads=[(state.res, g)], writes=[sb_.res])
                bo = rot()
                op("pe", "matmul", C.ps[bo], lhsT=CT3[:, g, cs_], rhs=sb_.ap, start=True, stop=True, reads=[CT.res, sb_.res], writes=[C.pres[bo]])
                op("dve", "tensor_tensor", out=yo.ap.rearrange("p (h d) -> p h d", d=64), in0=C.ps[bo].rearrange("p (h d) -> p h d", d=64),
                   in1=v3(ea)[:, c, hs].unsqueeze(2).broadcast_to([128, 8, 64]), op=ALU.mult, reads=[C.pres[bo], ea.res], writes=[yo.res])
                bd = rot()
                for h in range(8):
                    op("pe", "matmul", C.ps[bd][:, h * 64:(h + 1) * 64], lhsT=Mb3[:, h, :], rhs=xdt.ap[:, h * 64:(h + 1) * 64], start=True,
                       stop=True, reads=[Mb.res, xdt.res], writes=[C.pres[bd]])
                op("dve", "tensor_tensor", out=yy.ap, in0=C.ps[bd], in1=yo.ap, op=ALU.add, reads=[C.pres[bd], yo.res], writes=[yy.res])
                op("dve", "tensor_tensor", out=xd.ap.rearrange("p (h d) -> p h d", d=64), in0=xv,
                   in1=dbc.ap[:, hs].unsqueeze(2).broadcast_to([128, 8, 64]), op=ALU.mult, reads=[xtok.res, dbc.res], writes=[xd.res])
                op("dve", "tensor_tensor", out=yy.ap, in0=yy.ap, in1=xd.ap, op=ALU.add, reads=[yy.res, xd.res], writes=[yy.res])
                op("dve", "tensor_tensor", out=yy.ap, in0=yy.ap, in1=zs3[:, c, :], op=ALU.mult, reads=[yy.res, zs.res], writes=[yy.res])
                op("act", "activation", out=sq.ap, in_=yy.ap, func=AF.Square, accum_out=rs.ap[:, 0:1], reads=[yy.res], writes=[sq.res, rs.res])
                op("act", "activation", out=rs.ap[:, 1:2], in_=rs.ap[:, 0:1], func=AF.Sqrt, scale=1.0 / 512, bias=C.eps.ap[:, 0:1],
                   reads=[rs.res], writes=[rs.res])
                op("dve", "reciprocal", out=rs.ap[:, 2:3], in_=rs.ap[:, 1:2], reads=[rs.res], writes=[rs.res])
                op("dve", "tensor_scalar", out=ynb.ap, in0=yy.ap, scalar1=rs.ap[:, 2:3], scalar2=None, op0=ALU.mult, reads=[yy.res, rs.res],
                   writes=[ynb.res])
                bk = rot()
                for j in range(4):
                    op("pe", "transpose", out=C.psb[bk][:, j * 128:(j + 1) * 128], in_=ynb.ap[:, j * 128:(j + 1) * 128], identity=C.ident.ap,
                       reads=[ynb.res, C.ident.res], writes=[C.pres[bk]])
                for j in range(4):
                    op("act", "activation", out=yT3[:, g * 4 + j, cs_], in_=C.psb[bk][:, j * 128:(j + 1) * 128], func=AF.Copy,
                       scale=ng.ap[:, g * 4 + j:g * 4 + j + 1], reads=[C.pres[bk], ng.res], writes=[(yT.res, g)])
                bs = rot()
                op("pe", "matmul", C.ps[bs], lhsT=Btok4[:, c, g, :], rhs=xw.ap, start=True, stop=True, reads=[Btok.res, xw.res], writes=[C.pres[bs]])
                st3 = state3[:, g, :].rearrange("p (h d) -> p h d", d=64)
                op("dve", "tensor_tensor", out=st3, in0=st3, in1=v3(cdec)[:, c, hs].unsqueeze(2).broadcast_to([128, 8, 64]), op=ALU.mult,
                   reads=[(state.res, g), cdec.res], writes=[(state.res, g)])
                op("dve", "tensor_tensor", out=state3[:, g, :], in0=state3[:, g, :], in1=C.ps[bs], op=ALU.add,
                   reads=[(state.res, g), C.pres[bs]], writes=[(state.res, g)])
        wout = W["m_w_out"][o].rearrange("(kc p) f -> p kc f", p=128)
        yrd = [(yT.res, g_) for g_ in range(8)]
        for ds in range(4):
            w0t, w0v = load_w(wout[:, 0:16, ds * 512:(ds + 1) * 512], NKC)
            w1t, w1v = load_w(wout[:, 16:32, ds * 512:(ds + 1) * 512], NKC)
            for tq in range(4):
                bk = rot()
                for kc in range(32):
                    wv_ = w0v if kc < 16 else w1v
                    wt_ = w0t if kc < 16 else w1t
                    op("pe", "matmul", C.ps[bk], lhsT=yT3[:, kc, tq * 128:(tq + 1) * 128], rhs=wv_[:, kc % 16, :], start=(kc == 0), stop=(kc == 31),
                       reads=[wt_.res] + yrd, writes=[C.pres[bk]])
                xb = xs[nxs[0] % 4]
                nxs[0] += 1
                r0 = tt * TT + tq * 128
                dma("sp", xb.ap, src[r0:r0 + 128, ds * 512:(ds + 1) * 512], writes=[xb.res])
                op("dve", "tensor_tensor", out=xb.ap, in0=C.ps[bk], in1=xb.ap, op=ALU.add, reads=[C.pres[bk], xb.res], writes=[xb.res])
                dma("sp", dst[r0:r0 + 128, ds * 512:(ds + 1) * 512], xb.ap, reads=[xb.res])


WNAMES = ["norm_ffn1", "ffn1_gate", "ffn1_up", "ffn1_down", "norm_mix", "norm_ffn2", "ffn2_gate", "ffn2_up",
          "ffn2_down", "ev_w_in", "ev_sinks", "s5_a_re", "s5_a_im", "s5_log_dt", "s5_b_re", "s5_b_im",
          "s5_c_re", "s5_c_im", "s5_d", "s5_w_glu", "s5_b_glu", "ev_w_out", "m_w_in", "m_conv_w", "m_conv_b",
          "m_dt_bias", "m_a_log", "m_d", "m_norm", "m_w_out", "final_norm"]


def build_program(T, shapes, stages, stop_at=""):
    nc = bass.Bass("TRN2", target_bir_lowering=False)
    C = Ctx()
    C.stop_at = stop_at
    C.nc = nc
    C.T = T
    W = {n: nc.dram_tensor(n, list(shapes[n]), F32, kind="ExternalInput").ap() for n in WNAMES}
    xin = nc.dram_tensor("x", [T, D], F32, kind="ExternalInput").ap()
    pos = nc.dram_tensor("positions", [T], I32, kind="ExternalInput").ap()
    cst = nc.dram_tensor("cst", [128, 1024], F32, kind="ExternalInput").ap()
    y = nc.dram_tensor("y", [T, D], F32, kind="ExternalOutput").ap()
    xres = nc.dram_tensor("xres", [T, D], F32, kind="Internal").ap()
    C.W, C.pos, C.cst = W, pos, cst
    C.tabs = nc.dram_tensor("tabs", [32, 2, 128, 512], F32, kind="Internal").ap()
    C.lhsd = nc.dram_tensor("lhsd", [128, 4 * 32 * 128], BF16, kind="Internal").ap()
    C.smalld = nc.dram_tensor("smalld", [128, 64], F32, kind="Internal").ap()
    S = Sched(nc)
    C.S = S
    with contextlib.ExitStack() as st:
        arena_t = st.enter_context(nc.sbuf_tensor("arena", [128, 200 * 1024], U8))
        C.A = Arena(arena_t[:], 200 * 1024)
        pst = [st.enter_context(nc.psum_tensor(f"ps{i}", [128, 512], F32)) for i in range(8)]
        C.ps = [p[:] for p in pst]
        C.psb = [p[:].bitcast(BF16) for p in pst]
        C.pres = [f"psum{i}" for i in range(8)]
        C.bank_i = 0

        def next_bank():
            b = C.bank_i % 8
            C.bank_i += 1
            return b
        C.next_bank = next_bank
        C.cnt_norm = 0
        A = C.A
        cf = A.alloc("cstf", 1024, F32)
        S.dma("sp", cf.ap, cst, writes=[cf.res])
        C.cstf = cf
        C.ident = A.alloc("ident", 128, BF16)
        S.op("dve", "tensor_copy", out=C.ident.ap, in_=cf.ap[:, 0:128], reads=[cf.res], writes=[C.ident.res])
        C.eps = A.alloc("eps", 1, F32)
        S.op("dve", "tensor_copy", out=C.eps.ap, in_=cf.ap[:, 128:129], reads=[cf.res], writes=[C.eps.res])
        A.set_floor()
        C.last_out = None
        finals = []
        cur_src = xin
        for stg in stages:
            kind = stg[0]
            if kind == "ffn":
                l, which = stg[1], stg[2]
                ffn_stage(C, cur_src, xres, W[f"norm_ffn{which}"][l], W[f"ffn{which}_gate"][l],
                          W[f"ffn{which}_up"][l], W[f"ffn{which}_down"][l], tag=("ffn", l, which))
                cur_src = xres
            elif kind == "mix":
                l = stg[1]
                if l % 2 == 0:
                    even_stage(C, l, cur_src, xres)
                else:
                    odd_stage(C, l, cur_src, xres)
                cur_src = xres
            elif kind == "final":
                finals = final_norm_stage(C, cur_src, y, W["final_norm"])
            elif kind == "copyout":
                S.barrier()
                finals = [S.dma("sp", y, cur_src)]
        S.emit(final_waits=finals)
    return nc


def make_consts():
    c = np.zeros((128, 1024), np.float32)
    c[:, 0:128] = np.eye(128, dtype=np.float32)
    c[:, 128] = EPS
    p = np.arange(128)
    for g2 in range(2):
        c[:, 130 + g2] = (p // 64 == g2)
        c[:, 136 + g2] = ((p // 16) % 2 == g2)
    for jj in range(4):
        c[:, 132 + jj] = (p // 32 == jj)
    c[:, 140:148] = np.exp(-math.log(500000.0) * np.arange(8, dtype=np.float32) * (2.0 / 16)).astype(np.float32)[None, :]
    kj = p[:, None]
    qi = p[None, :]
    c[:, 256:384] = (kj <= qi)
    c[:, 384:512] = (kj > qi)
    c[:, 512:640] = np.where(kj > qi, -30000.0, 0.0)
    c[:, 640:768] = 1.0
    return c


def kernel(**inputs):
    x = np.ascontiguousarray(inputs["x"], dtype=np.float32)
    B, Sq, _ = x.shape
    MIXERS_READY = True
    stages = []
    for l in range(DEPTH):
        stages.append(("ffn", l, 1))
        if MIXERS_READY:
            stages.append(("mix", l))
        stages.append(("ffn", l, 2))
    stages.append(("final",))
    shapes = {n: inputs[n].shape for n in WNAMES}
    nc = build_program(Sq, shapes, stages)
    cst = make_consts()
    wmap = {n: np.ascontiguousarray(inputs[n], dtype=np.float32) for n in WNAMES}
    in_maps = []
    for b in range(B):
        m = dict(wmap)
        m["x"] = x[b]
        m["positions"] = np.ascontiguousarray(inputs["positions"][b], dtype=np.int32)
        m["cst"] = cst
        in_maps.append(m)
    res = run_bass_kernel_spmd(nc, in_maps, core_ids=list(range(B)))
    return np.stack([np.asarray(r["y"]) for r in res.results], axis=0).astype(np.float32)
```

```python
import contextlib
import math
import numpy as np
import concourse.bass as bass
import concourse.mybir as mybir
from concourse.bass_utils import run_bass_kernel_spmd

F32 = mybir.dt.float32
BF16 = mybir.dt.bfloat16
I32 = mybir.dt.int32
U8 = mybir.dt.uint8
AF = mybir.ActivationFunctionType
ALU = mybir.AluOpType

D = 2048
DFF = 5632
DEPTH = 4
EPS = 1e-5
NKC = D // 128
NFC = DFF // 128
TT = 512

ENGS = ("pe", "act", "dve", "pool", "sp")
DMA_K = 8
SEM_PHASE = 30000


class Sched:
    def __init__(self, nc):
        self.nc = nc
        self.prog = {e: [] for e in ENGS}
        self.last_w = {}
        self.readers = {}
        self.ndma = {e: 0 for e in ENGS}
        self.bar_deps = set()
        self.bar_pending = {e: False for e in ENGS}
        self.lastc = {}
        self.lastd = {}

    def barrier(self):
        deps = set()
        for e in ENGS:
            if self.lastc.get(e) is not None:
                deps.add((e, self.lastc[e]))
            for i in self.lastd.get(e, []):
                deps.add((e, i))
        self.bar_deps = deps
        self.bar_pending = {e: True for e in ENGS}
        self.last_w = {}
        self.readers = {}

    def _record(self, eng, kind, method, args, kwargs, reads, writes):
        px = [r for r in reads if isinstance(r, str) and r.startswith("psum")]
        if px:
            reads = [r for r in reads if r not in px]
            writes = list(writes) + px
        idx = len(self.prog[eng])
        deps = set()
        if self.bar_pending[eng]:
            deps |= self.bar_deps
            self.bar_pending[eng] = False
        for r in reads:
            w = self.last_w.get(r)
            if w is not None:
                deps.add(w)
        for r in writes:
            w = self.last_w.get(r)
            if w is not None:
                deps.add(w)
            rd = self.readers.get(r)
            if rd:
                for e2, i2 in rd.items():
                    deps.add((e2, i2))
        pe_sync = kwargs.pop("pe_sync", False) if kind == "c" else False
        if eng == "pe" and kind == "c" and not pe_sync:
            deps = {d for d in deps if not (d[0] == "pe" and self.prog["pe"][d[1]]["kind"] == "c")}
        deps.discard((eng, idx))
        rec = dict(kind=kind, method=method, args=args, kwargs=kwargs, deps=deps,
                   need_sig=False, dma_n=None)
        if kind == "d":
            rec["dma_n"] = self.ndma[eng]
            self.ndma[eng] += 1
        self.prog[eng].append(rec)
        if kind == "c":
            self.lastc[eng] = idx
        else:
            self.lastd[eng] = (self.lastd.get(eng, []) + [idx])[-DMA_K:]
        for r in reads:
            self.readers.setdefault(r, {})[eng] = idx
        for r in writes:
            self.last_w[r] = (eng, idx)
            self.readers[r] = {}
        return (eng, idx)

    def op(self, eng, method, *args, reads=(), writes=(), **kwargs):
        return self._record(eng, "c", method, args, kwargs, reads, writes)

    def dma(self, eng, out, in_, reads=(), writes=(), **kwargs):
        return self._record(eng, "d", "dma_start", (), dict(out=out, in_=in_, **kwargs), reads, writes)

    def emit(self, final_waits=()):
        nc = self.nc
        prog = self.prog
        for e in ENGS:
            for rec in prog[e]:
                for (e2, i2) in rec["deps"]:
                    prog[e2][i2]["need_sig"] = True
        for fw in final_waits:
            prog[fw[0]][fw[1]]["need_sig"] = True
        nphase = {}
        for e in ENGS:
            c = 0
            for rec in prog[e]:
                if rec["kind"] == "c" and rec["need_sig"]:
                    rec["sig"] = c
                    c += 1
            nphase[e] = (c + SEM_PHASE - 1) // SEM_PHASE
        with contextlib.ExitStack() as st:
            csems = {e: [st.enter_context(nc.semaphore(f"c_{e}_{p}")) for p in range(nphase[e])]
                     for e in ENGS}
            dsems = {e: ([st.enter_context(nc.semaphore(f"d_{e}_{k}")) for k in range(DMA_K)]
                         if self.ndma[e] else []) for e in ENGS}

            def sigof(e2, i2):
                rec = prog[e2][i2]
                if rec["kind"] == "c":
                    s = rec["sig"]
                    return csems[e2][s // SEM_PHASE], (s % SEM_PHASE) + 1, (e2, "c", s // SEM_PHASE)
                n = rec["dma_n"]
                return dsems[e2][n % DMA_K], 16 * (n // DMA_K + 1), (e2, "d", n % DMA_K)

            block = st.enter_context(nc.Block())
            engobj = dict(pe="tensor", act="scalar", dve="vector", pool="gpsimd", sp="sync")

            def make(e):
                def body(eng):
                    seen = {}
                    for rec in prog[e]:
                        waits = [sigof(e2, i2) for (e2, i2) in rec["deps"]]
                        if rec["kind"] == "d":
                            n = rec["dma_n"]
                            if n >= DMA_K:
                                waits.append((dsems[e][n % DMA_K], 16 * (n // DMA_K), (e, "d", n % DMA_K)))
                        best = {}
                        for sem, val, key in waits:
                            if seen.get(key, 0) >= val:
                                continue
                            if key not in best or best[key][1] < val:
                                best[key] = (sem, val)
                        for key, (sem, val) in best.items():
                            eng.wait_ge(sem, val)
                            seen[key] = val
                        ins = getattr(eng, rec["method"])(*rec["args"], **rec["kwargs"])
                        if rec["kind"] == "d":
                            n = rec["dma_n"]
                            ins.then_inc(dsems[e][n % DMA_K], 16)
                        elif rec["need_sig"]:
                            s = rec["sig"]
                            ins.then_inc(csems[e][s // SEM_PHASE], 1)
                    if e == "sp":
                        for fw in final_waits:
                            sem, val, key = sigof(*fw)
                            if seen.get(key, 0) < val:
                                eng.wait_ge(sem, val)
                                seen[key] = val
                return body

            for e in ENGS:
                if prog[e] or e == "sp":
                    getattr(block, engobj[e])(make(e))


class Tile:
    def __init__(self, ap, res):
        self.ap = ap
        self.res = res

    def __getitem__(self, k):
        return self.ap[k]


class Arena:
    def __init__(self, base_ap, nbytes):
        self.base = base_ap
        self.nbytes = nbytes
        self.off = 0
        self.gen = 0
        self.floor = 0

    def set_floor(self):
        self.floor = self.off

    def reset(self):
        self.off = self.floor
        self.gen += 1

    def alloc(self, name, cols, dtype, shape=None):
        esz = {F32: 4, BF16: 2, I32: 4}[dtype]
        nb = (cols * esz + 63) // 64 * 64
        assert self.off + nb <= self.nbytes, (name, self.off, nb, self.nbytes)
        ap = self.base[:, self.off:self.off + cols * esz].bitcast(dtype)
        self.off += nb
        return Tile(ap, f"{name}#{self.gen}")


class Ctx:
    pass


def norm_tile(C, src_rows, gcol, xnT, tagbase):
    S = C.S
    xnT3 = xnT.ap.rearrange("p (k t) -> p k t", k=NKC)
    for tq in range(4):
        i = C.cnt_norm
        C.cnt_norm += 1
        xt = C.xt[i % 2]
        xn = C.xn[i % 2]
        ss = C.ss[i % 2]
        S.dma("sp", xt.ap, src_rows[tq * 128:(tq + 1) * 128, :], reads=[tagbase], writes=[xt.res])
        S.op("act", "activation", out=xn.ap, in_=xt.ap, func=AF.Square, accum_out=ss.ap[:, 0:1],
             reads=[xt.res], writes=[xn.res, ss.res])
        S.op("act", "activation", out=ss.ap[:, 1:2], in_=ss.ap[:, 0:1], func=AF.Sqrt, scale=1.0 / D,
             bias=C.eps.ap[:, 0:1], reads=[ss.res], writes=[ss.res])
        S.op("dve", "reciprocal", out=ss.ap[:, 2:3], in_=ss.ap[:, 1:2], reads=[ss.res], writes=[ss.res])
        S.op("dve", "tensor_scalar", out=xn.ap, in0=xt.ap, scalar1=ss.ap[:, 2:3], scalar2=None, op0=ALU.mult,
             reads=[xt.res, ss.res], writes=[xn.res])
        for half in range(2):
            bank = C.next_bank()
            pb = C.psb[bank]
            for j in range(8):
                dc = half * 8 + j
                S.op("pe", "transpose", out=pb[:, j * 128:(j + 1) * 128], in_=xn.ap[:, dc * 128:(dc + 1) * 128],
                     identity=C.ident.ap, reads=[xn.res, C.ident.res], writes=[C.pres[bank]])
            dst = xnT3[:, half * 8:(half + 1) * 8, tq * 128:(tq + 1) * 128]
            srcv = pb.rearrange("p (j t) -> p j t", j=8)
            S.op("dve", "tensor_tensor", out=dst, in0=srcv,
                 in1=gcol.ap[:, half * 8:(half + 1) * 8].unsqueeze(2).broadcast_to([128, 8, 128]), op=ALU.mult,
                 reads=[C.pres[bank], gcol.res], writes=[xnT.res])


def load_gcol(C, tile, gain_row):
    C.S.dma("sp", tile.ap, gain_row.rearrange("(a p) -> p a", p=128), writes=[tile.res], allow_slow_non_contiguous=True)


def load_bcast_row(C, tile, src_row_ap, n):
    C.S.dma("sp", tile.ap[:, 0:n], src_row_ap.partition_broadcast(128), writes=[tile.res])


def ffn_stage(C, src, dst, gain_row, wg, wu, wd, tag):
    S, A = C.S, C.A
    S.barrier()
    A.reset()
    T = C.T
    gbc = A.alloc("gcol", NKC, F32)
    load_gcol(C, gbc, gain_row)
    C.xt = [A.alloc(f"xt{i}", D, F32) for i in range(2)]
    C.xn = [A.alloc(f"xn{i}", D, BF16) for i in range(2)]
    C.ss = [A.alloc(f"ss{i}", 4, F32) for i in range(2)]
    xnTs = [A.alloc(f"xnT{i}", NKC * TT, BF16) for i in range(2)]
    hT = A.alloc("hT", NFC * TT, BF16)
    hT3 = hT.ap.rearrange("p (f t) -> p f t", f=NFC)
    wgt = [A.alloc(f"wg{i}", NKC * 256, BF16) for i in range(2)]
    wut = [A.alloc(f"wu{i}", NKC * 256, BF16) for i in range(2)]
    wdt = [A.alloc(f"wd{i}", 4 * 512, BF16) for i in range(3)]
    sg = [A.alloc(f"sg{i}", TT, F32) for i in range(2)]
    xs = [A.alloc(f"xs{i}", 512, F32) for i in range(4)]
    wgv = wg.rearrange("(kc p) f -> p kc f", p=128)
    wuv = wu.rearrange("(kc p) f -> p kc f", p=128)
    wdv = wd.rearrange("(fc p) d -> p fc d", p=128)
    nw = 0
    nd = 0
    nx = 0
    for tt in range(T // TT):
        rows = slice(tt * TT, (tt + 1) * TT)
        xnT = xnTs[tt % 2]
        xnT3 = xnT.ap.rearrange("p (k t) -> p k t", k=NKC)
        norm_tile(C, src[rows, :], gbc, xnT, (tag, "x", tt) if src is dst else ("xin", tt))
        for fg in range(NFC // 2):
            wgb, wub = wgt[nw % 2], wut[nw % 2]
            nw += 1
            S.dma("pool", wgb.ap.rearrange("p (k f) -> p k f", k=NKC), wgv[:, :, fg * 256:(fg + 1) * 256],
                  writes=[wgb.res])
            S.dma("pool", wub.ap.rearrange("p (k f) -> p k f", k=NKC), wuv[:, :, fg * 256:(fg + 1) * 256],
                  writes=[wub.res])
            wg3 = wgb.ap.rearrange("p (k f) -> p k f", k=NKC)
            wu3 = wub.ap.rearrange("p (k f) -> p k f", k=NKC)
            for j in range(2):
                fc = fg * 2 + j
                bg = C.next_bank()
                bu = C.next_bank()
                for kc in range(NKC):
                    S.op("pe", "matmul", C.ps[bg], lhsT=wg3[:, kc, j * 128:(j + 1) * 128], rhs=xnT3[:, kc, :],
                         start=(kc == 0), stop=(kc == NKC - 1), reads=[wgb.res, xnT.res], writes=[C.pres[bg]])
                for kc in range(NKC):
                    S.op("pe", "matmul", C.ps[bu], lhsT=wu3[:, kc, j * 128:(j + 1) * 128], rhs=xnT3[:, kc, :],
                         start=(kc == 0), stop=(kc == NKC - 1), reads=[wub.res, xnT.res], writes=[C.pres[bu]])
                sgt = sg[fc % 2]
                S.op("act", "activation", out=sgt.ap, in_=C.ps[bg], func=AF.Silu, reads=[C.pres[bg]], writes=[sgt.res])
                S.op("dve", "tensor_tensor", out=hT3[:, fc, :], in0=sgt.ap, in1=C.ps[bu], op=ALU.mult,
                     reads=[sgt.res, C.pres[bu]], writes=[(hT.res, fc)])
        for q in range(4):
            banks = [C.next_bank() for _ in range(4)]
            for fcg in range(NFC // 4):
                wdb = wdt[nd % 3]
                nd += 1
                S.dma("pool", wdb.ap.rearrange("p (f d) -> p f d", f=4),
                      wdv[:, fcg * 4:(fcg + 1) * 4, q * 512:(q + 1) * 512], writes=[wdb.res])
                wd3 = wdb.ap.rearrange("p (f d) -> p f d", f=4)
                for j in range(4):
                    fc = fcg * 4 + j
                    for tq in range(4):
                        S.op("pe", "matmul", C.ps[banks[tq]], lhsT=hT3[:, fc, tq * 128:(tq + 1) * 128], rhs=wd3[:, j, :],
                             start=(fc == 0), stop=(fc == NFC - 1), reads=[(hT.res, fc), wdb.res],
                             writes=[C.pres[banks[tq]]])
            for tq in range(4):
                xb = xs[nx % 4]
                nx += 1
                r0 = tt * TT + tq * 128
                S.dma("sp", xb.ap, src[r0:r0 + 128, q * 512:(q + 1) * 512],
                      reads=[(tag, "x", tt) if src is dst else ("xin", tt)], writes=[xb.res])
                S.op("dve", "scalar_tensor_tensor", out=xb.ap, in0=C.ps[banks[tq]], scalar=0.5, in1=xb.ap,
                     op0=ALU.mult, op1=ALU.add, reads=[C.pres[banks[tq]], xb.res], writes=[xb.res])
                C.last_out = S.dma("sp", dst[r0:r0 + 128, q * 512:(q + 1) * 512], xb.ap, reads=[xb.res],
                                   writes=[(tag, "xo", tt, tq, q)])


def final_norm_stage(C, src, dst, gain_row):
    S, A = C.S, C.A
    S.barrier()
    A.reset()
    gbc = A.alloc("gbc", D, F32)
    load_bcast_row(C, gbc, gain_row, D)
    xt = [A.alloc(f"xt{i}", D, F32) for i in range(3)]
    junk = A.alloc("junk", D, BF16)
    ss = [A.alloc(f"ss{i}", 4, F32) for i in range(3)]
    outs = []
    for i in range(C.T // 128):
        x, s_ = xt[i % 3], ss[i % 3]
        S.dma("sp", x.ap, src[i * 128:(i + 1) * 128, :], writes=[x.res])
        S.op("act", "activation", out=junk.ap, in_=x.ap, func=AF.Square, accum_out=s_.ap[:, 0:1],
             reads=[x.res], writes=[junk.res, s_.res])
        S.op("act", "activation", out=s_.ap[:, 1:2], in_=s_.ap[:, 0:1], func=AF.Sqrt, scale=1.0 / D,
             bias=C.eps.ap[:, 0:1], reads=[s_.res], writes=[s_.res])
        S.op("dve", "reciprocal", out=s_.ap[:, 2:3], in_=s_.ap[:, 1:2], reads=[s_.res], writes=[s_.res])
        S.op("dve", "scalar_tensor_tensor", out=x.ap, in0=x.ap, scalar=s_.ap[:, 2:3], in1=gbc.ap,
             op0=ALU.mult, op1=ALU.mult, reads=[x.res, s_.res, gbc.res], writes=[x.res])
        outs.append(S.dma("sp", dst[i * 128:(i + 1) * 128, :], x.ap, reads=[x.res]))
    return outs


PI = math.pi
TWO_PI_HI = 6.28125
TWO_PI_LO = 2.0 * math.pi - 6.28125


def wrap_pi(S, t, m, shape_ap=None):
    S.op("dve", "tensor_scalar", out=m.ap, in0=t.ap, scalar1=PI, scalar2=-2.0 * PI, op0=ALU.is_gt, op1=ALU.mult,
         reads=[t.res], writes=[m.res])
    S.op("dve", "tensor_tensor", out=t.ap, in0=t.ap, in1=m.ap, op=ALU.add, reads=[t.res, m.res], writes=[t.res])
    S.op("dve", "tensor_scalar", out=m.ap, in0=t.ap, scalar1=-PI, scalar2=2.0 * PI, op0=ALU.is_lt, op1=ALU.mult,
         reads=[t.res], writes=[m.res])
    S.op("dve", "tensor_tensor", out=t.ap, in0=t.ap, in1=m.ap, op=ALU.add, reads=[t.res, m.res], writes=[t.res])


def make_rot(names_banks):
    st = {"i": 0}

    def f():
        b = names_banks[st["i"] % len(names_banks)]
        st["i"] += 1
        return b
    return f


def even_stage(C, l, src, dst):
    e = l // 2
    S, A, W = C.S, C.A, C.W
    T = C.T
    NB = T // 128
    cf = C.cstf
    op, dma = S.op, S.dma
    tabs = C.tabs
    lhsd = C.lhsd
    smalld = C.smalld
    S.barrier()
    A.reset()

    def L(name, cols, dt=F32):
        return A.alloc(name, cols, dt)
    are, aim, ldt = L("are", 32), L("aim", 32), L("ldt", 32)
    for g2 in range(2):
        ps_ = slice(g2 * 64, (g2 + 1) * 64)
        dma("sp", are.ap[ps_, :], W["s5_a_re"][e].rearrange("(j g) p -> g p j", g=2)[g2], writes=[are.res],
            allow_slow_non_contiguous=True)
        dma("sp", aim.ap[ps_, :], W["s5_a_im"][e].rearrange("(j g) p -> g p j", g=2)[g2], writes=[aim.res],
            allow_slow_non_contiguous=True)
        dma("sp", ldt.ap[ps_, :], W["s5_log_dt"][e].rearrange("(j g) -> g j", g=2)[g2].partition_broadcast(64),
            writes=[ldt.res], allow_slow_non_contiguous=True)
    dt_, mag, th, m_, thc = L("dt", 32), L("mag", 32), L("th", 32), L("m", 32), L("thc", 32)
    cs, sn = L("cs", 32), L("sn", 32)
    op("act", "activation", out=dt_.ap, in_=ldt.ap, func=AF.Exp, reads=[ldt.res], writes=[dt_.res])
    op("dve", "tensor_tensor", out=mag.ap, in0=are.ap, in1=dt_.ap, op=ALU.mult, reads=[are.res, dt_.res], writes=[mag.res])
    op("act", "activation", out=mag.ap, in_=mag.ap, func=AF.Exp, reads=[mag.res], writes=[mag.res])
    op("dve", "tensor_tensor", out=th.ap, in0=aim.ap, in1=dt_.ap, op=ALU.mult, reads=[aim.res, dt_.res], writes=[th.res])
    for _ in range(5):
        wrap_pi(S, th, m_)
    op("dve", "tensor_scalar", out=thc.ap, in0=th.ap, scalar1=PI / 2, scalar2=None, op0=ALU.add, reads=[th.res], writes=[thc.res])
    wrap_pi(S, thc, m_)
    op("act", "activation", out=sn.ap, in_=th.ap, func=AF.Sin, reads=[th.res], writes=[sn.res])
    op("act", "activation", out=cs.ap, in_=thc.ap, func=AF.Sin, reads=[thc.res], writes=[cs.res])
    abr, abi, den, cre, cim, t1_, t2_ = L("abr", 32), L("abi", 32), L("den", 32), L("cre", 32), L("cim", 32), L("t1_", 32), L("t2_", 32)
    TTm = lambda o, a, b, o_: op("dve", "tensor_tensor", out=o.ap, in0=a.ap, in1=b.ap, op=o_, reads=[a.res, b.res], writes=[o.res])
    TTm(abr, mag, cs, ALU.mult)
    TTm(abi, mag, sn, ALU.mult)
    op("dve", "tensor_scalar", out=t1_.ap, in0=abr.ap, scalar1=-1.0, scalar2=None, op0=ALU.add, reads=[abr.res], writes=[t1_.res])
    TTm(den, are, are, ALU.mult)
    TTm(t2_, aim, aim, ALU.mult)
    TTm(den, den, t2_, ALU.add)
    op("dve", "reciprocal", out=den.ap, in_=den.ap, reads=[den.res], writes=[den.res])
    TTm(cre, t1_, are, ALU.mult)
    TTm(t2_, abi, aim, ALU.mult)
    TTm(cre, cre, t2_, ALU.add)
    TTm(cre, cre, den, ALU.mult)
    TTm(cim, abi, are, ALU.mult)
    TTm(t2_, t1_, aim, ALU.mult)
    TTm(cim, cim, t2_, ALU.subtract)
    TTm(cim, cim, den, ALU.mult)
    dma("sp", smalld[:, 0:32], mag.ap, reads=[mag.res], writes=["smalld"])
    bre, bim, bbr, bbi, tb = L("bre", 512), L("bim", 512), L("bbr", 512), L("bbi", 512), L("tb", 512)
    v3 = lambda t: t.ap.rearrange("p (j c) -> p j c", j=32)
    for g2 in range(2):
        ps_ = slice(g2 * 64, (g2 + 1) * 64)
        dma("sp", v3(bre)[ps_], W["s5_b_re"][e].rearrange("(j g) p c -> g p j c", g=2)[g2], writes=[bre.res])
        dma("sp", v3(bim)[ps_], W["s5_b_im"][e].rearrange("(j g) p c -> g p j c", g=2)[g2], writes=[bim.res])
    bc3 = lambda t: t.ap.unsqueeze(2).broadcast_to([128, 32, 16])

    def TB(o, a, b_bc_tile, o_):
        op("dve", "tensor_tensor", out=v3(o), in0=v3(a), in1=bc3(b_bc_tile), op=o_, reads=[a.res, b_bc_tile.res], writes=[o.res])
    TB(bbr, bre, cre, ALU.mult)
    TB(tb, bim, cim, ALU.mult)
    TTm(bbr, bbr, tb, ALU.subtract)
    TB(bbi, bim, cre, ALU.mult)
    TB(tb, bre, cim, ALU.mult)
    TTm(bbi, bbi, tb, ALU.add)
    X = [L(f"X{i}", 128) for i in range(2)]
    lo = [L(f"lo{i}", 512, BF16) for i in range(2)]
    cnat = [L(f"cnat{i}", 64) for i in range(2)]
    identF = cf.ap[:, 0:128]
    prot = make_rot([0, 1, 2, 3, 4, 5, 6, 7])
    k = 0
    for ch in range(8):
        for ri, bb in enumerate((bbr, bbi)):
            Xt = X[k % 2]
            lot = lo[k % 2]
            k += 1
            X4 = Xt.ap.rearrange("p (j g c) -> p j g c", j=4, g=2)
            for g2 in range(2):
                op("dve", "tensor_scalar", out=X4[:, :, g2, :], in0=v3(bb)[:, 4 * ch:4 * ch + 4, :], scalar1=cf.ap[:, 130 + g2:131 + g2],
                   scalar2=None, op0=ALU.mult, reads=[bb.res, cf.res], writes=[Xt.res])
            bk = prot()
            op("pe", "transpose", out=C.ps[bk][:, 0:128], in_=Xt.ap, identity=identF, reads=[Xt.res, cf.res], writes=[C.pres[bk]])
            for jj in range(4):
                op("dve", "tensor_scalar", out=lot.ap[:, jj * 128:(jj + 1) * 128], in0=C.ps[bk][:, 0:128],
                   scalar1=cf.ap[:, 132 + jj:133 + jj], scalar2=None, op0=ALU.mult, reads=[C.pres[bk], cf.res], writes=[lot.res])
            dma("sp", lhsd[:, (ri * 32 + ch * 4) * 128:(ri * 32 + ch * 4 + 4) * 128], lot.ap, reads=[lot.res], writes=["lhsd"])
        for ri, (cw, sgn) in enumerate(((W["s5_c_re"], 1.0), (W["s5_c_im"], -1.0))):
            Xt = X[k % 2]
            lot = lo[k % 2]
            cn = cnat[k % 2]
            k += 1
            dma("sp", cn.ap, cw[e].rearrange("g c p -> (g c) p")[ch * 128:(ch + 1) * 128, :], writes=[cn.res])
            for g2 in range(2):
                op("dve", "tensor_scalar", out=Xt.ap[:, g2 * 64:(g2 + 1) * 64], in0=cn.ap, scalar1=cf.ap[:, 136 + g2:137 + g2],
                   scalar2=sgn, op0=ALU.mult, op1=ALU.mult, reads=[cn.res, cf.res], writes=[Xt.res])
            bk = prot()
            op("pe", "transpose", out=C.ps[bk][:, 0:128], in_=Xt.ap, identity=identF, reads=[Xt.res, cf.res], writes=[C.pres[bk]])
            op("dve", "memset", lot.ap, 0.0, writes=[lot.res])
            for jj in range(4):
                op("dve", "tensor_copy", out=lot.ap[:, jj * 128 + jj * 32:jj * 128 + jj * 32 + 32], in_=C.ps[bk][:, jj * 32:jj * 32 + 32],
                   reads=[C.pres[bk]], writes=[lot.res])
            dma("sp", lhsd[:, ((2 + ri) * 32 + ch * 4) * 128:((2 + ri) * 32 + ch * 4 + 4) * 128], lot.ap, reads=[lot.res], writes=["lhsd"])
    NP = 8
    Ct, St, Ut, Vt = L("Ct", NP * 512), L("St", NP * 512), L("Ut", NP * 256), L("Vt", NP * 256)
    C3 = Ct.ap.rearrange("p (j t) -> p j t", j=NP)
    S3 = St.ap.rearrange("p (j t) -> p j t", j=NP)
    U3 = Ut.ap.rearrange("p (j t) -> p j t", j=NP)
    V3 = Vt.ap.rearrange("p (j t) -> p j t", j=NP)
    for q8 in range(32 // NP):
        js = slice(q8 * NP, (q8 + 1) * NP)
        op("dve", "tensor_copy", out=C3[:, :, 0:1], in_=cs.ap[:, js].unsqueeze(2), reads=[cs.res], writes=[Ct.res])
        op("dve", "tensor_copy", out=S3[:, :, 0:1], in_=sn.ap[:, js].unsqueeze(2), reads=[sn.res], writes=[St.res])
        n = 1
        while n < 512:
            cn_b = C3[:, :, n - 1:n].broadcast_to([128, NP, n])
            sn_b = S3[:, :, n - 1:n].broadcast_to([128, NP, n])
            rw = [Ct.res, St.res]
            op("dve", "tensor_tensor", out=U3[:, :, 0:n], in0=S3[:, :, 0:n], in1=sn_b, op=ALU.mult, reads=rw, writes=[Ut.res])
            op("dve", "tensor_tensor", out=V3[:, :, 0:n], in0=C3[:, :, 0:n], in1=sn_b, op=ALU.mult, reads=rw, writes=[Vt.res])
            op("dve", "tensor_tensor", out=C3[:, :, n:2 * n], in0=C3[:, :, 0:n], in1=cn_b, op=ALU.mult, reads=rw, writes=[Ct.res])
            op("dve", "tensor_tensor", out=S3[:, :, n:2 * n], in0=S3[:, :, 0:n], in1=cn_b, op=ALU.mult, reads=rw, writes=[St.res])
            op("dve", "tensor_tensor", out=C3[:, :, n:2 * n], in0=C3[:, :, n:2 * n], in1=U3[:, :, 0:n], op=ALU.subtract,
               reads=[Ct.res, Ut.res], writes=[Ct.res])
            op("dve", "tensor_tensor", out=S3[:, :, n:2 * n], in0=S3[:, :, n:2 * n], in1=V3[:, :, 0:n], op=ALU.add,
               reads=[St.res, Vt.res], writes=[St.res])
            n *= 2
        dma("sp", tabs[js, 0].rearrange("j p t -> p j t"), C3, reads=[Ct.res], writes=["tabs"])
        dma("sp", tabs[js, 1].rearrange("j p t -> p j t"), S3, reads=[St.res], writes=["tabs"])
    if getattr(C, "stop_at", "") == "prep":
        return
    S.barrier()
    A.reset()
    gbc = L("gcol", NKC)
    load_gcol(C, gbc, W["norm_mix"][l])
    _xt = L("xt0", D)
    C.xt = [_xt, _xt]
    _xn = L("xn0", D, BF16)
    C.xn = [_xn, _xn]
    C.ss = [L(f"ss{i}", 4) for i in range(2)]
    hnT = L("hnT", NKC * TT, BF16)
    hnT3 = hnT.ap.rearrange("p (k t) -> p k t", k=NKC)
    wb = [L(f"wb{i}", NKC * 512, BF16) for i in range(2)]
    nwb = [0]

    def load_w(src_ap3, nk):
        t = wb[nwb[0] % 2]
        nwb[0] += 1
        v = t.ap[:, 0:nk * 512].rearrange("p (k f) -> p k f", k=nk)
        dma("pool", v, src_ap3, writes=[t.res])
        return t, v
    lhs = L("lhs", 4 * 32 * 128, BF16)
    dma("sp", lhs.ap, lhsd, reads=["lhsd"], writes=[lhs.res])
    lhs3 = lhs.ap.rearrange("p (m k) -> p m k", k=128)
    rho = L("rho", 32)
    dma("sp", rho.ap, smalld[:, 0:32], reads=["smalld"], writes=[rho.res])
    hst = L("hst", 64)
    op("dve", "memset", hst.ap, 0.0, writes=[hst.res])
    dcol = L("dcol", 8)
    dma("sp", dcol.ap, W["s5_d"][e].rearrange("g c -> (g c)").rearrange("(a p) -> p a", p=128), writes=[dcol.res],
        allow_slow_non_contiguous=True)
    bglu = L("bglu", 8)
    dma("sp", bglu.ap, W["s5_b_glu"][e].rearrange("(a p) -> p a", p=128), writes=[bglu.res], allow_slow_non_contiguous=True)
    dD = L("dD", 8 * 128, BF16)
    for ch in range(8):
        op("dve", "tensor_scalar", out=dD.ap[:, ch * 128:(ch + 1) * 128], in0=cf.ap[:, 0:128], scalar1=dcol.ap[:, ch:ch + 1],
           scalar2=None, op0=ALU.mult, reads=[cf.res, dcol.res], writes=[dD.res])
    if getattr(C, "stop_at", "") == "setup1":
        return
    esk = L("esk", 16)
    load_bcast_row(C, esk, W["ev_sinks"][e], 16)
    op("act", "activation", out=esk.ap, in_=esk.ap, func=AF.Exp, reads=[esk.res], writes=[esk.res])
    mcur, mprev = L("mcur", 128, BF16), L("mprev", 128, BF16)
    op("dve", "tensor_copy", out=mcur.ap, in_=cf.ap[:, 256:384], reads=[cf.res], writes=[mcur.res])
    op("dve", "tensor_copy", out=mprev.ap, in_=cf.ap[:, 384:512], reads=[cf.res], writes=[mprev.res])
    if getattr(C, "stop_at", "") == "setup2":
        return
    posi = L("posi", NB, I32)
    dma("sp", posi.ap, C.pos.rearrange("(b p) -> p b", p=128), writes=[posi.res], allow_slow_non_contiguous=True)
    posf = L("posf", NB)
    op("dve", "tensor_copy", out=posf.ap, in_=posi.ap, reads=[posi.res], writes=[posf.res])
    ang, qf, mm_ = L("ang", NB * 8), L("qf", NB * 8), L("mm_", NB * 8)
    angc = qf
    rsin, rcos = L("rsin", NB * 8), L("rcos", NB * 8)
    a3 = lambda t: t.ap.rearrange("p (b i) -> p b i", i=8)
    op("dve", "tensor_tensor", out=a3(ang), in0=posf.ap.unsqueeze(2).broadcast_to([128, NB, 8]),
       in1=cf.ap[:, 140:148].unsqueeze(1).broadcast_to([128, NB, 8]), op=ALU.mult, reads=[posf.res, cf.res], writes=[ang.res])
    op("dve", "tensor_scalar", out=qf.ap, in0=ang.ap, scalar1=1.0 / (2 * PI), scalar2=None, op0=ALU.mult, reads=[ang.res], writes=[qf.res])
    op("dve", "tensor_scalar", out=qf.ap, in0=qf.ap, scalar1=12582912.0, scalar2=None, op0=ALU.add, reads=[qf.res], writes=[qf.res])
    op("dve", "tensor_scalar", out=qf.ap, in0=qf.ap, scalar1=-12582912.0, scalar2=None, op0=ALU.add, reads=[qf.res], writes=[qf.res])
    op("dve", "scalar_tensor_tensor", out=ang.ap, in0=qf.ap, scalar=-TWO_PI_HI, in1=ang.ap, op0=ALU.mult, op1=ALU.add,
       reads=[qf.res, ang.res], writes=[ang.res])
    op("dve", "scalar_tensor_tensor", out=ang.ap, in0=qf.ap, scalar=-TWO_PI_LO, in1=ang.ap, op0=ALU.mult, op1=ALU.add,
       reads=[qf.res, ang.res], writes=[ang.res])
    wrap_pi(S, ang, mm_)
    wrap_pi(S, ang, mm_)
    op("dve", "tensor_scalar", out=angc.ap, in0=ang.ap, scalar1=PI / 2, scalar2=None, op0=ALU.add, reads=[ang.res], writes=[angc.res])
    wrap_pi(S, angc, mm_)
    for t_ in (ang, angc):
        op("dve", "tensor_scalar", out=t_.ap, in0=t_.ap, scalar1=-PI, scalar2=PI, op0=ALU.max, op1=ALU.min, reads=[t_.res], writes=[t_.res])
    op("act", "activation", out=rsin.ap, in_=ang.ap, func=AF.Sin, reads=[ang.res], writes=[rsin.res])
    op("act", "activation", out=rcos.ap, in_=angc.ap, func=AF.Sin, reads=[angc.res], writes=[rcos.res])
    qtok = L("qtok", 4 * 1024, BF16)
    qtok3 = qtok.ap.rearrange("p (b f) -> p b f", b=4)
    kdup = [L(f"kdup{i}", 512, BF16) for i in range(2)]
    vext = L("vext", 5 * 4 * 65, BF16)
    vext4 = vext.ap.rearrange("p (s k d) -> p s k d", s=5, k=4)
    op("dve", "memset", vext.ap, 1.0, writes=[vext.res])
    qT = L("qT", 8 * 512, BF16)
    qT3 = qT.ap.rearrange("p (c t) -> p c t", c=8)
    kT = L("kT", 4 * 640, BF16)
    kT3 = kT.ap.rearrange("p (k t) -> p k t", k=4)
    PTb = [L(f"PT{i}", 512, BF16) for i in range(2)] * 2
    Eb = [L(f"E{i}", 512, BF16) for i in range(2)]
    _atok = L("atok0", 1024, BF16)
    atok = [_atok, _atok]
    featT = L("featT", 16 * 512, BF16)
    featT3 = featT.ap.rearrange("p (c t) -> p c t", c=16)
    den4 = [L(f"den4{i}", 8) for i in range(2)]
    rt = [L(f"rt{i}", 2 * 64 * 2 + 16) for i in range(2)]
    uT = L("uT", 8 * 512, BF16)
    uT3 = uT.ap.rearrange("p (c t) -> p c t", c=8)
    gT = Tile(qtok.ap, qtok.res)
    gT3 = gT.ap.rearrange("p (c t) -> p c t", c=8)
    tabC = [L(f"tabC{i}", 512) for i in range(2)]
    tabS = [L(f"tabS{i}", 512) for i in range(2)]
    s5t = [[L(f"s5t{i}_{k_}", 512) for k_ in range(6)] for i in range(2)]
    hbf = [[L(f"hbf{i}_{k_}", 512, BF16) for k_ in range(2)] for i in range(2)]
    sig = [L(f"sig{i}", 512, BF16) for i in range(2)]
    xs = [L(f"xs{i}", 512) for i in range(2)] * 2
    rotA = make_rot([0, 1, 2, 3, 4, 5, 6, 7])
    nE = [0]
    nPT = [0]
    nrt = [0]
    nxs = [0]

    def rotary(bank_ap, nh, bres, b):
        r_ = rt[nrt[0] % 2]
        nrt[0] += 1
        pv = bank_ap.rearrange("p (h d) -> p h d", d=64)
        cb_ = a3(rcos)[:, b, :].unsqueeze(1).broadcast_to([128, nh, 8])
        sb_ = a3(rsin)[:, b, :].unsqueeze(1).broadcast_to([128, nh, 8])
        ta = r_.ap[:, 0:nh * 8].rearrange("p (h i) -> p h i", i=8)
        tb_ = r_.ap[:, 64:64 + nh * 8].rearrange("p (h i) -> p h i", i=8)
        tc = r_.ap[:, 128:128 + nh * 8].rearrange("p (h i) -> p h i", i=8)
        td = r_.ap[:, 192:192 + nh * 8].rearrange("p (h i) -> p h i", i=8)
        rr = [bres, rcos.res, rsin.res]
        op("dve", "tensor_tensor", out=ta, in0=pv[:, :, 0:8], in1=cb_, op=ALU.mult, reads=rr, writes=[r_.res])
        op("dve", "tensor_tensor", out=tb_, in0=pv[:, :, 8:16], in1=sb_, op=ALU.mult, reads=rr, writes=[r_.res])
        op("dve", "tensor_tensor", out=tc, in0=pv[:, :, 8:16], in1=cb_, op=ALU.mult, reads=rr, writes=[r_.res])
        op("dve", "tensor_tensor", out=td, in0=pv[:, :, 0:8], in1=sb_, op=ALU.mult, reads=rr, writes=[r_.res])
        return ta, tb_, tc, td, r_

    if getattr(C, "stop_at", "") == "setup":
        return
    for tt in range(T // TT):
        rows = slice(tt * TT, (tt + 1) * TT)
        norm_tile(C, src[rows, :], gbc, hnT, ("mixx", tt))
        if getattr(C, "stop_at", "") == "p0":
            return
        w_in = W["ev_w_in"][e].rearrange("(kc p) f -> p kc f", p=128)
        for piece in range(3):
            if getattr(C, "stop_at", "") == "p%d" % (piece + 1) and piece > 0:
                return
            wt, wv = load_w(w_in[:, :, piece * 512:(piece + 1) * 512], NKC)
            for b in range(4):
                bk = rotA()
                for kc in range(NKC):
                    op("pe", "matmul", C.ps[bk], lhsT=hnT3[:, kc, b * 128:(b + 1) * 128], rhs=wv[:, kc, :], start=(kc == 0),
                       stop=(kc == NKC - 1), reads=[hnT.res, wt.res], writes=[C.pres[bk]])
                if piece < 2:
                    dst3 = qtok3[:, b, piece * 512:(piece + 1) * 512].rearrange("p (h d) -> p h d", d=64)
                    op("act", "activation", out=qtok3[:, b, piece * 512:(piece + 1) * 512], in_=C.ps[bk], func=AF.Copy,
                       reads=[C.pres[bk]], writes=[qtok.res])
                    ta, tb_, tc, td, r_ = rotary(C.ps[bk], 8, C.pres[bk], tt * 4 + b)
                    op("dve", "tensor_tensor", out=dst3[:, :, 0:8], in0=ta, in1=tb_, op=ALU.subtract, reads=[r_.res], writes=[qtok.res])
                    op("dve", "tensor_tensor", out=dst3[:, :, 8:16], in0=tc, in1=td, op=ALU.add, reads=[r_.res], writes=[qtok.res])
                else:
                    kd = kdup[b % 2]
                    kd4 = kd.ap.rearrange("p (k u d) -> p k u d", k=4, u=2)
                    kps = C.ps[bk][:, 0:256].rearrange("p (k d) -> p k d", d=64)
                    op("act", "activation", out=kd4[:, :, 0, :], in_=kps, func=AF.Copy, reads=[C.pres[bk]], writes=[kd.res])
                    ta, tb_, tc, td, r_ = rotary(C.ps[bk][:, 0:256], 4, C.pres[bk], tt * 4 + b)
                    op("dve", "tensor_tensor", out=kd4[:, :, 0, 0:8], in0=ta, in1=tb_, op=ALU.subtract, reads=[r_.res], writes=[kd.res])
                    op("dve", "tensor_tensor", out=kd4[:, :, 0, 8:16], in0=tc, in1=td, op=ALU.add, reads=[r_.res], writes=[kd.res])
                    op("dve", "tensor_copy", out=kd4[:, :, 1, :], in_=kd4[:, :, 0, :], reads=[kd.res], writes=[kd.res])
                    op("act", "activation", out=vext4[:, 1 + b, :, 0:64], in_=C.ps[bk][:, 256:512].rearrange("p (k d) -> p k d", d=64),
                       func=AF.Copy, reads=[C.pres[bk]], writes=[vext.res])
                    bk2 = rotA()
                    for kv in range(4):
                        op("pe", "transpose", out=C.psb[bk2][:, kv * 128:(kv + 1) * 128], in_=kd.ap[:, kv * 128:(kv + 1) * 128],
                           identity=C.ident.ap, reads=[kd.res, C.ident.res], writes=[C.pres[bk2]])
                    op("act", "activation", out=kT3[:, :, (1 + b) * 128:(2 + b) * 128],
                       in_=C.psb[bk2][:, 0:512].rearrange("p (k t) -> p k t", k=4), func=AF.Copy, reads=[C.pres[bk2]], writes=[kT.res])
        if getattr(C, "stop_at", "") == "p4":
            return
        for b in range(4):
            bk2 = rotA()
            for c in range(8):
                op("pe", "transpose", out=C.psb[bk2][:, c * 128:(c + 1) * 128], in_=qtok3[:, b, c * 128:(c + 1) * 128],
                   identity=C.ident.ap, reads=[qtok.res, C.ident.res], writes=[C.pres[bk2]])
            op("dve", "tensor_copy", out=qT3[:, :, b * 128:(b + 1) * 128], in_=C.psb[bk2].rearrange("p (c t) -> p c t", c=8),
               reads=[C.pres[bk2]], writes=[qT.res])
        if getattr(C, "stop_at", "") == "p5":
            return
        for piece in range(2):
            wt, wv = load_w(w_in[:, :, 1536 + piece * 512:1536 + (piece + 1) * 512], NKC)
            for j in range(4):
                uc = piece * 4 + j
                bk = rotA()
                for kc in range(NKC):
                    op("pe", "matmul", C.ps[bk], lhsT=wv[:, kc, j * 128:(j + 1) * 128], rhs=hnT3[:, kc, :], start=(kc == 0),
                       stop=(kc == NKC - 1), reads=[hnT.res, wt.res], writes=[C.pres[bk]])
                op("act", "activation", out=uT3[:, uc, :], in_=C.ps[bk], func=AF.Copy, reads=[C.pres[bk]], writes=[(uT.res, uc)])
        if getattr(C, "stop_at", "") == "proj":
            return
        for b in range(4):
            gb = tt * 4 + b
            at = atok[b % 2]
            at3 = at.ap.rearrange("p (h d) -> p h d", d=64)
            for kv in range(4):
                kbs = [0, 1] if gb > 0 else [1]
                pts = []
                for kb in kbs:
                    bk = rotA()
                    kcol = (b + kb) * 128
                    for hh in range(4):
                        h = 4 * kv + hh
                        c, s_ = h // 2, h % 2
                        ps_ = slice(s_ * 64, (s_ + 1) * 64)
                        op("pe", "matmul", C.ps[bk][:, hh * 128:(hh + 1) * 128], lhsT=kT3[ps_, kv, kcol:kcol + 128],
                           rhs=qT3[ps_, c, b * 128:(b + 1) * 128], start=True, stop=True, reads=[kT.res, qT.res], writes=[C.pres[bk]],
                           pe_sync=True)
                    Et = Eb[nE[0] % 2]
                    nE[0] += 1
                    op("act", "activation", out=Et.ap, in_=C.ps[bk], func=AF.Exp, scale=0.125, reads=[C.pres[bk]], writes=[Et.res])
                    Pt = PTb[nPT[0] % 4]
                    nPT[0] += 1
                    mk = mcur if kb == 1 else mprev
                    op("pool", "tensor_tensor", out=Pt.ap.rearrange("p (h q) -> p h q", h=4), in0=Et.ap.rearrange("p (h q) -> p h q", h=4),
                       in1=mk.ap.unsqueeze(1).broadcast_to([128, 4, 128]), op=ALU.mult, reads=[Et.res, mk.res], writes=[Pt.res])
                    pts.append((Pt, kb))
                if getattr(C, "stop_at", "") == "a1" and (b, kv) == (0, 0):
                    return
                bo = rotA()
                for hh in range(4):
                    for i_, (Pt, kb) in enumerate(pts):
                        op("pe", "matmul", C.ps[bo][:, hh * 65:(hh + 1) * 65], lhsT=Pt.ap[:, hh * 128:(hh + 1) * 128],
                           rhs=vext4[:, b + kb, kv, :], start=(i_ == 0), stop=(i_ == len(pts) - 1), reads=[Pt.res, vext.res],
                           writes=[C.pres[bo]])
                if getattr(C, "stop_at", "") == "a2" and (b, kv) == (0, 0):
                    return
                ov = C.ps[bo][:, 0:260].rearrange("p (h d) -> p h d", d=65)
                dn = den4[kv % 2]
                op("dve", "tensor_tensor", out=dn.ap[:, 0:4].unsqueeze(2), in0=ov[:, :, 64:65], in1=esk.ap[:, 4 * kv:4 * kv + 4].unsqueeze(2),
                   op=ALU.add, reads=[C.pres[bo], esk.res], writes=[dn.res])
                op("dve", "reciprocal", out=dn.ap[:, 4:8], in_=dn.ap[:, 0:4], reads=[dn.res], writes=[dn.res])
                op("dve", "tensor_tensor", out=at3[:, 4 * kv:4 * kv + 4, :], in0=ov[:, :, 0:64],
                   in1=dn.ap[:, 4:8].unsqueeze(2).broadcast_to([128, 4, 64]), op=ALU.mult, reads=[C.pres[bo], dn.res], writes=[at.res])
            if getattr(C, "stop_at", "") == "a3" and b == 0:
                return
            bk2 = rotA()
            for c in range(8):
                op("pe", "transpose", out=C.psb[bk2][:, c * 128:(c + 1) * 128], in_=at.ap[:, c * 128:(c + 1) * 128],
                   identity=C.ident.ap, reads=[at.res, C.ident.res], writes=[C.pres[bk2]])
            op("act", "activation", out=featT3[:, 0:8, b * 128:(b + 1) * 128], in_=C.psb[bk2].rearrange("p (c t) -> p c t", c=8),
               func=AF.Copy, reads=[C.pres[bk2]], writes=[(featT.res, "a", b)])
        op("act", "activation", out=kT3[:, :, 0:128], in_=kT3[:, :, 512:640], func=AF.Copy, reads=[kT.res], writes=[kT.res])
        op("dve", "tensor_copy", out=vext4[:, 0], in_=vext4[:, 4], reads=[vext.res], writes=[vext.res])
        if getattr(C, "stop_at", "") == "attn":
            return
        rotB = make_rot([0, 1, 2, 3])

        def s5_a(j):
            ch = j // 4
            i2 = j % 2
            tC, tS = tabC[i2], tabS[i2]
            dma("sp", tC.ap, tabs[j, 0], reads=["tabs"], writes=[tC.res])
            dma("sp", tS.ap, tabs[j, 1], reads=["tabs"], writes=[tS.res])
            bR, bI = rotB(), rotB()
            op("pe", "matmul", C.ps[bR], lhsT=lhs3[:, j, :], rhs=uT3[:, ch, :], start=True, stop=True,
               reads=[lhs.res, (uT.res, ch)], writes=[C.pres[bR]])
            op("pe", "matmul", C.ps[bI], lhsT=lhs3[:, 32 + j, :], rhs=uT3[:, ch, :], start=True, stop=True,
               reads=[lhs.res, (uT.res, ch)], writes=[C.pres[bI]])
            t = s5t[i2]
            op("dve", "tensor_tensor", out=t[0].ap, in0=C.ps[bR], in1=tC.ap, op=ALU.mult, reads=[C.pres[bR], tC.res], writes=[t[0].res])
            op("dve", "tensor_tensor", out=t[1].ap, in0=C.ps[bI], in1=tS.ap, op=ALU.mult, reads=[C.pres[bI], tS.res], writes=[t[1].res])
            op("dve", "tensor_tensor", out=t[2].ap, in0=C.ps[bI], in1=tC.ap, op=ALU.mult, reads=[C.pres[bI], tC.res], writes=[t[2].res])
            op("dve", "tensor_tensor", out=t[3].ap, in0=C.ps[bR], in1=tS.ap, op=ALU.mult, reads=[C.pres[bR], tS.res], writes=[t[3].res])
            op("pool", "tensor_tensor", out=t[0].ap, in0=t[0].ap, in1=t[1].ap, op=ALU.add, reads=[t[0].res, t[1].res], writes=[t[0].res])
            op("pool", "tensor_tensor", out=t[2].ap, in0=t[2].ap, in1=t[3].ap, op=ALU.subtract, reads=[t[2].res, t[3].res], writes=[t[2].res])
            rb = rho.ap[:, j:j + 1].broadcast_to([128, 512])
            op("dve", "tensor_tensor_scan", out=t[4].ap, data0=rb, data1=t[0].ap, initial=hst.ap[:, 2 * j:2 * j + 1], op0=ALU.mult,
               op1=ALU.add, reads=[rho.res, t[0].res, (hst.res, j)], writes=[t[4].res])
            op("dve", "tensor_tensor_scan", out=t[5].ap, data0=rb, data1=t[2].ap, initial=hst.ap[:, 2 * j + 1:2 * j + 2], op0=ALU.mult,
               op1=ALU.add, reads=[rho.res, t[2].res, (hst.res, j)], writes=[t[5].res])

        def s5_b(j):
            ch, jj = j // 4, j % 4
            yb = 4 + (ch % 2)
            i2 = j % 2
            tC, tS = tabC[i2], tabS[i2]
            t = s5t[i2]
            op("pool", "tensor_tensor", out=t[0].ap, in0=t[4].ap, in1=tC.ap, op=ALU.mult, reads=[t[4].res, tC.res], writes=[t[0].res])
            op("pool", "tensor_tensor", out=t[1].ap, in0=t[5].ap, in1=tS.ap, op=ALU.mult, reads=[t[5].res, tS.res], writes=[t[1].res])
            op("pool", "tensor_tensor", out=t[2].ap, in0=t[4].ap, in1=tS.ap, op=ALU.mult, reads=[t[4].res, tS.res], writes=[t[2].res])
            op("pool", "tensor_tensor", out=t[3].ap, in0=t[5].ap, in1=tC.ap, op=ALU.mult, reads=[t[5].res, tC.res], writes=[t[3].res])
            op("dve", "tensor_tensor", out=t[0].ap, in0=t[0].ap, in1=t[1].ap, op=ALU.subtract, reads=[t[0].res, t[1].res], writes=[t[0].res])
            op("dve", "tensor_tensor", out=t[2].ap, in0=t[2].ap, in1=t[3].ap, op=ALU.add, reads=[t[2].res, t[3].res], writes=[t[2].res])
            hr, hi = hbf[i2]
            op("act", "activation", out=hr.ap, in_=t[0].ap, func=AF.Copy, reads=[t[0].res], writes=[hr.res])
            op("act", "activation", out=hi.ap, in_=t[2].ap, func=AF.Copy, reads=[t[2].res], writes=[hi.res])
            op("act", "activation", out=hst.ap[:, 2 * j:2 * j + 1], in_=t[0].ap[:, 511:512], func=AF.Copy, reads=[t[0].res], writes=[(hst.res, j)])
            op("act", "activation", out=hst.ap[:, 2 * j + 1:2 * j + 2], in_=t[2].ap[:, 511:512], func=AF.Copy, reads=[t[2].res], writes=[(hst.res, j)])
            op("pe", "matmul", C.ps[yb], lhsT=lhs3[:, 64 + j, :], rhs=hr.ap, start=(jj == 0), stop=False,
               reads=[lhs.res, hr.res], writes=[C.pres[yb]])
            op("pe", "matmul", C.ps[yb], lhsT=lhs3[:, 96 + j, :], rhs=hi.ap, start=False, stop=False,
               reads=[lhs.res, hi.res], writes=[C.pres[yb]])
            if jj == 3:
                op("pe", "matmul", C.ps[yb], lhsT=dD.ap[:, ch * 128:(ch + 1) * 128], rhs=uT3[:, ch, :], start=False, stop=True,
                   reads=[dD.res, (uT.res, ch)], writes=[C.pres[yb]])
                op("act", "activation", out=gT3[:, ch, :], in_=C.ps[yb], func=AF.Gelu_apprx_tanh, reads=[C.pres[yb]], writes=[gT.res])

        s5_a(0)
        for j in range(32):
            if j + 1 < 32:
                s5_a(j + 1)
            s5_b(j)
        if getattr(C, "stop_at", "") == "s5":
            return
        wglu = W["s5_w_glu"][e].rearrange("(kc p) f -> p kc f", p=128)
        for piece in range(2):
            wt, wv = load_w(wglu[:, :, piece * 512:(piece + 1) * 512], 8)
            for j in range(4):
                oc = piece * 4 + j
                bk = rotA()
                for kc in range(8):
                    op("pe", "matmul", C.ps[bk], lhsT=wv[:, kc, j * 128:(j + 1) * 128], rhs=gT3[:, kc, :], start=(kc == 0), stop=(kc == 7),
                       reads=[wt.res] + [gT.res], writes=[C.pres[bk]])
                sg_ = sig[oc % 2]
                op("act", "activation", out=sg_.ap, in_=C.ps[bk], func=AF.Sigmoid, bias=bglu.ap[:, oc:oc + 1], reads=[C.pres[bk], bglu.res],
                   writes=[sg_.res])
                op("dve", "tensor_tensor", out=featT3[:, 8 + oc, :], in0=gT3[:, oc, :], in1=sg_.ap, op=ALU.mult, reads=[gT.res, sg_.res],
                   writes=[(featT.res, "s", oc)])
        wout = W["ev_w_out"][e].rearrange("(kc p) f -> p kc f", p=128)
        frd = [(featT.res, "a", b_) for b_ in range(4)] + [(featT.res, "s", o_) for o_ in range(8)]
        for ds in range(4):
            wt, wv = load_w(wout[:, :, ds * 512:(ds + 1) * 512], NKC)
            for tq in range(4):
                bk = rotA()
                for kc in range(16):
                    op("pe", "matmul", C.ps[bk], lhsT=featT3[:, kc, tq * 128:(tq + 1) * 128], rhs=wv[:, kc, :], start=(kc == 0), stop=(kc == 15),
                       reads=[wt.res] + frd, writes=[C.pres[bk]])
                xb = xs[nxs[0] % 4]
                nxs[0] += 1
                r0 = tt * TT + tq * 128
                dma("sp", xb.ap, src[r0:r0 + 128, ds * 512:(ds + 1) * 512], writes=[xb.res])
                op("dve", "tensor_tensor", out=xb.ap, in0=C.ps[bk], in1=xb.ap, op=ALU.add, reads=[C.pres[bk], xb.res], writes=[xb.res])
                dma("sp", dst[r0:r0 + 128, ds * 512:(ds + 1) * 512], xb.ap, reads=[xb.res])


def odd_stage(C, l, src, dst):
    o = l // 2
    S, A, W = C.S, C.A, C.W
    T = C.T
    cf = C.cstf
    op, dma = S.op, S.dma
    S.barrier()
    A.reset()

    def L(name, cols, dt=F32):
        return A.alloc(name, cols, dt)
    gbc = L("gcol", NKC)
    load_gcol(C, gbc, W["norm_mix"][l])
    _xt = L("xt0", D)
    C.xt = [_xt, _xt]
    _xn = L("xn0", D, BF16)
    C.xn = [_xn, _xn]
    C.ss = [L(f"ss{i}", 4) for i in range(2)]
    hnT = L("hnT", NKC * TT, BF16)
    hnT3 = hnT.ap.rearrange("p (k t) -> p k t", k=NKC)
    wb = [L(f"wb{i}", NKC * 512, BF16) for i in range(2)]
    nwb = [0]

    def load_w(src_ap3, nk, ncol=512):
        t = wb[nwb[0] % 2]
        nwb[0] += 1
        v = t.ap[:, 0:nk * ncol].rearrange("p (k f) -> p k f", k=nk)
        dma("pool", v, src_ap3, writes=[t.res])
        return t, v
    abc, dtb, dbc = L("abc", 64), L("dtb", 64), L("dbc", 64)
    load_bcast_row(C, abc, W["m_a_log"][o], 64)
    load_bcast_row(C, dtb, W["m_dt_bias"][o], 64)
    load_bcast_row(C, dbc, W["m_d"][o], 64)
    op("act", "activation", out=abc.ap, in_=abc.ap, func=AF.Exp, reads=[abc.res], writes=[abc.res])
    op("dve", "tensor_scalar", out=abc.ap, in0=abc.ap, scalar1=-1.0, scalar2=None, op0=ALU.mult, reads=[abc.res], writes=[abc.res])
    ng = L("ng", 32)
    dma("sp", ng.ap, W["m_norm"][o].rearrange("(a p) -> p a", p=128), writes=[ng.res], allow_slow_non_contiguous=True)
    cw = L("cw", 48 * 4)
    cw3 = cw.ap.rearrange("p (a k) -> p a k", k=4)
    for k_ in range(4):
        dma("sp", cw3[:, :, k_], W["m_conv_w"][o][k_].rearrange("(a p) -> p a", p=128), writes=[cw.res], allow_slow_non_contiguous=True)
    cbias = L("cbias", 48)
    dma("sp", cbias.ap, W["m_conv_b"][o].rearrange("(a p) -> p a", p=128), writes=[cbias.res], allow_slow_non_contiguous=True)
    halo = L("halo", 48 * 3)
    halo3 = halo.ap.rearrange("p (a k) -> p a k", k=3)
    op("dve", "memset", halo.ap, 0.0, writes=[halo.res])
    state = L("state", 8 * 512)
    state3 = state.ap.rearrange("p (g f) -> p g f", g=8)
    op("dve", "memset", state.ap, 0.0, writes=[state.res])
    stbf = [L(f"stbf{i}", 512, BF16) for i in range(2)]
    trif = cf.ap[:, 256:384]
    onesf = cf.ap[:, 640:768]
    identF = cf.ap[:, 0:128]
    negm4 = L("negm4", 512)
    for i in range(4):
        op("dve", "tensor_copy", out=negm4.ap[:, i * 128:(i + 1) * 128], in_=cf.ap[:, 512:640], reads=[cf.res], writes=[negm4.res])
    cv = [L(f"cv{i}", 515) for i in range(2)]
    acc = [L(f"acc{i}", 512) for i in range(2)]
    BT, CT = L("BT", 8 * 512, BF16), L("CT", 8 * 512, BF16)
    BT3 = BT.ap.rearrange("p (g t) -> p g t", g=8)
    CT3 = CT.ap.rearrange("p (g t) -> p g t", g=8)
    Btok = L("Btok", 4 * 1024, BF16)
    Btok4 = Btok.ap.rearrange("p (c g n) -> p c g n", c=4, g=8)
    xTg = L("xTg", 4 * 512, BF16)
    xTg3 = xTg.ap.rearrange("p (j t) -> p j t", j=4)
    xtok = L("xtok", 4 * 512, BF16)
    xtok3 = xtok.ap.rearrange("p (c f) -> p c f", c=4)
    zs = L("zs", 4 * 512, BF16)
    zs3 = zs.ap.rearrange("p (c f) -> p c f", c=4)
    yT = L("yT", 32 * 512, BF16)
    yT3 = yT.ap.rearrange("p (a t) -> p a t", a=32)
    dtt, adt, acum, asum, ea, wdec, cdec = (L(n_, 256) for n_ in ("dtt", "adt", "acum", "asum", "ea", "wdec", "cdec"))
    v3 = lambda t: t.ap.rearrange("p (c h) -> p c h", c=4)
    LM = []
    for i_ in range(2):
        d_ = dict(R1=[L(f"R1{i_}_{k_}", 512) for k_ in range(2)], Lb=L(f"Lb{i_}", 1024, BF16), Mb=L(f"Mb{i_}", 1024, BF16),
                  cbt=L(f"cbt{i_}", 128, BF16))
        LM.append(d_)
    xdt, xw = L("xdt", 512, BF16), L("xw", 512, BF16)
    yo, yy = L("yo", 512), L("yy", 512)
    xd = yo
    ynb = L("ynb", 512, BF16)
    sq = ynb
    rs = L("rs", 4)
    xs = [L(f"xs{i}", 512) for i in range(2)] * 2
    nxs = [0]
    rot = make_rot([0, 1, 2, 3, 4, 5, 6, 7])
    w_in = W["m_w_in"][o].rearrange("(kc p) f -> p kc f", p=128)
    ncv = [0]

    def conv_chunk(bank, fcg, out_ap, out_res):
        i = ncv[0] % 2
        ncv[0] += 1
        c_, a_ = cv[i], acc[i]
        op("dve", "tensor_copy", out=c_.ap[:, 0:3], in_=halo3[:, fcg, :], reads=[(halo.res, fcg)], writes=[c_.res])
        op("act", "activation", out=c_.ap[:, 3:515], in_=C.ps[bank], func=AF.Copy, reads=[C.pres[bank]], writes=[c_.res])
        op("dve", "tensor_copy", out=halo3[:, fcg, :], in_=c_.ap[:, 512:515], reads=[c_.res], writes=[(halo.res, fcg)])
        op("dve", "tensor_scalar", out=a_.ap, in0=c_.ap[:, 0:512], scalar1=cw3[:, fcg, 0:1], scalar2=cbias.ap[:, fcg:fcg + 1],
           op0=ALU.mult, op1=ALU.add, reads=[c_.res, cw.res, cbias.res], writes=[a_.res])
        for k_ in range(1, 4):
            op("dve", "scalar_tensor_tensor", out=a_.ap, in0=c_.ap[:, k_:k_ + 512], scalar=cw3[:, fcg, k_:k_ + 1], in1=a_.ap,
               op0=ALU.mult, op1=ALU.add, reads=[c_.res, cw.res, a_.res], writes=[a_.res])
        op("act", "activation", out=out_ap, in_=a_.ap, func=AF.Silu, reads=[a_.res], writes=[out_res])

    def build_lm(g, c, buf):
        R1, Lb, Mb, cbt = buf["R1"], buf["Lb"], buf["Mb"], buf["cbt"]
        sgt = R1
        Lb3 = Lb.ap.rearrange("p (h l) -> p h l", h=8)
        Mb3 = Mb.ap.rearrange("p (h l) -> p h l", h=8)
        cs_ = slice(c * 128, (c + 1) * 128)
        for half in range(2):
            h0 = g * 8 + half * 4
            r1 = R1[half]
            op("dve", "tensor_tensor", out=r1.ap.rearrange("p (h l) -> p h l", h=4), in0=trif.unsqueeze(1).broadcast_to([128, 4, 128]),
               in1=v3(adt)[:, c, h0:h0 + 4].unsqueeze(2).broadcast_to([128, 4, 128]), op=ALU.mult, reads=[cf.res, adt.res], writes=[r1.res])
            bk = rot()
            op("pe", "matmul", C.ps[bk], lhsT=onesf, rhs=r1.ap, start=True, stop=False, reads=[cf.res, r1.res], writes=[C.pres[bk]])
            op("pe", "matmul", C.ps[bk], lhsT=identF, rhs=negm4.ap, start=False, stop=True, reads=[cf.res, negm4.res], writes=[C.pres[bk]])
            sg_ = sgt[half]
            op("dve", "tensor_tensor", out=sg_.ap.rearrange("p (h l) -> p h l", h=4), in0=C.ps[bk].rearrange("p (h l) -> p h l", h=4),
               in1=v3(acum)[:, c, h0:h0 + 4].unsqueeze(2).broadcast_to([128, 4, 128]), op=ALU.subtract,
               reads=[C.pres[bk], acum.res], writes=[sg_.res])
            op("act", "activation", out=Lb.ap[:, half * 512:(half + 1) * 512], in_=sg_.ap, func=AF.Exp, reads=[sg_.res], writes=[Lb.res])
        bk = rot()
        op("pe", "matmul", C.ps[bk][:, 0:128], lhsT=BT3[:, g, cs_], rhs=CT3[:, g, cs_], start=True, stop=True,
           reads=[BT.res, CT.res], writes=[C.pres[bk]])
        op("act", "activation", out=cbt.ap, in_=C.ps[bk][:, 0:128], func=AF.Copy, reads=[C.pres[bk]], writes=[cbt.res])
        op("pool", "tensor_tensor", out=Mb3, in0=Lb3, in1=cbt.ap.unsqueeze(1).broadcast_to([128, 8, 128]), op=ALU.mult,
           reads=[Lb.res, cbt.res], writes=[Mb.res])

    def y_phase(g, c, buf):
        Mb = buf["Mb"]
        Mb3 = Mb.ap.rearrange("p (h l) -> p h l", h=8)
        hs = slice(g * 8, (g + 1) * 8)
        cs_ = slice(c * 128, (c + 1) * 128)
        xv = xtok3[:, c, :].rearrange("p (h d) -> p h d", d=64)
        op("dve", "tensor_tensor", out=xdt.ap.rearrange("p (h d) -> p h d", d=64), in0=xv,
           in1=v3(dtt)[:, c, hs].unsqueeze(2).broadcast_to([128, 8, 64]), op=ALU.mult, reads=[xtok.res, dtt.res], writes=[xdt.res])
        op("dve", "tensor_tensor", out=xw.ap.rearrange("p (h d) -> p h d", d=64), in0=xv,
           in1=v3(wdec)[:, c, hs].unsqueeze(2).broadcast_to([128, 8, 64]), op=ALU.mult, reads=[xtok.res, wdec.res], writes=[xw.res])
        sb_ = stbf[c % 2]
        op("act", "activation", out=sb_.ap, in_=state3[:, g, :], func=AF.Copy, reads=[(state.res, g)], writes=[sb_.res])
        bo = rot()
        op("pe", "matmul", C.ps[bo], lhsT=CT3[:, g, cs_], rhs=sb_.ap, start=True, stop=True, reads=[CT.res, sb_.res], writes=[C.pres[bo]])
        op("dve", "tensor_tensor", out=yo.ap.rearrange("p (h d) -> p h d", d=64), in0=C.ps[bo].rearrange("p (h d) -> p h d", d=64),
           in1=v3(ea)[:, c, hs].unsqueeze(2).broadcast_to([128, 8, 64]), op=ALU.mult, reads=[C.pres[bo], ea.res], writes=[yo.res])
        bd = rot()
        for h in range(8):
            op("pe", "matmul", C.ps[bd][:, h * 64:(h + 1) * 64], lhsT=Mb3[:, h, :], rhs=xdt.ap[:, h * 64:(h + 1) * 64], start=True,
               stop=True, reads=[Mb.res, xdt.res], writes=[C.pres[bd]])
        op("dve", "tensor_tensor", out=yy.ap, in0=C.ps[bd], in1=yo.ap, op=ALU.add, reads=[C.pres[bd], yo.res], writes=[yy.res])
        op("dve", "tensor_tensor", out=xd.ap.rearrange("p (h d) -> p h d", d=64), in0=xv,
           in1=dbc.ap[:, hs].unsqueeze(2).broadcast_to([128, 8, 64]), op=ALU.mult, reads=[xtok.res, dbc.res], writes=[xd.res])
        op("dve", "tensor_tensor", out=yy.ap, in0=yy.ap, in1=xd.ap, op=ALU.add, reads=[yy.res, xd.res], writes=[yy.res])
        op("dve", "tensor_tensor", out=yy.ap, in0=yy.ap, in1=zs3[:, c, :], op=ALU.mult, reads=[yy.res, zs.res], writes=[yy.res])
        op("act", "activation", out=sq.ap, in_=yy.ap, func=AF.Square, accum_out=rs.ap[:, 0:1], reads=[yy.res], writes=[sq.res, rs.res])
        op("act", "activation", out=rs.ap[:, 1:2], in_=rs.ap[:, 0:1], func=AF.Sqrt, scale=1.0 / 512, bias=C.eps.ap[:, 0:1],
           reads=[rs.res], writes=[rs.res])
        op("dve", "reciprocal", out=rs.ap[:, 2:3], in_=rs.ap[:, 1:2], reads=[rs.res], writes=[rs.res])
        op("dve", "tensor_scalar", out=ynb.ap, in0=yy.ap, scalar1=rs.ap[:, 2:3], scalar2=None, op0=ALU.mult, reads=[yy.res, rs.res],
           writes=[ynb.res])
        bk = rot()
        for j in range(4):
            op("pe", "transpose", out=C.psb[bk][:, j * 128:(j + 1) * 128], in_=ynb.ap[:, j * 128:(j + 1) * 128], identity=C.ident.ap,
               reads=[ynb.res, C.ident.res], writes=[C.pres[bk]])
        for j in range(4):
            op("act", "activation", out=yT3[:, g * 4 + j, cs_], in_=C.psb[bk][:, j * 128:(j + 1) * 128], func=AF.Copy,
               scale=ng.ap[:, g * 4 + j:g * 4 + j + 1], reads=[C.pres[bk], ng.res], writes=[(yT.res, g)])
        bs = rot()
        op("pe", "matmul", C.ps[bs], lhsT=Btok4[:, c, g, :], rhs=xw.ap, start=True, stop=True, reads=[Btok.res, xw.res], writes=[C.pres[bs]])
        st3 = state3[:, g, :].rearrange("p (h d) -> p h d", d=64)
        op("dve", "tensor_tensor", out=st3, in0=st3, in1=v3(cdec)[:, c, hs].unsqueeze(2).broadcast_to([128, 8, 64]), op=ALU.mult,
           reads=[(state.res, g), cdec.res], writes=[(state.res, g)])
        op("dve", "tensor_tensor", out=state3[:, g, :], in0=state3[:, g, :], in1=C.ps[bs], op=ALU.add,
           reads=[(state.res, g), C.pres[bs]], writes=[(state.res, g)])

    def proj_group(g):
        wt, wv = load_w(w_in[:, :, 4096 + g * 512:4096 + (g + 1) * 512], NKC)
        for j in range(4):
            bk = rot()
            for kc in range(NKC):
                op("pe", "matmul", C.ps[bk], lhsT=wv[:, kc, j * 128:(j + 1) * 128], rhs=hnT3[:, kc, :], start=(kc == 0),
                   stop=(kc == NKC - 1), reads=[hnT.res, wt.res], writes=[C.pres[bk]])
            conv_chunk(bk, g * 4 + j, xTg3[:, j, :], xTg.res)
        for c in range(4):
            bk = rot()
            for j in range(4):
                op("pe", "transpose", out=C.psb[bk][:, j * 128:(j + 1) * 128], in_=xTg3[:, j, c * 128:(c + 1) * 128],
                   identity=C.ident.ap, reads=[xTg.res, C.ident.res], writes=[C.pres[bk]])
            op("act", "activation", out=xtok3[:, c, :], in_=C.psb[bk][:, 0:512], func=AF.Copy, reads=[C.pres[bk]], writes=[xtok.res])
        wt, wv = load_w(w_in[:, :, g * 512:(g + 1) * 512], NKC)
        for c in range(4):
            bk = rot()
            for kc in range(NKC):
                op("pe", "matmul", C.ps[bk], lhsT=hnT3[:, kc, c * 128:(c + 1) * 128], rhs=wv[:, kc, :], start=(kc == 0),
                   stop=(kc == NKC - 1), reads=[hnT.res, wt.res], writes=[C.pres[bk]])
            op("act", "activation", out=zs3[:, c, :], in_=C.ps[bk], func=AF.Silu, reads=[C.pres[bk]], writes=[zs.res])

    for tt in range(T // TT):
        rows = slice(tt * TT, (tt + 1) * TT)
        norm_tile(C, src[rows, :], gbc, hnT, ("mixx", tt))
        for piece in range(4):
            wt, wv = load_w(w_in[:, :, 8192 + piece * 512:8192 + (piece + 1) * 512], NKC)
            for j in range(4):
                fcl = piece * 4 + j
                bk = rot()
                for kc in range(NKC):
                    op("pe", "matmul", C.ps[bk], lhsT=wv[:, kc, j * 128:(j + 1) * 128], rhs=hnT3[:, kc, :], start=(kc == 0),
                       stop=(kc == NKC - 1), reads=[hnT.res, wt.res], writes=[C.pres[bk]])
                if fcl < 8:
                    conv_chunk(bk, 32 + fcl, BT3[:, fcl, :], BT.res)
                else:
                    conv_chunk(bk, 32 + fcl, CT3[:, fcl - 8, :], CT.res)
        for c in range(4):
            bk = rot()
            for g in range(8):
                op("pe", "transpose", out=C.psb[bk][:, g * 128:(g + 1) * 128], in_=BT3[:, g, c * 128:(c + 1) * 128],
                   identity=C.ident.ap, reads=[BT.res, C.ident.res], writes=[C.pres[bk]])
            op("act", "activation", out=Btok4[:, c], in_=C.psb[bk].rearrange("p (g n) -> p g n", g=8), func=AF.Copy,
               reads=[C.pres[bk]], writes=[Btok.res])
        wt, wv = load_w(w_in[:, :, 10240:10304], NKC, 64)
        bk = rot()
        for c in range(4):
            for kc in range(NKC):
                op("pe", "matmul", C.ps[bk][:, c * 64:(c + 1) * 64], lhsT=hnT3[:, kc, c * 128:(c + 1) * 128], rhs=wv[:, kc, :],
                   start=(kc == 0), stop=(kc == NKC - 1), reads=[hnT.res, wt.res], writes=[C.pres[bk]])
        op("dve", "tensor_tensor", out=v3(dtt), in0=C.ps[bk][:, 0:256].rearrange("p (c h) -> p c h", c=4),
           in1=dtb.ap.unsqueeze(1).broadcast_to([128, 4, 64]), op=ALU.add, reads=[C.pres[bk], dtb.res], writes=[dtt.res])
        op("act", "activation", out=dtt.ap, in_=dtt.ap, func=AF.Exp, reads=[dtt.res], writes=[dtt.res])
        op("dve", "tensor_scalar", out=dtt.ap, in0=dtt.ap, scalar1=1.0, scalar2=None, op0=ALU.add, reads=[dtt.res], writes=[dtt.res])
        op("act", "activation", out=dtt.ap, in_=dtt.ap, func=AF.Ln, reads=[dtt.res], writes=[dtt.res])
        op("dve", "tensor_tensor", out=v3(adt), in0=v3(dtt), in1=abc.ap.unsqueeze(1).broadcast_to([128, 4, 64]), op=ALU.mult,
           reads=[dtt.res, abc.res], writes=[adt.res])
        bk = rot()
        op("pe", "matmul", C.ps[bk][:, 0:256], lhsT=trif, rhs=adt.ap, start=True, stop=True, reads=[cf.res, adt.res], writes=[C.pres[bk]])
        op("pe", "matmul", C.ps[bk][:, 256:512], lhsT=onesf, rhs=adt.ap, start=True, stop=True, reads=[cf.res, adt.res], writes=[C.pres[bk]])
        op("act", "activation", out=acum.ap, in_=C.ps[bk][:, 0:256], func=AF.Copy, reads=[C.pres[bk]], writes=[acum.res])
        op("act", "activation", out=asum.ap, in_=C.ps[bk][:, 256:512], func=AF.Copy, reads=[C.pres[bk]], writes=[asum.res])
        op("act", "activation", out=ea.ap, in_=acum.ap, func=AF.Exp, reads=[acum.res], writes=[ea.res])
        op("act", "activation", out=cdec.ap, in_=asum.ap, func=AF.Exp, reads=[asum.res], writes=[cdec.res])
        op("dve", "tensor_tensor", out=wdec.ap, in0=asum.ap, in1=acum.ap, op=ALU.subtract, reads=[asum.res, acum.res], writes=[wdec.res])
        op("act", "activation", out=wdec.ap, in_=wdec.ap, func=AF.Exp, reads=[wdec.res], writes=[wdec.res])
        op("dve", "tensor_tensor", out=wdec.ap, in0=wdec.ap, in1=dtt.ap, op=ALU.mult, reads=[wdec.res, dtt.res], writes=[wdec.res])
        items = [(g_, c_) for g_ in range(8) for c_ in range(4)]
        build_lm(0, 0, LM[0])
        for idx, (g, c) in enumerate(items):
            if c == 0:
                proj_group(g)
            if idx + 1 < len(items):
                build_lm(items[idx + 1][0], items[idx + 1][1], LM[(idx + 1) % 2])
            y_phase(g, c, LM[idx % 2])
        wout = W["m_w_out"][o].rearrange("(kc p) f -> p kc f", p=128)
        yrd = [(yT.res, g_) for g_ in range(8)]
        for ds in range(4):
            w0t, w0v = load_w(wout[:, 0:16, ds * 512:(ds + 1) * 512], NKC)
            w1t, w1v = load_w(wout[:, 16:32, ds * 512:(ds + 1) * 512], NKC)
            for tq in range(4):
                bk = rot()
                for kc in range(32):
                    wv_ = w0v if kc < 16 else w1v
                    wt_ = w0t if kc < 16 else w1t
                    op("pe", "matmul", C.ps[bk], lhsT=yT3[:, kc, tq * 128:(tq + 1) * 128], rhs=wv_[:, kc % 16, :], start=(kc == 0), stop=(kc == 31),
                       reads=[wt_.res] + yrd, writes=[C.pres[bk]])
                xb = xs[nxs[0] % 4]
                nxs[0] += 1
                r0 = tt * TT + tq * 128
                dma("sp", xb.ap, src[r0:r0 + 128, ds * 512:(ds + 1) * 512], writes=[xb.res])
                op("dve", "tensor_tensor", out=xb.ap, in0=C.ps[bk], in1=xb.ap, op=ALU.add, reads=[C.pres[bk], xb.res], writes=[xb.res])
                dma("sp", dst[r0:r0 + 128, ds * 512:(ds + 1) * 512], xb.ap, reads=[xb.res])


WNAMES = ["norm_ffn1", "ffn1_gate", "ffn1_up", "ffn1_down", "norm_mix", "norm_ffn2", "ffn2_gate", "ffn2_up",
          "ffn2_down", "ev_w_in", "ev_sinks", "s5_a_re", "s5_a_im", "s5_log_dt", "s5_b_re", "s5_b_im",
          "s5_c_re", "s5_c_im", "s5_d", "s5_w_glu", "s5_b_glu", "ev_w_out", "m_w_in", "m_conv_w", "m_conv_b",
          "m_dt_bias", "m_a_log", "m_d", "m_norm", "m_w_out", "final_norm"]


def build_program(T, shapes, stages, stop_at=""):
    nc = bass.Bass("TRN2", target_bir_lowering=False)
    C = Ctx()
    C.stop_at = stop_at
    C.nc = nc
    C.T = T
    W = {n: nc.dram_tensor(n, list(shapes[n]), F32, kind="ExternalInput").ap() for n in WNAMES}
    xin = nc.dram_tensor("x", [T, D], F32, kind="ExternalInput").ap()
    pos = nc.dram_tensor("positions", [T], I32, kind="ExternalInput").ap()
    cst = nc.dram_tensor("cst", [128, 1024], F32, kind="ExternalInput").ap()
    y = nc.dram_tensor("y", [T, D], F32, kind="ExternalOutput").ap()
    xres = nc.dram_tensor("xres", [T, D], F32, kind="Internal").ap()
    C.W, C.pos, C.cst = W, pos, cst
    C.tabs = nc.dram_tensor("tabs", [32, 2, 128, 512], F32, kind="Internal").ap()
    C.lhsd = nc.dram_tensor("lhsd", [128, 4 * 32 * 128], BF16, kind="Internal").ap()
    C.smalld = nc.dram_tensor("smalld", [128, 64], F32, kind="Internal").ap()
    S = Sched(nc)
    C.S = S
    with contextlib.ExitStack() as st:
        arena_t = st.enter_context(nc.sbuf_tensor("arena", [128, 206 * 1024], U8))
        C.A = Arena(arena_t[:], 206 * 1024)
        pst = [st.enter_context(nc.psum_tensor(f"ps{i}", [128, 512], F32)) for i in range(8)]
        C.ps = [p[:] for p in pst]
        C.psb = [p[:].bitcast(BF16) for p in pst]
        C.pres = [f"psum{i}" for i in range(8)]
        C.bank_i = 0

        def next_bank():
            b = C.bank_i % 8
            C.bank_i += 1
            return b
        C.next_bank = next_bank
        C.cnt_norm = 0
        A = C.A
        cf = A.alloc("cstf", 1024, F32)
        S.dma("sp", cf.ap, cst, writes=[cf.res])
        C.cstf = cf
        C.ident = A.alloc("ident", 128, BF16)
        S.op("dve", "tensor_copy", out=C.ident.ap, in_=cf.ap[:, 0:128], reads=[cf.res], writes=[C.ident.res])
        C.eps = A.alloc("eps", 1, F32)
        S.op("dve", "tensor_copy", out=C.eps.ap, in_=cf.ap[:, 128:129], reads=[cf.res], writes=[C.eps.res])
        A.set_floor()
        C.last_out = None
        finals = []
        cur_src = xin
        for stg in stages:
            kind = stg[0]
            if kind == "ffn":
                l, which = stg[1], stg[2]
                ffn_stage(C, cur_src, xres, W[f"norm_ffn{which}"][l], W[f"ffn{which}_gate"][l],
                          W[f"ffn{which}_up"][l], W[f"ffn{which}_down"][l], tag=("ffn", l, which))
                cur_src = xres
            elif kind == "mix":
                l = stg[1]
                if l % 2 == 0:
                    even_stage(C, l, cur_src, xres)
                else:
                    odd_stage(C, l, cur_src, xres)
                cur_src = xres
            elif kind == "final":
                finals = final_norm_stage(C, cur_src, y, W["final_norm"])
            elif kind == "copyout":
                S.barrier()
                finals = [S.dma("sp", y, cur_src)]
        S.emit(final_waits=finals)
    return nc


def make_consts():
    c = np.zeros((128, 1024), np.float32)
    c[:, 0:128] = np.eye(128, dtype=np.float32)
    c[:, 128] = EPS
    p = np.arange(128)
    for g2 in range(2):
        c[:, 130 + g2] = (p // 64 == g2)
        c[:, 136 + g2] = ((p // 16) % 2 == g2)
    for jj in range(4):
        c[:, 132 + jj] = (p // 32 == jj)
    c[:, 140:148] = np.exp(-math.log(500000.0) * np.arange(8, dtype=np.float32) * (2.0 / 16)).astype(np.float32)[None, :]
    kj = p[:, None]
    qi = p[None, :]
    c[:, 256:384] = (kj <= qi)
    c[:, 384:512] = (kj > qi)
    c[:, 512:640] = np.where(kj > qi, -30000.0, 0.0)
    c[:, 640:768] = 1.0
    return c


def kernel(**inputs):
    x = np.ascontiguousarray(inputs["x"], dtype=np.float32)
    B, Sq, _ = x.shape
    MIXERS_READY = True
    stages = []
    for l in range(DEPTH):
        stages.append(("ffn", l, 1))
        if MIXERS_READY:
            stages.append(("mix", l))
        stages.append(("ffn", l, 2))
    stages.append(("final",))
    shapes = {n: inputs[n].shape for n in WNAMES}
    nc = build_program(Sq, shapes, stages)
    cst = make_consts()
    wmap = {n: np.ascontiguousarray(inputs[n], dtype=np.float32) for n in WNAMES}
    in_maps = []
    for b in range(B):
        m = dict(wmap)
        m["x"] = x[b]
        m["positions"] = np.ascontiguousarray(inputs["positions"][b], dtype=np.int32)
        m["cst"] = cst
        in_maps.append(m)
    res = run_bass_kernel_spmd(nc, in_maps, core_ids=list(range(B)))
    return np.stack([np.asarray(r["y"]) for r in res.results], axis=0).astype(np.float32)
```

```python
import contextlib
import math
import numpy as np
import concourse.bass as bass
import concourse.mybir as mybir
from concourse.bass_utils import run_bass_kernel_spmd

F32 = mybir.dt.float32
BF16 = mybir.dt.bfloat16
I32 = mybir.dt.int32
U8 = mybir.dt.uint8
AF = mybir.ActivationFunctionType
ALU = mybir.AluOpType

D = 2048
DFF = 5632
DEPTH = 4
EPS = 1e-5
NKC = D // 128
NFC = DFF // 128
TT = 512

ENGS = ("pe", "act", "dve", "pool", "sp")
DMA_K = 8
SEM_PHASE = 30000


class Sched:
    def __init__(self, nc):
        self.nc = nc
        self.prog = {e: [] for e in ENGS}
        self.last_w = {}
        self.readers = {}
        self.ndma = {e: 0 for e in ENGS}
        self.bar_deps = set()
        self.bar_pending = {e: False for e in ENGS}
        self.lastc = {}
        self.lastd = {}

    def barrier(self):
        deps = set()
        for e in ENGS:
            if self.lastc.get(e) is not None:
                deps.add((e, self.lastc[e]))
            for i in self.lastd.get(e, []):
                deps.add((e, i))
        self.bar_deps = deps
        self.bar_pending = {e: True for e in ENGS}
        self.last_w = {}
        self.readers = {}

    def _record(self, eng, kind, method, args, kwargs, reads, writes):
        px = [r for r in reads if isinstance(r, str) and r.startswith("psum")]
        if px:
            reads = [r for r in reads if r not in px]
            writes = list(writes) + px
        idx = len(self.prog[eng])
        deps = set()
        if self.bar_pending[eng]:
            deps |= self.bar_deps
            self.bar_pending[eng] = False
        for r in reads:
            w = self.last_w.get(r)
            if w is not None:
                deps.add(w)
        for r in writes:
            w = self.last_w.get(r)
            if w is not None:
                deps.add(w)
            rd = self.readers.get(r)
            if rd:
                for e2, i2 in rd.items():
                    deps.add((e2, i2))
        pe_sync = kwargs.pop("pe_sync", False) if kind == "c" else False
        if eng == "pe" and kind == "c" and not pe_sync:
            deps = {d for d in deps if not (d[0] == "pe" and self.prog["pe"][d[1]]["kind"] == "c")}
        deps.discard((eng, idx))
        rec = dict(kind=kind, method=method, args=args, kwargs=kwargs, deps=deps,
                   need_sig=False, dma_n=None)
        if kind == "d":
            rec["dma_n"] = self.ndma[eng]
            self.ndma[eng] += 1
        self.prog[eng].append(rec)
        if kind == "c":
            self.lastc[eng] = idx
        else:
            self.lastd[eng] = (self.lastd.get(eng, []) + [idx])[-DMA_K:]
        for r in reads:
            self.readers.setdefault(r, {})[eng] = idx
        for r in writes:
            self.last_w[r] = (eng, idx)
            self.readers[r] = {}
        return (eng, idx)

    def op(self, eng, method, *args, reads=(), writes=(), **kwargs):
        return self._record(eng, "c", method, args, kwargs, reads, writes)

    def dma(self, eng, out, in_, reads=(), writes=(), **kwargs):
        return self._record(eng, "d", "dma_start", (), dict(out=out, in_=in_, **kwargs), reads, writes)

    def emit(self, final_waits=()):
        nc = self.nc
        prog = self.prog
        for e in ENGS:
            for rec in prog[e]:
                for (e2, i2) in rec["deps"]:
                    prog[e2][i2]["need_sig"] = True
        for fw in final_waits:
            prog[fw[0]][fw[1]]["need_sig"] = True
        nphase = {}
        for e in ENGS:
            c = 0
            for rec in prog[e]:
                if rec["kind"] == "c" and rec["need_sig"]:
                    rec["sig"] = c
                    c += 1
            nphase[e] = (c + SEM_PHASE - 1) // SEM_PHASE
        with contextlib.ExitStack() as st:
            csems = {e: [st.enter_context(nc.semaphore(f"c_{e}_{p}")) for p in range(nphase[e])]
                     for e in ENGS}
            dsems = {e: ([st.enter_context(nc.semaphore(f"d_{e}_{k}")) for k in range(DMA_K)]
                         if self.ndma[e] else []) for e in ENGS}

            def sigof(e2, i2):
                rec = prog[e2][i2]
                if rec["kind"] == "c":
                    s = rec["sig"]
                    return csems[e2][s // SEM_PHASE], (s % SEM_PHASE) + 1, (e2, "c", s // SEM_PHASE)
                n = rec["dma_n"]
                return dsems[e2][n % DMA_K], 16 * (n // DMA_K + 1), (e2, "d", n % DMA_K)

            block = st.enter_context(nc.Block())
            engobj = dict(pe="tensor", act="scalar", dve="vector", pool="gpsimd", sp="sync")

            def make(e):
                def body(eng):
                    seen = {}
                    for rec in prog[e]:
                        waits = [sigof(e2, i2) for (e2, i2) in rec["deps"]]
                        if rec["kind"] == "d":
                            n = rec["dma_n"]
                            if n >= DMA_K:
                                waits.append((dsems[e][n % DMA_K], 16 * (n // DMA_K), (e, "d", n % DMA_K)))
                        best = {}
                        for sem, val, key in waits:
                            if seen.get(key, 0) >= val:
                                continue
                            if key not in best or best[key][1] < val:
                                best[key] = (sem, val)
                        for key, (sem, val) in best.items():
                            eng.wait_ge(sem, val)
                            seen[key] = val
                        ins = getattr(eng, rec["method"])(*rec["args"], **rec["kwargs"])
                        if rec["kind"] == "d":
                            n = rec["dma_n"]
                            ins.then_inc(dsems[e][n % DMA_K], 16)
                        elif rec["need_sig"]:
                            s = rec["sig"]
                            ins.then_inc(csems[e][s // SEM_PHASE], 1)
                    if e == "sp":
                        for fw in final_waits:
                            sem, val, key = sigof(*fw)
                            if seen.get(key, 0) < val:
                                eng.wait_ge(sem, val)
                                seen[key] = val
                return body

            for e in ENGS:
                if prog[e] or e == "sp":
                    getattr(block, engobj[e])(make(e))


class Tile:
    def __init__(self, ap, res):
        self.ap = ap
        self.res = res

    def __getitem__(self, k):
        return self.ap[k]


class Arena:
    def __init__(self, base_ap, nbytes):
        self.base = base_ap
        self.nbytes = nbytes
        self.off = 0
        self.gen = 0
        self.floor = 0

    def set_floor(self):
        self.floor = self.off

    def reset(self):
        self.off = self.floor
        self.gen += 1

    def alloc(self, name, cols, dtype, shape=None):
        esz = {F32: 4, BF16: 2, I32: 4}[dtype]
        nb = (cols * esz + 63) // 64 * 64
        assert self.off + nb <= self.nbytes, (name, self.off, nb, self.nbytes)
        ap = self.base[:, self.off:self.off + cols * esz].bitcast(dtype)
        self.off += nb
        return Tile(ap, f"{name}#{self.gen}")


class Ctx:
    pass


def norm_front(C, src_rows, tagbase):
    S = C.S
    outs = []
    for tq in range(4):
        i = C.cnt_norm
        C.cnt_norm += 1
        xt = C.xt[i % len(C.xt)]
        xn = C.xn[i % len(C.xn)]
        ss = C.ss[i % len(C.ss)]
        S.dma("sp", xt.ap, src_rows[tq * 128:(tq + 1) * 128, :], reads=[tagbase], writes=[xt.res])
        S.op("act", "activation", out=xn.ap, in_=xt.ap, func=AF.Square, accum_out=ss.ap[:, 0:1],
             reads=[xt.res], writes=[xn.res, ss.res])
        S.op("act", "activation", out=ss.ap[:, 1:2], in_=ss.ap[:, 0:1], func=AF.Sqrt, scale=1.0 / D,
             bias=C.eps.ap[:, 0:1], reads=[ss.res], writes=[ss.res])
        S.op("dve", "reciprocal", out=ss.ap[:, 2:3], in_=ss.ap[:, 1:2], reads=[ss.res], writes=[ss.res])
        S.op("dve", "tensor_scalar", out=xn.ap, in0=xt.ap, scalar1=ss.ap[:, 2:3], scalar2=None, op0=ALU.mult,
             reads=[xt.res, ss.res], writes=[xn.res])
        outs.append(xn)
    return outs


def norm_back(C, xns, gcol, xnT):
    S = C.S
    xnT3 = xnT.ap.rearrange("p (k t) -> p k t", k=NKC)
    for tq, xn in enumerate(xns):
        for half in range(2):
            bank = C.next_bank()
            pb = C.psb[bank]
            for j in range(8):
                dc = half * 8 + j
                S.op("pe", "transpose", out=pb[:, j * 128:(j + 1) * 128], in_=xn.ap[:, dc * 128:(dc + 1) * 128],
                     identity=C.ident.ap, reads=[xn.res, C.ident.res], writes=[C.pres[bank]])
            dst = xnT3[:, half * 8:(half + 1) * 8, tq * 128:(tq + 1) * 128]
            srcv = pb.rearrange("p (j t) -> p j t", j=8)
            S.op("dve", "tensor_tensor", out=dst, in0=srcv,
                 in1=gcol.ap[:, half * 8:(half + 1) * 8].unsqueeze(2).broadcast_to([128, 8, 128]), op=ALU.mult,
                 reads=[C.pres[bank], gcol.res], writes=[xnT.res])


def norm_tile(C, src_rows, gcol, xnT, tagbase):
    S = C.S
    for tq in range(4):
        xns = norm_front_one(C, src_rows, tagbase, tq)
        norm_back_one(C, xns, gcol, xnT, tq)


def norm_front_one(C, src_rows, tagbase, tq):
    S = C.S
    i = C.cnt_norm
    C.cnt_norm += 1
    xt = C.xt[i % len(C.xt)]
    xn = C.xn[i % len(C.xn)]
    ss = C.ss[i % len(C.ss)]
    S.dma("sp", xt.ap, src_rows[tq * 128:(tq + 1) * 128, :], reads=[tagbase], writes=[xt.res])
    S.op("act", "activation", out=xn.ap, in_=xt.ap, func=AF.Square, accum_out=ss.ap[:, 0:1],
         reads=[xt.res], writes=[xn.res, ss.res])
    S.op("act", "activation", out=ss.ap[:, 1:2], in_=ss.ap[:, 0:1], func=AF.Sqrt, scale=1.0 / D,
         bias=C.eps.ap[:, 0:1], reads=[ss.res], writes=[ss.res])
    S.op("dve", "reciprocal", out=ss.ap[:, 2:3], in_=ss.ap[:, 1:2], reads=[ss.res], writes=[ss.res])
    S.op("dve", "tensor_scalar", out=xn.ap, in0=xt.ap, scalar1=ss.ap[:, 2:3], scalar2=None, op0=ALU.mult,
         reads=[xt.res, ss.res], writes=[xn.res])
    return xn


def norm_back_one(C, xn, gcol, xnT, tq):
    S = C.S
    xnT3 = xnT.ap.rearrange("p (k t) -> p k t", k=NKC)
    for half in range(2):
        bank = C.next_bank()
        pb = C.psb[bank]
        for j in range(8):
            dc = half * 8 + j
            S.op("pe", "transpose", out=pb[:, j * 128:(j + 1) * 128], in_=xn.ap[:, dc * 128:(dc + 1) * 128],
                 identity=C.ident.ap, reads=[xn.res, C.ident.res], writes=[C.pres[bank]])
        dst = xnT3[:, half * 8:(half + 1) * 8, tq * 128:(tq + 1) * 128]
        srcv = pb.rearrange("p (j t) -> p j t", j=8)
        S.op("dve", "tensor_tensor", out=dst, in0=srcv,
             in1=gcol.ap[:, half * 8:(half + 1) * 8].unsqueeze(2).broadcast_to([128, 8, 128]), op=ALU.mult,
             reads=[C.pres[bank], gcol.res], writes=[xnT.res])


def load_gcol(C, tile, gain_row):
    C.S.dma("sp", tile.ap, gain_row.rearrange("(a p) -> p a", p=128), writes=[tile.res], allow_slow_non_contiguous=True)


def load_bcast_row(C, tile, src_row_ap, n):
    C.S.dma("sp", tile.ap[:, 0:n], src_row_ap.partition_broadcast(128), writes=[tile.res])


def ffn_stage(C, src, dst, gain_row, wg, wu, wd, tag):
    S, A = C.S, C.A
    S.barrier()
    A.reset()
    T = C.T
    gbc = A.alloc("gcol", NKC, F32)
    load_gcol(C, gbc, gain_row)
    C.xt = [A.alloc(f"xt{i}", D, F32) for i in range(4)]
    C.xn = [A.alloc(f"xn{i}", D, BF16) for i in range(4)]
    C.ss = [A.alloc(f"ss{i}", 4, F32) for i in range(4)]
    C.cnt_norm = 0
    xnTs = [A.alloc(f"xnT{i}", NKC * TT, BF16) for i in range(2)]
    hT = A.alloc("hT", NFC * TT, BF16)
    hT3 = hT.ap.rearrange("p (f t) -> p f t", f=NFC)
    wgt = [A.alloc(f"wg{i}", NKC * 256, BF16) for i in range(2)]
    wut = [A.alloc(f"wu{i}", NKC * 256, BF16) for i in range(2)]
    wdt = [A.alloc(f"wd{i}", 4 * 512, BF16) for i in range(3)]
    sg = [A.alloc(f"sg{i}", TT, F32) for i in range(2)]
    xs = [A.alloc(f"xs{i}", 512, F32) for i in range(4)]
    wgv = wg.rearrange("(kc p) f -> p kc f", p=128)
    wuv = wu.rearrange("(kc p) f -> p kc f", p=128)
    wdv = wd.rearrange("(fc p) d -> p fc d", p=128)
    nw = 0
    nd = 0
    nx = 0
    for tt in range(T // TT):
        rows = slice(tt * TT, (tt + 1) * TT)
        xnT = xnTs[tt % 2]
        xnT3 = xnT.ap.rearrange("p (k t) -> p k t", k=NKC)
        if tt == 0:
            pend = norm_front(C, src[rows, :], ("xin", tt))
        norm_back(C, pend, gbc, xnT)
        for fg in range(NFC // 2):
            if fg == NFC // 4 and tt + 1 < T // TT:
                pend = norm_front(C, src[(tt + 1) * TT:(tt + 2) * TT, :], ("xin", tt + 1))
            wgb, wub = wgt[nw % 2], wut[nw % 2]
            nw += 1
            S.dma("pool", wgb.ap.rearrange("p (k f) -> p k f", k=NKC), wgv[:, :, fg * 256:(fg + 1) * 256],
                  writes=[wgb.res])
            S.dma("pool", wub.ap.rearrange("p (k f) -> p k f", k=NKC), wuv[:, :, fg * 256:(fg + 1) * 256],
                  writes=[wub.res])
            wg3 = wgb.ap.rearrange("p (k f) -> p k f", k=NKC)
            wu3 = wub.ap.rearrange("p (k f) -> p k f", k=NKC)
            for j in range(2):
                fc = fg * 2 + j
                bg = C.next_bank()
                bu = C.next_bank()
                for kc in range(NKC):
                    S.op("pe", "matmul", C.ps[bg], lhsT=wg3[:, kc, j * 128:(j + 1) * 128], rhs=xnT3[:, kc, :],
                         start=(kc == 0), stop=(kc == NKC - 1), reads=[wgb.res, xnT.res], writes=[C.pres[bg]])
                for kc in range(NKC):
                    S.op("pe", "matmul", C.ps[bu], lhsT=wu3[:, kc, j * 128:(j + 1) * 128], rhs=xnT3[:, kc, :],
                         start=(kc == 0), stop=(kc == NKC - 1), reads=[wub.res, xnT.res], writes=[C.pres[bu]])
                sgt = sg[fc % 2]
                S.op("act", "activation", out=sgt.ap, in_=C.ps[bg], func=AF.Silu, reads=[C.pres[bg]], writes=[sgt.res])
                S.op("dve", "tensor_tensor", out=hT3[:, fc, :], in0=sgt.ap, in1=C.ps[bu], op=ALU.mult,
                     reads=[sgt.res, C.pres[bu]], writes=[(hT.res, fc)])
        for q in range(4):
            banks = [C.next_bank() for _ in range(4)]
            xq = []
            for tq in range(4):
                xb = xs[nx % 4]
                nx += 1
                r0 = tt * TT + tq * 128
                S.dma("sp", xb.ap, src[r0:r0 + 128, q * 512:(q + 1) * 512], writes=[xb.res])
                xq.append(xb)
            for fcg in range(NFC // 4):
                wdb = wdt[nd % 3]
                nd += 1
                S.dma("pool", wdb.ap.rearrange("p (f d) -> p f d", f=4),
                      wdv[:, fcg * 4:(fcg + 1) * 4, q * 512:(q + 1) * 512], writes=[wdb.res])
                wd3 = wdb.ap.rearrange("p (f d) -> p f d", f=4)
                for j in range(4):
                    fc = fcg * 4 + j
                    for tq in range(4):
                        S.op("pe", "matmul", C.ps[banks[tq]], lhsT=hT3[:, fc, tq * 128:(tq + 1) * 128], rhs=wd3[:, j, :],
                             start=(fc == 0), stop=(fc == NFC - 1), reads=[(hT.res, fc), wdb.res],
                             writes=[C.pres[banks[tq]]])
            for tq in range(4):
                xb = xq[tq]
                r0 = tt * TT + tq * 128
                S.op("dve", "scalar_tensor_tensor", out=xb.ap, in0=C.ps[banks[tq]], scalar=0.5, in1=xb.ap,
                     op0=ALU.mult, op1=ALU.add, reads=[C.pres[banks[tq]], xb.res], writes=[xb.res])
                C.last_out = S.dma("sp", dst[r0:r0 + 128, q * 512:(q + 1) * 512], xb.ap, reads=[xb.res],
                                   writes=[(tag, "xo", tt, tq, q)])


def final_norm_stage(C, src, dst, gain_row):
    S, A = C.S, C.A
    S.barrier()
    A.reset()
    gbc = A.alloc("gbc", D, F32)
    load_bcast_row(C, gbc, gain_row, D)
    xt = [A.alloc(f"xt{i}", D, F32) for i in range(3)]
    junk = A.alloc("junk", D, BF16)
    ss = [A.alloc(f"ss{i}", 4, F32) for i in range(3)]
    outs = []
    for i in range(C.T // 128):
        x, s_ = xt[i % 3], ss[i % 3]
        S.dma("sp", x.ap, src[i * 128:(i + 1) * 128, :], writes=[x.res])
        S.op("act", "activation", out=junk.ap, in_=x.ap, func=AF.Square, accum_out=s_.ap[:, 0:1],
             reads=[x.res], writes=[junk.res, s_.res])
        S.op("act", "activation", out=s_.ap[:, 1:2], in_=s_.ap[:, 0:1], func=AF.Sqrt, scale=1.0 / D,
             bias=C.eps.ap[:, 0:1], reads=[s_.res], writes=[s_.res])
        S.op("dve", "reciprocal", out=s_.ap[:, 2:3], in_=s_.ap[:, 1:2], reads=[s_.res], writes=[s_.res])
        S.op("dve", "scalar_tensor_tensor", out=x.ap, in0=x.ap, scalar=s_.ap[:, 2:3], in1=gbc.ap,
             op0=ALU.mult, op1=ALU.mult, reads=[x.res, s_.res, gbc.res], writes=[x.res])
        outs.append(S.dma("sp", dst[i * 128:(i + 1) * 128, :], x.ap, reads=[x.res]))
    return outs


PI = math.pi
TWO_PI_HI = 6.28125
TWO_PI_LO = 2.0 * math.pi - 6.28125


def wrap_pi(S, t, m, shape_ap=None):
    S.op("dve", "tensor_scalar", out=m.ap, in0=t.ap, scalar1=PI, scalar2=-2.0 * PI, op0=ALU.is_gt, op1=ALU.mult,
         reads=[t.res], writes=[m.res])
    S.op("dve", "tensor_tensor", out=t.ap, in0=t.ap, in1=m.ap, op=ALU.add, reads=[t.res, m.res], writes=[t.res])
    S.op("dve", "tensor_scalar", out=m.ap, in0=t.ap, scalar1=-PI, scalar2=2.0 * PI, op0=ALU.is_lt, op1=ALU.mult,
         reads=[t.res], writes=[m.res])
    S.op("dve", "tensor_tensor", out=t.ap, in0=t.ap, in1=m.ap, op=ALU.add, reads=[t.res, m.res], writes=[t.res])


def make_rot(names_banks):
    st = {"i": 0}

    def f():
        b = names_banks[st["i"] % len(names_banks)]
        st["i"] += 1
        return b
    return f


def even_stage(C, l, src, dst):
    e = l // 2
    S, A, W = C.S, C.A, C.W
    T = C.T
    NB = T // 128
    cf = C.cstf
    op, dma = S.op, S.dma
    tabs = C.tabs
    lhsd = C.lhsd
    smalld = C.smalld
    S.barrier()
    A.reset()

    def L(name, cols, dt=F32):
        return A.alloc(name, cols, dt)
    are, aim, ldt = L("are", 32), L("aim", 32), L("ldt", 32)
    for g2 in range(2):
        ps_ = slice(g2 * 64, (g2 + 1) * 64)
        dma("sp", are.ap[ps_, :], W["s5_a_re"][e].rearrange("(j g) p -> g p j", g=2)[g2], writes=[are.res],
            allow_slow_non_contiguous=True)
        dma("sp", aim.ap[ps_, :], W["s5_a_im"][e].rearrange("(j g) p -> g p j", g=2)[g2], writes=[aim.res],
            allow_slow_non_contiguous=True)
        dma("sp", ldt.ap[ps_, :], W["s5_log_dt"][e].rearrange("(j g) -> g j", g=2)[g2].partition_broadcast(64),
            writes=[ldt.res], allow_slow_non_contiguous=True)
    dt_, mag, th, m_, thc = L("dt", 32), L("mag", 32), L("th", 32), L("m", 32), L("thc", 32)
    cs, sn = L("cs", 32), L("sn", 32)
    op("act", "activation", out=dt_.ap, in_=ldt.ap, func=AF.Exp, reads=[ldt.res], writes=[dt_.res])
    op("dve", "tensor_tensor", out=mag.ap, in0=are.ap, in1=dt_.ap, op=ALU.mult, reads=[are.res, dt_.res], writes=[mag.res])
    op("act", "activation", out=mag.ap, in_=mag.ap, func=AF.Exp, reads=[mag.res], writes=[mag.res])
    op("dve", "tensor_tensor", out=th.ap, in0=aim.ap, in1=dt_.ap, op=ALU.mult, reads=[aim.res, dt_.res], writes=[th.res])
    for _ in range(5):
        wrap_pi(S, th, m_)
    op("dve", "tensor_scalar", out=thc.ap, in0=th.ap, scalar1=PI / 2, scalar2=None, op0=ALU.add, reads=[th.res], writes=[thc.res])
    wrap_pi(S, thc, m_)
    op("act", "activation", out=sn.ap, in_=th.ap, func=AF.Sin, reads=[th.res], writes=[sn.res])
    op("act", "activation", out=cs.ap, in_=thc.ap, func=AF.Sin, reads=[thc.res], writes=[cs.res])
    abr, abi, den, cre, cim, t1_, t2_ = L("abr", 32), L("abi", 32), L("den", 32), L("cre", 32), L("cim", 32), L("t1_", 32), L("t2_", 32)
    TTm = lambda o, a, b, o_: op("dve", "tensor_tensor", out=o.ap, in0=a.ap, in1=b.ap, op=o_, reads=[a.res, b.res], writes=[o.res])
    TTm(abr, mag, cs, ALU.mult)
    TTm(abi, mag, sn, ALU.mult)
    op("dve", "tensor_scalar", out=t1_.ap, in0=abr.ap, scalar1=-1.0, scalar2=None, op0=ALU.add, reads=[abr.res], writes=[t1_.res])
    TTm(den, are, are, ALU.mult)
    TTm(t2_, aim, aim, ALU.mult)
    TTm(den, den, t2_, ALU.add)
    op("dve", "reciprocal", out=den.ap, in_=den.ap, reads=[den.res], writes=[den.res])
    TTm(cre, t1_, are, ALU.mult)
    TTm(t2_, abi, aim, ALU.mult)
    TTm(cre, cre, t2_, ALU.add)
    TTm(cre, cre, den, ALU.mult)
    TTm(cim, abi, are, ALU.mult)
    TTm(t2_, t1_, aim, ALU.mult)
    TTm(cim, cim, t2_, ALU.subtract)
    TTm(cim, cim, den, ALU.mult)
    dma("sp", smalld[:, 0:32], mag.ap, reads=[mag.res], writes=["smalld"])
    bre, bim, bbr, bbi, tb = L("bre", 512), L("bim", 512), L("bbr", 512), L("bbi", 512), L("tb", 512)
    v3 = lambda t: t.ap.rearrange("p (j c) -> p j c", j=32)
    for g2 in range(2):
        ps_ = slice(g2 * 64, (g2 + 1) * 64)
        dma("sp", v3(bre)[ps_], W["s5_b_re"][e].rearrange("(j g) p c -> g p j c", g=2)[g2], writes=[bre.res])
        dma("sp", v3(bim)[ps_], W["s5_b_im"][e].rearrange("(j g) p c -> g p j c", g=2)[g2], writes=[bim.res])
    bc3 = lambda t: t.ap.unsqueeze(2).broadcast_to([128, 32, 16])

    def TB(o, a, b_bc_tile, o_):
        op("dve", "tensor_tensor", out=v3(o), in0=v3(a), in1=bc3(b_bc_tile), op=o_, reads=[a.res, b_bc_tile.res], writes=[o.res])
    TB(bbr, bre, cre, ALU.mult)
    TB(tb, bim, cim, ALU.mult)
    TTm(bbr, bbr, tb, ALU.subtract)
    TB(bbi, bim, cre, ALU.mult)
    TB(tb, bre, cim, ALU.mult)
    TTm(bbi, bbi, tb, ALU.add)
    X = [L(f"X{i}", 128) for i in range(2)]
    lo = [L(f"lo{i}", 512, BF16) for i in range(2)]
    cnat = [L(f"cnat{i}", 64) for i in range(2)]
    identF = cf.ap[:, 0:128]
    prot = make_rot([0, 1, 2, 3, 4, 5, 6, 7])
    k = 0
    for ch in range(8):
        for ri, bb in enumerate((bbr, bbi)):
            Xt = X[k % 2]
            lot = lo[k % 2]
            k += 1
            X4 = Xt.ap.rearrange("p (j g c) -> p j g c", j=4, g=2)
            for g2 in range(2):
                op("dve", "tensor_scalar", out=X4[:, :, g2, :], in0=v3(bb)[:, 4 * ch:4 * ch + 4, :], scalar1=cf.ap[:, 130 + g2:131 + g2],
                   scalar2=None, op0=ALU.mult, reads=[bb.res, cf.res], writes=[Xt.res])
            bk = prot()
            op("pe", "transpose", out=C.ps[bk][:, 0:128], in_=Xt.ap, identity=identF, reads=[Xt.res, cf.res], writes=[C.pres[bk]])
            for jj in range(4):
                op("dve", "tensor_scalar", out=lot.ap[:, jj * 128:(jj + 1) * 128], in0=C.ps[bk][:, 0:128],
                   scalar1=cf.ap[:, 132 + jj:133 + jj], scalar2=None, op0=ALU.mult, reads=[C.pres[bk], cf.res], writes=[lot.res])
            dma("sp", lhsd[:, (ri * 32 + ch * 4) * 128:(ri * 32 + ch * 4 + 4) * 128], lot.ap, reads=[lot.res], writes=["lhsd"])
        for ri, (cw, sgn) in enumerate(((W["s5_c_re"], 1.0), (W["s5_c_im"], -1.0))):
            Xt = X[k % 2]
            lot = lo[k % 2]
            cn = cnat[k % 2]
            k += 1
            dma("sp", cn.ap, cw[e].rearrange("g c p -> (g c) p")[ch * 128:(ch + 1) * 128, :], writes=[cn.res])
            for g2 in range(2):
                op("dve", "tensor_scalar", out=Xt.ap[:, g2 * 64:(g2 + 1) * 64], in0=cn.ap, scalar1=cf.ap[:, 136 + g2:137 + g2],
                   scalar2=sgn, op0=ALU.mult, op1=ALU.mult, reads=[cn.res, cf.res], writes=[Xt.res])
            bk = prot()
            op("pe", "transpose", out=C.ps[bk][:, 0:128], in_=Xt.ap, identity=identF, reads=[Xt.res, cf.res], writes=[C.pres[bk]])
            op("dve", "memset", lot.ap, 0.0, writes=[lot.res])
            for jj in range(4):
                op("dve", "tensor_copy", out=lot.ap[:, jj * 128 + jj * 32:jj * 128 + jj * 32 + 32], in_=C.ps[bk][:, jj * 32:jj * 32 + 32],
                   reads=[C.pres[bk]], writes=[lot.res])
            dma("sp", lhsd[:, ((2 + ri) * 32 + ch * 4) * 128:((2 + ri) * 32 + ch * 4 + 4) * 128], lot.ap, reads=[lot.res], writes=["lhsd"])
    NP = 8
    Ct, St, Ut, Vt = L("Ct", NP * 512), L("St", NP * 512), L("Ut", NP * 256), L("Vt", NP * 256)
    C3 = Ct.ap.rearrange("p (j t) -> p j t", j=NP)
    S3 = St.ap.rearrange("p (j t) -> p j t", j=NP)
    U3 = Ut.ap.rearrange("p (j t) -> p j t", j=NP)
    V3 = Vt.ap.rearrange("p (j t) -> p j t", j=NP)
    for q8 in range(32 // NP):
        js = slice(q8 * NP, (q8 + 1) * NP)
        op("dve", "tensor_copy", out=C3[:, :, 0:1], in_=cs.ap[:, js].unsqueeze(2), reads=[cs.res], writes=[Ct.res])
        op("dve", "tensor_copy", out=S3[:, :, 0:1], in_=sn.ap[:, js].unsqueeze(2), reads=[sn.res], writes=[St.res])
        n = 1
        while n < 512:
            cn_b = C3[:, :, n - 1:n].broadcast_to([128, NP, n])
            sn_b = S3[:, :, n - 1:n].broadcast_to([128, NP, n])
            rw = [Ct.res, St.res]
            op("dve", "tensor_tensor", out=U3[:, :, 0:n], in0=S3[:, :, 0:n], in1=sn_b, op=ALU.mult, reads=rw, writes=[Ut.res])
            op("dve", "tensor_tensor", out=V3[:, :, 0:n], in0=C3[:, :, 0:n], in1=sn_b, op=ALU.mult, reads=rw, writes=[Vt.res])
            op("dve", "tensor_tensor", out=C3[:, :, n:2 * n], in0=C3[:, :, 0:n], in1=cn_b, op=ALU.mult, reads=rw, writes=[Ct.res])
            op("dve", "tensor_tensor", out=S3[:, :, n:2 * n], in0=S3[:, :, 0:n], in1=cn_b, op=ALU.mult, reads=rw, writes=[St.res])
            op("dve", "tensor_tensor", out=C3[:, :, n:2 * n], in0=C3[:, :, n:2 * n], in1=U3[:, :, 0:n], op=ALU.subtract,
               reads=[Ct.res, Ut.res], writes=[Ct.res])
            op("dve", "tensor_tensor", out=S3[:, :, n:2 * n], in0=S3[:, :, n:2 * n], in1=V3[:, :, 0:n], op=ALU.add,
               reads=[St.res, Vt.res], writes=[St.res])
            n *= 2
        dma("sp", tabs[js, 0].rearrange("j p t -> p j t"), C3, reads=[Ct.res], writes=["tabs"])
        dma("sp", tabs[js, 1].rearrange("j p t -> p j t"), S3, reads=[St.res], writes=["tabs"])
    if getattr(C, "stop_at", "") == "prep":
        return
    S.barrier()
    A.reset()
    gbc = L("gcol", NKC)
    load_gcol(C, gbc, W["norm_mix"][l])
    _xt = L("xt0", D)
    C.xt = [_xt, _xt]
    _xn = L("xn0", D, BF16)
    C.xn = [_xn, _xn]
    C.ss = [L(f"ss{i}", 4) for i in range(2)]
    hnT = L("hnT", NKC * TT, BF16)
    hnT3 = hnT.ap.rearrange("p (k t) -> p k t", k=NKC)
    wb = [L(f"wb{i}", NKC * 512, BF16) for i in range(2)]
    nwb = [0]

    def load_w(src_ap3, nk):
        t = wb[nwb[0] % 2]
        nwb[0] += 1
        v = t.ap[:, 0:nk * 512].rearrange("p (k f) -> p k f", k=nk)
        dma("pool", v, src_ap3, writes=[t.res])
        return t, v
    lhs = L("lhs", 4 * 32 * 128, BF16)
    dma("sp", lhs.ap, lhsd, reads=["lhsd"], writes=[lhs.res])
    lhs3 = lhs.ap.rearrange("p (m k) -> p m k", k=128)
    rho = L("rho", 32)
    dma("sp", rho.ap, smalld[:, 0:32], reads=["smalld"], writes=[rho.res])
    hst = L("hst", 64)
    op("dve", "memset", hst.ap, 0.0, writes=[hst.res])
    dcol = L("dcol", 8)
    dma("sp", dcol.ap, W["s5_d"][e].rearrange("g c -> (g c)").rearrange("(a p) -> p a", p=128), writes=[dcol.res],
        allow_slow_non_contiguous=True)
    bglu = L("bglu", 8)
    dma("sp", bglu.ap, W["s5_b_glu"][e].rearrange("(a p) -> p a", p=128), writes=[bglu.res], allow_slow_non_contiguous=True)
    dD = L("dD", 8 * 128, BF16)
    for ch in range(8):
        op("dve", "tensor_scalar", out=dD.ap[:, ch * 128:(ch + 1) * 128], in0=cf.ap[:, 0:128], scalar1=dcol.ap[:, ch:ch + 1],
           scalar2=None, op0=ALU.mult, reads=[cf.res, dcol.res], writes=[dD.res])
    if getattr(C, "stop_at", "") == "setup1":
        return
    esk = L("esk", 16)
    load_bcast_row(C, esk, W["ev_sinks"][e], 16)
    op("act", "activation", out=esk.ap, in_=esk.ap, func=AF.Exp, reads=[esk.res], writes=[esk.res])
    mcur, mprev = L("mcur", 128, BF16), L("mprev", 128, BF16)
    op("dve", "tensor_copy", out=mcur.ap, in_=cf.ap[:, 256:384], reads=[cf.res], writes=[mcur.res])
    op("dve", "tensor_copy", out=mprev.ap, in_=cf.ap[:, 384:512], reads=[cf.res], writes=[mprev.res])
    if getattr(C, "stop_at", "") == "setup2":
        return
    posi = L("posi", NB, I32)
    dma("sp", posi.ap, C.pos.rearrange("(b p) -> p b", p=128), writes=[posi.res], allow_slow_non_contiguous=True)
    posf = L("posf", NB)
    op("dve", "tensor_copy", out=posf.ap, in_=posi.ap, reads=[posi.res], writes=[posf.res])
    ang, qf, mm_ = L("ang", NB * 8), L("qf", NB * 8), L("mm_", NB * 8)
    angc = qf
    rsin, rcos = L("rsin", NB * 8), L("rcos", NB * 8)
    a3 = lambda t: t.ap.rearrange("p (b i) -> p b i", i=8)
    op("dve", "tensor_tensor", out=a3(ang), in0=posf.ap.unsqueeze(2).broadcast_to([128, NB, 8]),
       in1=cf.ap[:, 140:148].unsqueeze(1).broadcast_to([128, NB, 8]), op=ALU.mult, reads=[posf.res, cf.res], writes=[ang.res])
    op("dve", "tensor_scalar", out=qf.ap, in0=ang.ap, scalar1=1.0 / (2 * PI), scalar2=None, op0=ALU.mult, reads=[ang.res], writes=[qf.res])
    op("dve", "tensor_scalar", out=qf.ap, in0=qf.ap, scalar1=12582912.0, scalar2=None, op0=ALU.add, reads=[qf.res], writes=[qf.res])
    op("dve", "tensor_scalar", out=qf.ap, in0=qf.ap, scalar1=-12582912.0, scalar2=None, op0=ALU.add, reads=[qf.res], writes=[qf.res])
    op("dve", "scalar_tensor_tensor", out=ang.ap, in0=qf.ap, scalar=-TWO_PI_HI, in1=ang.ap, op0=ALU.mult, op1=ALU.add,
       reads=[qf.res, ang.res], writes=[ang.res])
    op("dve", "scalar_tensor_tensor", out=ang.ap, in0=qf.ap, scalar=-TWO_PI_LO, in1=ang.ap, op0=ALU.mult, op1=ALU.add,
       reads=[qf.res, ang.res], writes=[ang.res])
    wrap_pi(S, ang, mm_)
    wrap_pi(S, ang, mm_)
    op("dve", "tensor_scalar", out=angc.ap, in0=ang.ap, scalar1=PI / 2, scalar2=None, op0=ALU.add, reads=[ang.res], writes=[angc.res])
    wrap_pi(S, angc, mm_)
    for t_ in (ang, angc):
        op("dve", "tensor_scalar", out=t_.ap, in0=t_.ap, scalar1=-PI, scalar2=PI, op0=ALU.max, op1=ALU.min, reads=[t_.res], writes=[t_.res])
    op("act", "activation", out=rsin.ap, in_=ang.ap, func=AF.Sin, reads=[ang.res], writes=[rsin.res])
    op("act", "activation", out=rcos.ap, in_=angc.ap, func=AF.Sin, reads=[angc.res], writes=[rcos.res])
    qtok = L("qtok", 4 * 1024, BF16)
    qtok3 = qtok.ap.rearrange("p (b f) -> p b f", b=4)
    kdup = [L(f"kdup{i}", 512, BF16) for i in range(2)]
    vext = L("vext", 5 * 4 * 65, BF16)
    vext4 = vext.ap.rearrange("p (s k d) -> p s k d", s=5, k=4)
    op("dve", "memset", vext.ap, 1.0, writes=[vext.res])
    qT = L("qT", 8 * 512, BF16)
    qT3 = qT.ap.rearrange("p (c t) -> p c t", c=8)
    kT = L("kT", 4 * 640, BF16)
    kT3 = kT.ap.rearrange("p (k t) -> p k t", k=4)
    PTb = [L(f"PT{i}", 512, BF16) for i in range(2)] * 2
    Eb = [L(f"E{i}", 512, BF16) for i in range(2)]
    _atok = L("atok0", 1024, BF16)
    atok = [_atok, _atok]
    featT = L("featT", 16 * 512, BF16)
    featT3 = featT.ap.rearrange("p (c t) -> p c t", c=16)
    den4 = [L(f"den4{i}", 8) for i in range(2)]
    rt = [L(f"rt{i}", 2 * 64 * 2 + 16) for i in range(2)]
    uT = L("uT", 8 * 512, BF16)
    uT3 = uT.ap.rearrange("p (c t) -> p c t", c=8)
    gT = Tile(qtok.ap, qtok.res)
    gT3 = gT.ap.rearrange("p (c t) -> p c t", c=8)
    tabC = [L(f"tabC{i}", 512) for i in range(2)]
    tabS = [L(f"tabS{i}", 512) for i in range(2)]
    s5t = [[L(f"s5t{i}_{k_}", 512) for k_ in range(6)] for i in range(2)]
    hbf = [[L(f"hbf{i}_{k_}", 512, BF16) for k_ in range(2)] for i in range(2)]
    sig = [L(f"sig{i}", 512, BF16) for i in range(2)]
    xs = [L(f"xs{i}", 512) for i in range(2)] * 2
    rotA = make_rot([0, 1, 2, 3, 4, 5, 6, 7])
    nE = [0]
    nPT = [0]
    nrt = [0]
    nxs = [0]

    def rotary(bank_ap, nh, bres, b):
        r_ = rt[nrt[0] % 2]
        nrt[0] += 1
        pv = bank_ap.rearrange("p (h d) -> p h d", d=64)
        cb_ = a3(rcos)[:, b, :].unsqueeze(1).broadcast_to([128, nh, 8])
        sb_ = a3(rsin)[:, b, :].unsqueeze(1).broadcast_to([128, nh, 8])
        ta = r_.ap[:, 0:nh * 8].rearrange("p (h i) -> p h i", i=8)
        tb_ = r_.ap[:, 64:64 + nh * 8].rearrange("p (h i) -> p h i", i=8)
        tc = r_.ap[:, 128:128 + nh * 8].rearrange("p (h i) -> p h i", i=8)
        td = r_.ap[:, 192:192 + nh * 8].rearrange("p (h i) -> p h i", i=8)
        rr = [bres, rcos.res, rsin.res]
        op("dve", "tensor_tensor", out=ta, in0=pv[:, :, 0:8], in1=cb_, op=ALU.mult, reads=rr, writes=[r_.res])
        op("dve", "tensor_tensor", out=tb_, in0=pv[:, :, 8:16], in1=sb_, op=ALU.mult, reads=rr, writes=[r_.res])
        op("dve", "tensor_tensor", out=tc, in0=pv[:, :, 8:16], in1=cb_, op=ALU.mult, reads=rr, writes=[r_.res])
        op("dve", "tensor_tensor", out=td, in0=pv[:, :, 0:8], in1=sb_, op=ALU.mult, reads=rr, writes=[r_.res])
        return ta, tb_, tc, td, r_

    if getattr(C, "stop_at", "") == "setup":
        return
    for tt in range(T // TT):
        rows = slice(tt * TT, (tt + 1) * TT)
        norm_tile(C, src[rows, :], gbc, hnT, ("mixx", tt))
        if getattr(C, "stop_at", "") == "p0":
            return
        w_in = W["ev_w_in"][e].rearrange("(kc p) f -> p kc f", p=128)
        for piece in range(3):
            if getattr(C, "stop_at", "") == "p%d" % (piece + 1) and piece > 0:
                return
            wt, wv = load_w(w_in[:, :, piece * 512:(piece + 1) * 512], NKC)
            for b in range(4):
                bk = rotA()
                for kc in range(NKC):
                    op("pe", "matmul", C.ps[bk], lhsT=hnT3[:, kc, b * 128:(b + 1) * 128], rhs=wv[:, kc, :], start=(kc == 0),
                       stop=(kc == NKC - 1), reads=[hnT.res, wt.res], writes=[C.pres[bk]])
                if piece < 2:
                    dst3 = qtok3[:, b, piece * 512:(piece + 1) * 512].rearrange("p (h d) -> p h d", d=64)
                    op("act", "activation", out=qtok3[:, b, piece * 512:(piece + 1) * 512], in_=C.ps[bk], func=AF.Copy,
                       reads=[C.pres[bk]], writes=[qtok.res])
                    ta, tb_, tc, td, r_ = rotary(C.ps[bk], 8, C.pres[bk], tt * 4 + b)
                    op("dve", "tensor_tensor", out=dst3[:, :, 0:8], in0=ta, in1=tb_, op=ALU.subtract, reads=[r_.res], writes=[qtok.res])
                    op("dve", "tensor_tensor", out=dst3[:, :, 8:16], in0=tc, in1=td, op=ALU.add, reads=[r_.res], writes=[qtok.res])
                else:
                    kd = kdup[b % 2]
                    kd4 = kd.ap.rearrange("p (k u d) -> p k u d", k=4, u=2)
                    kps = C.ps[bk][:, 0:256].rearrange("p (k d) -> p k d", d=64)
                    op("act", "activation", out=kd4[:, :, 0, :], in_=kps, func=AF.Copy, reads=[C.pres[bk]], writes=[kd.res])
                    ta, tb_, tc, td, r_ = rotary(C.ps[bk][:, 0:256], 4, C.pres[bk], tt * 4 + b)
                    op("dve", "tensor_tensor", out=kd4[:, :, 0, 0:8], in0=ta, in1=tb_, op=ALU.subtract, reads=[r_.res], writes=[kd.res])
                    op("dve", "tensor_tensor", out=kd4[:, :, 0, 8:16], in0=tc, in1=td, op=ALU.add, reads=[r_.res], writes=[kd.res])
                    op("dve", "tensor_copy", out=kd4[:, :, 1, :], in_=kd4[:, :, 0, :], reads=[kd.res], writes=[kd.res])
                    op("act", "activation", out=vext4[:, 1 + b, :, 0:64], in_=C.ps[bk][:, 256:512].rearrange("p (k d) -> p k d", d=64),
                       func=AF.Copy, reads=[C.pres[bk]], writes=[vext.res])
                    bk2 = rotA()
                    for kv in range(4):
                        op("pe", "transpose", out=C.psb[bk2][:, kv * 128:(kv + 1) * 128], in_=kd.ap[:, kv * 128:(kv + 1) * 128],
                           identity=C.ident.ap, reads=[kd.res, C.ident.res], writes=[C.pres[bk2]])
                    op("act", "activation", out=kT3[:, :, (1 + b) * 128:(2 + b) * 128],
                       in_=C.psb[bk2][:, 0:512].rearrange("p (k t) -> p k t", k=4), func=AF.Copy, reads=[C.pres[bk2]], writes=[kT.res])
        if getattr(C, "stop_at", "") == "p4":
            return
        for b in range(4):
            bk2 = rotA()
            for c in range(8):
                op("pe", "transpose", out=C.psb[bk2][:, c * 128:(c + 1) * 128], in_=qtok3[:, b, c * 128:(c + 1) * 128],
                   identity=C.ident.ap, reads=[qtok.res, C.ident.res], writes=[C.pres[bk2]])
            op("dve", "tensor_copy", out=qT3[:, :, b * 128:(b + 1) * 128], in_=C.psb[bk2].rearrange("p (c t) -> p c t", c=8),
               reads=[C.pres[bk2]], writes=[qT.res])
        if getattr(C, "stop_at", "") == "p5":
            return
        for piece in range(2):
            wt, wv = load_w(w_in[:, :, 1536 + piece * 512:1536 + (piece + 1) * 512], NKC)
            for j in range(4):
                uc = piece * 4 + j
                bk = rotA()
                for kc in range(NKC):
                    op("pe", "matmul", C.ps[bk], lhsT=wv[:, kc, j * 128:(j + 1) * 128], rhs=hnT3[:, kc, :], start=(kc == 0),
                       stop=(kc == NKC - 1), reads=[hnT.res, wt.res], writes=[C.pres[bk]])
                op("act", "activation", out=uT3[:, uc, :], in_=C.ps[bk], func=AF.Copy, reads=[C.pres[bk]], writes=[(uT.res, uc)])
        if getattr(C, "stop_at", "") == "proj":
            return
        for b in range(4):
            gb = tt * 4 + b
            at = atok[b % 2]
            at3 = at.ap.rearrange("p (h d) -> p h d", d=64)
            for kv in range(4):
                kbs = [0, 1] if gb > 0 else [1]
                pts = []
                for kb in kbs:
                    bk = rotA()
                    kcol = (b + kb) * 128
                    for hh in range(4):
                        h = 4 * kv + hh
                        c, s_ = h // 2, h % 2
                        ps_ = slice(s_ * 64, (s_ + 1) * 64)
                        op("pe", "matmul", C.ps[bk][:, hh * 128:(hh + 1) * 128], lhsT=kT3[ps_, kv, kcol:kcol + 128],
                           rhs=qT3[ps_, c, b * 128:(b + 1) * 128], start=True, stop=True, reads=[kT.res, qT.res], writes=[C.pres[bk]],
                           pe_sync=True)
                    Et = Eb[nE[0] % 2]
                    nE[0] += 1
                    op("act", "activation", out=Et.ap, in_=C.ps[bk], func=AF.Exp, scale=0.125, reads=[C.pres[bk]], writes=[Et.res])
                    Pt = PTb[nPT[0] % 4]
                    nPT[0] += 1
                    mk = mcur if kb == 1 else mprev
                    op("pool", "tensor_tensor", out=Pt.ap.rearrange("p (h q) -> p h q", h=4), in0=Et.ap.rearrange("p (h q) -> p h q", h=4),
                       in1=mk.ap.unsqueeze(1).broadcast_to([128, 4, 128]), op=ALU.mult, reads=[Et.res, mk.res], writes=[Pt.res])
                    pts.append((Pt, kb))
                if getattr(C, "stop_at", "") == "a1" and (b, kv) == (0, 0):
                    return
                bo = rotA()
                for hh in range(4):
                    for i_, (Pt, kb) in enumerate(pts):
                        op("pe", "matmul", C.ps[bo][:, hh * 65:(hh + 1) * 65], lhsT=Pt.ap[:, hh * 128:(hh + 1) * 128],
                           rhs=vext4[:, b + kb, kv, :], start=(i_ == 0), stop=(i_ == len(pts) - 1), reads=[Pt.res, vext.res],
                           writes=[C.pres[bo]])
                if getattr(C, "stop_at", "") == "a2" and (b, kv) == (0, 0):
                    return
                ov = C.ps[bo][:, 0:260].rearrange("p (h d) -> p h d", d=65)
                dn = den4[kv % 2]
                op("dve", "tensor_tensor", out=dn.ap[:, 0:4].unsqueeze(2), in0=ov[:, :, 64:65], in1=esk.ap[:, 4 * kv:4 * kv + 4].unsqueeze(2),
                   op=ALU.add, reads=[C.pres[bo], esk.res], writes=[dn.res])
                op("dve", "reciprocal", out=dn.ap[:, 4:8], in_=dn.ap[:, 0:4], reads=[dn.res], writes=[dn.res])
                op("dve", "tensor_tensor", out=at3[:, 4 * kv:4 * kv + 4, :], in0=ov[:, :, 0:64],
                   in1=dn.ap[:, 4:8].unsqueeze(2).broadcast_to([128, 4, 64]), op=ALU.mult, reads=[C.pres[bo], dn.res], writes=[at.res])
            if getattr(C, "stop_at", "") == "a3" and b == 0:
                return
            bk2 = rotA()
            for c in range(8):
                op("pe", "transpose", out=C.psb[bk2][:, c * 128:(c + 1) * 128], in_=at.ap[:, c * 128:(c + 1) * 128],
                   identity=C.ident.ap, reads=[at.res, C.ident.res], writes=[C.pres[bk2]])
            op("act", "activation", out=featT3[:, 0:8, b * 128:(b + 1) * 128], in_=C.psb[bk2].rearrange("p (c t) -> p c t", c=8),
               func=AF.Copy, reads=[C.pres[bk2]], writes=[(featT.res, "a", b)])
        op("act", "activation", out=kT3[:, :, 0:128], in_=kT3[:, :, 512:640], func=AF.Copy, reads=[kT.res], writes=[kT.res])
        op("dve", "tensor_copy", out=vext4[:, 0], in_=vext4[:, 4], reads=[vext.res], writes=[vext.res])
        if getattr(C, "stop_at", "") == "attn":
            return
        rotB = make_rot([0, 1, 2, 3])

        def s5_a(j):
            ch = j // 4
            i2 = j % 2
            tC, tS = tabC[i2], tabS[i2]
            dma("sp", tC.ap, tabs[j, 0], reads=["tabs"], writes=[tC.res])
            dma("sp", tS.ap, tabs[j, 1], reads=["tabs"], writes=[tS.res])
            bR, bI = rotB(), rotB()
            op("pe", "matmul", C.ps[bR], lhsT=lhs3[:, j, :], rhs=uT3[:, ch, :], start=True, stop=True,
               reads=[lhs.res, (uT.res, ch)], writes=[C.pres[bR]])
            op("pe", "matmul", C.ps[bI], lhsT=lhs3[:, 32 + j, :], rhs=uT3[:, ch, :], start=True, stop=True,
               reads=[lhs.res, (uT.res, ch)], writes=[C.pres[bI]])
            t = s5t[i2]
            op("dve", "tensor_tensor", out=t[0].ap, in0=C.ps[bR], in1=tC.ap, op=ALU.mult, reads=[C.pres[bR], tC.res], writes=[t[0].res])
            op("dve", "tensor_tensor", out=t[1].ap, in0=C.ps[bI], in1=tS.ap, op=ALU.mult, reads=[C.pres[bI], tS.res], writes=[t[1].res])
            op("dve", "tensor_tensor", out=t[2].ap, in0=C.ps[bI], in1=tC.ap, op=ALU.mult, reads=[C.pres[bI], tC.res], writes=[t[2].res])
            op("dve", "tensor_tensor", out=t[3].ap, in0=C.ps[bR], in1=tS.ap, op=ALU.mult, reads=[C.pres[bR], tS.res], writes=[t[3].res])
            op("pool", "tensor_tensor", out=t[0].ap, in0=t[0].ap, in1=t[1].ap, op=ALU.add, reads=[t[0].res, t[1].res], writes=[t[0].res])
            op("pool", "tensor_tensor", out=t[2].ap, in0=t[2].ap, in1=t[3].ap, op=ALU.subtract, reads=[t[2].res, t[3].res], writes=[t[2].res])
            rb = rho.ap[:, j:j + 1].broadcast_to([128, 512])
            op("dve", "tensor_tensor_scan", out=t[4].ap, data0=rb, data1=t[0].ap, initial=hst.ap[:, 2 * j:2 * j + 1], op0=ALU.mult,
               op1=ALU.add, reads=[rho.res, t[0].res, (hst.res, j)], writes=[t[4].res])
            op("dve", "tensor_tensor_scan", out=t[5].ap, data0=rb, data1=t[2].ap, initial=hst.ap[:, 2 * j + 1:2 * j + 2], op0=ALU.mult,
               op1=ALU.add, reads=[rho.res, t[2].res, (hst.res, j)], writes=[t[5].res])

        def s5_b(j):
            ch, jj = j // 4, j % 4
            yb = 4 + (ch % 2)
            i2 = j % 2
            tC, tS = tabC[i2], tabS[i2]
            t = s5t[i2]
            op("pool", "tensor_tensor", out=t[0].ap, in0=t[4].ap, in1=tC.ap, op=ALU.mult, reads=[t[4].res, tC.res], writes=[t[0].res])
            op("pool", "tensor_tensor", out=t[1].ap, in0=t[5].ap, in1=tS.ap, op=ALU.mult, reads=[t[5].res, tS.res], writes=[t[1].res])
            op("pool", "tensor_tensor", out=t[2].ap, in0=t[4].ap, in1=tS.ap, op=ALU.mult, reads=[t[4].res, tS.res], writes=[t[2].res])
            op("pool", "tensor_tensor", out=t[3].ap, in0=t[5].ap, in1=tC.ap, op=ALU.mult, reads=[t[5].res, tC.res], writes=[t[3].res])
            op("dve", "tensor_tensor", out=t[0].ap, in0=t[0].ap, in1=t[1].ap, op=ALU.subtract, reads=[t[0].res, t[1].res], writes=[t[0].res])
            op("dve", "tensor_tensor", out=t[2].ap, in0=t[2].ap, in1=t[3].ap, op=ALU.add, reads=[t[2].res, t[3].res], writes=[t[2].res])
            hr, hi = hbf[i2]
            op("act", "activation", out=hr.ap, in_=t[0].ap, func=AF.Copy, reads=[t[0].res], writes=[hr.res])
            op("act", "activation", out=hi.ap, in_=t[2].ap, func=AF.Copy, reads=[t[2].res], writes=[hi.res])
            op("act", "activation", out=hst.ap[:, 2 * j:2 * j + 1], in_=t[0].ap[:, 511:512], func=AF.Copy, reads=[t[0].res], writes=[(hst.res, j)])
            op("act", "activation", out=hst.ap[:, 2 * j + 1:2 * j + 2], in_=t[2].ap[:, 511:512], func=AF.Copy, reads=[t[2].res], writes=[(hst.res, j)])
            op("pe", "matmul", C.ps[yb], lhsT=lhs3[:, 64 + j, :], rhs=hr.ap, start=(jj == 0), stop=False,
               reads=[lhs.res, hr.res], writes=[C.pres[yb]])
            op("pe", "matmul", C.ps[yb], lhsT=lhs3[:, 96 + j, :], rhs=hi.ap, start=False, stop=False,
               reads=[lhs.res, hi.res], writes=[C.pres[yb]])
            if jj == 3:
                op("pe", "matmul", C.ps[yb], lhsT=dD.ap[:, ch * 128:(ch + 1) * 128], rhs=uT3[:, ch, :], start=False, stop=True,
                   reads=[dD.res, (uT.res, ch)], writes=[C.pres[yb]])
                op("act", "activation", out=gT3[:, ch, :], in_=C.ps[yb], func=AF.Gelu_apprx_tanh, reads=[C.pres[yb]], writes=[gT.res])

        s5_a(0)
        for j in range(32):
            if j + 1 < 32:
                s5_a(j + 1)
            s5_b(j)
        if getattr(C, "stop_at", "") == "s5":
            return
        wglu = W["s5_w_glu"][e].rearrange("(kc p) f -> p kc f", p=128)
        for piece in range(2):
            wt, wv = load_w(wglu[:, :, piece * 512:(piece + 1) * 512], 8)
            for j in range(4):
                oc = piece * 4 + j
                bk = rotA()
                for kc in range(8):
                    op("pe", "matmul", C.ps[bk], lhsT=wv[:, kc, j * 128:(j + 1) * 128], rhs=gT3[:, kc, :], start=(kc == 0), stop=(kc == 7),
                       reads=[wt.res] + [gT.res], writes=[C.pres[bk]])
                sg_ = sig[oc % 2]
                op("act", "activation", out=sg_.ap, in_=C.ps[bk], func=AF.Sigmoid, bias=bglu.ap[:, oc:oc + 1], reads=[C.pres[bk], bglu.res],
                   writes=[sg_.res])
                op("dve", "tensor_tensor", out=featT3[:, 8 + oc, :], in0=gT3[:, oc, :], in1=sg_.ap, op=ALU.mult, reads=[gT.res, sg_.res],
                   writes=[(featT.res, "s", oc)])
        wout = W["ev_w_out"][e].rearrange("(kc p) f -> p kc f", p=128)
        frd = [(featT.res, "a", b_) for b_ in range(4)] + [(featT.res, "s", o_) for o_ in range(8)]
        for ds in range(4):
            wt, wv = load_w(wout[:, :, ds * 512:(ds + 1) * 512], NKC)
            for tq in range(4):
                bk = rotA()
                for kc in range(16):
                    op("pe", "matmul", C.ps[bk], lhsT=featT3[:, kc, tq * 128:(tq + 1) * 128], rhs=wv[:, kc, :], start=(kc == 0), stop=(kc == 15),
                       reads=[wt.res] + frd, writes=[C.pres[bk]])
                xb = xs[nxs[0] % 4]
                nxs[0] += 1
                r0 = tt * TT + tq * 128
                dma("sp", xb.ap, src[r0:r0 + 128, ds * 512:(ds + 1) * 512], writes=[xb.res])
                op("dve", "tensor_tensor", out=xb.ap, in0=C.ps[bk], in1=xb.ap, op=ALU.add, reads=[C.pres[bk], xb.res], writes=[xb.res])
                dma("sp", dst[r0:r0 + 128, ds * 512:(ds + 1) * 512], xb.ap, reads=[xb.res])


def odd_stage(C, l, src, dst):
    o = l // 2
    S, A, W = C.S, C.A, C.W
    T = C.T
    cf = C.cstf
    op, dma = S.op, S.dma
    S.barrier()
    A.reset()

    def L(name, cols, dt=F32):
        return A.alloc(name, cols, dt)
    gbc = L("gcol", NKC)
    load_gcol(C, gbc, W["norm_mix"][l])
    _xt = L("xt0", D)
    C.xt = [_xt, _xt]
    _xn = L("xn0", D, BF16)
    C.xn = [_xn, _xn]
    C.ss = [L(f"ss{i}", 4) for i in range(2)]
    hnT = L("hnT", NKC * TT, BF16)
    hnT3 = hnT.ap.rearrange("p (k t) -> p k t", k=NKC)
    wb = [L(f"wb{i}", NKC * 512, BF16) for i in range(2)]
    nwb = [0]

    def load_w(src_ap3, nk, ncol=512):
        t = wb[nwb[0] % 2]
        nwb[0] += 1
        v = t.ap[:, 0:nk * ncol].rearrange("p (k f) -> p k f", k=nk)
        dma("pool", v, src_ap3, writes=[t.res])
        return t, v
    abc, dtb, dbc = L("abc", 64), L("dtb", 64), L("dbc", 64)
    load_bcast_row(C, abc, W["m_a_log"][o], 64)
    load_bcast_row(C, dtb, W["m_dt_bias"][o], 64)
    load_bcast_row(C, dbc, W["m_d"][o], 64)
    op("act", "activation", out=abc.ap, in_=abc.ap, func=AF.Exp, reads=[abc.res], writes=[abc.res])
    op("dve", "tensor_scalar", out=abc.ap, in0=abc.ap, scalar1=-1.0, scalar2=None, op0=ALU.mult, reads=[abc.res], writes=[abc.res])
    ng = L("ng", 32)
    dma("sp", ng.ap, W["m_norm"][o].rearrange("(a p) -> p a", p=128), writes=[ng.res], allow_slow_non_contiguous=True)
    cw = L("cw", 48 * 4)
    cw3 = cw.ap.rearrange("p (a k) -> p a k", k=4)
    for k_ in range(4):
        dma("sp", cw3[:, :, k_], W["m_conv_w"][o][k_].rearrange("(a p) -> p a", p=128), writes=[cw.res], allow_slow_non_contiguous=True)
    cbias = L("cbias", 48)
    dma("sp", cbias.ap, W["m_conv_b"][o].rearrange("(a p) -> p a", p=128), writes=[cbias.res], allow_slow_non_contiguous=True)
    halo = L("halo", 48 * 3)
    halo3 = halo.ap.rearrange("p (a k) -> p a k", k=3)
    op("dve", "memset", halo.ap, 0.0, writes=[halo.res])
    state = L("state", 8 * 512)
    state3 = state.ap.rearrange("p (g f) -> p g f", g=8)
    op("dve", "memset", state.ap, 0.0, writes=[state.res])
    stbf = [L(f"stbf{i}", 512, BF16) for i in range(2)]
    trif = cf.ap[:, 256:384]
    onesf = cf.ap[:, 640:768]
    identF = cf.ap[:, 0:128]
    negm4 = L("negm4", 512)
    for i in range(4):
        op("dve", "tensor_copy", out=negm4.ap[:, i * 128:(i + 1) * 128], in_=cf.ap[:, 512:640], reads=[cf.res], writes=[negm4.res])
    cv = [L(f"cv{i}", 515) for i in range(2)]
    acc = [L(f"acc{i}", 512) for i in range(2)]
    BT, CT = L("BT", 8 * 512, BF16), L("CT", 8 * 512, BF16)
    BT3 = BT.ap.rearrange("p (g t) -> p g t", g=8)
    CT3 = CT.ap.rearrange("p (g t) -> p g t", g=8)
    Btok = L("Btok", 4 * 1024, BF16)
    Btok4 = Btok.ap.rearrange("p (c g n) -> p c g n", c=4, g=8)
    GB = []
    for i_ in range(2):
        xTg_ = L(f"xTg{i_}", 4 * 512, BF16)
        xtok_ = L(f"xtok{i_}", 4 * 512, BF16)
        zs_ = L(f"zs{i_}", 4 * 512, BF16)
        GB.append(dict(xTg=xTg_, xTg3=xTg_.ap.rearrange("p (j t) -> p j t", j=4), xtok=xtok_,
                       xtok3=xtok_.ap.rearrange("p (c f) -> p c f", c=4), zs=zs_, zs3=zs_.ap.rearrange("p (c f) -> p c f", c=4)))
    yT = L("yT", 32 * 512, BF16)
    yT3 = yT.ap.rearrange("p (a t) -> p a t", a=32)
    dtt, adt, acum, asum, ea, wdec, cdec = (L(n_, 256) for n_ in ("dtt", "adt", "acum", "asum", "ea", "wdec", "cdec"))
    v3 = lambda t: t.ap.rearrange("p (c h) -> p c h", c=4)
    _R1 = [L(f"R1_{k_}", 512) for k_ in range(2)]
    _Lb = L("Lb", 1024, BF16)
    _cbt = L("cbt", 128, BF16)
    LM = [dict(R1=_R1, Lb=_Lb, cbt=_cbt, Mb=L(f"Mb{i_}", 1024, BF16)) for i_ in range(2)]
    xdt, xw = L("xdt", 512, BF16), L("xw", 512, BF16)
    yo, yy = L("yo", 512), L("yy", 512)
    xd = yo
    ynb = L("ynb", 512, BF16)
    sq = ynb
    rs = L("rs", 4)
    xs = [acc[0], acc[1]] * 2
    nxs = [0]
    rot = make_rot([0, 1, 2, 3, 4, 5, 6, 7])
    w_in = W["m_w_in"][o].rearrange("(kc p) f -> p kc f", p=128)
    ncv = [0]

    def conv_chunk(bank, fcg, out_ap, out_res):
        i = ncv[0] % 2
        ncv[0] += 1
        c_, a_ = cv[i], acc[i]
        op("dve", "tensor_copy", out=c_.ap[:, 0:3], in_=halo3[:, fcg, :], reads=[(halo.res, fcg)], writes=[c_.res])
        op("act", "activation", out=c_.ap[:, 3:515], in_=C.ps[bank], func=AF.Copy, reads=[C.pres[bank]], writes=[c_.res])
        op("dve", "tensor_copy", out=halo3[:, fcg, :], in_=c_.ap[:, 512:515], reads=[c_.res], writes=[(halo.res, fcg)])
        op("dve", "tensor_scalar", out=a_.ap, in0=c_.ap[:, 0:512], scalar1=cw3[:, fcg, 0:1], scalar2=cbias.ap[:, fcg:fcg + 1],
           op0=ALU.mult, op1=ALU.add, reads=[c_.res, cw.res, cbias.res], writes=[a_.res])
        for k_ in range(1, 4):
            op("dve", "scalar_tensor_tensor", out=a_.ap, in0=c_.ap[:, k_:k_ + 512], scalar=cw3[:, fcg, k_:k_ + 1], in1=a_.ap,
               op0=ALU.mult, op1=ALU.add, reads=[c_.res, cw.res, a_.res], writes=[a_.res])
        op("act", "activation", out=out_ap, in_=a_.ap, func=AF.Silu, reads=[a_.res], writes=[out_res])

    def build_lm(g, c, buf):
        R1, Lb, Mb, cbt = buf["R1"], buf["Lb"], buf["Mb"], buf["cbt"]
        sgt = R1
        Lb3 = Lb.ap.rearrange("p (h l) -> p h l", h=8)
        Mb3 = Mb.ap.rearrange("p (h l) -> p h l", h=8)
        cs_ = slice(c * 128, (c + 1) * 128)
        for half in range(2):
            h0 = g * 8 + half * 4
            r1 = R1[half]
            op("dve", "tensor_tensor", out=r1.ap.rearrange("p (h l) -> p h l", h=4), in0=trif.unsqueeze(1).broadcast_to([128, 4, 128]),
               in1=v3(adt)[:, c, h0:h0 + 4].unsqueeze(2).broadcast_to([128, 4, 128]), op=ALU.mult, reads=[cf.res, adt.res], writes=[r1.res])
            bk = rot()
            op("pe", "matmul", C.ps[bk], lhsT=onesf, rhs=r1.ap, start=True, stop=False, reads=[cf.res, r1.res], writes=[C.pres[bk]])
            op("pe", "matmul", C.ps[bk], lhsT=identF, rhs=negm4.ap, start=False, stop=True, reads=[cf.res, negm4.res], writes=[C.pres[bk]])
            sg_ = sgt[half]
            op("dve", "tensor_tensor", out=sg_.ap.rearrange("p (h l) -> p h l", h=4), in0=C.ps[bk].rearrange("p (h l) -> p h l", h=4),
               in1=v3(acum)[:, c, h0:h0 + 4].unsqueeze(2).broadcast_to([128, 4, 128]), op=ALU.subtract,
               reads=[C.pres[bk], acum.res], writes=[sg_.res])
            op("act", "activation", out=Lb.ap[:, half * 512:(half + 1) * 512], in_=sg_.ap, func=AF.Exp, reads=[sg_.res], writes=[Lb.res])
        bk = rot()
        op("pe", "matmul", C.ps[bk][:, 0:128], lhsT=BT3[:, g, cs_], rhs=CT3[:, g, cs_], start=True, stop=True,
           reads=[BT.res, CT.res], writes=[C.pres[bk]])
        op("act", "activation", out=cbt.ap, in_=C.ps[bk][:, 0:128], func=AF.Copy, reads=[C.pres[bk]], writes=[cbt.res])
        op("pool", "tensor_tensor", out=Mb3, in0=Lb3, in1=cbt.ap.unsqueeze(1).broadcast_to([128, 8, 128]), op=ALU.mult,
           reads=[Lb.res, cbt.res], writes=[Mb.res])

    def y_phase(g, c, buf):
        Mb = buf["Mb"]
        gb_ = GB[g % 2]
        xtok, xtok3, zs, zs3 = gb_["xtok"], gb_["xtok3"], gb_["zs"], gb_["zs3"]
        Mb3 = Mb.ap.rearrange("p (h l) -> p h l", h=8)
        hs = slice(g * 8, (g + 1) * 8)
        cs_ = slice(c * 128, (c + 1) * 128)
        xv = xtok3[:, c, :].rearrange("p (h d) -> p h d", d=64)
        op("dve", "tensor_tensor", out=xdt.ap.rearrange("p (h d) -> p h d", d=64), in0=xv,
           in1=v3(dtt)[:, c, hs].unsqueeze(2).broadcast_to([128, 8, 64]), op=ALU.mult, reads=[xtok.res, dtt.res], writes=[xdt.res])
        op("dve", "tensor_tensor", out=xw.ap.rearrange("p (h d) -> p h d", d=64), in0=xv,
           in1=v3(wdec)[:, c, hs].unsqueeze(2).broadcast_to([128, 8, 64]), op=ALU.mult, reads=[xtok.res, wdec.res], writes=[xw.res])
        sb_ = stbf[c % 2]
        op("act", "activation", out=sb_.ap, in_=state3[:, g, :], func=AF.Copy, reads=[(state.res, g)], writes=[sb_.res])
        bo = rot()
        op("pe", "matmul", C.ps[bo], lhsT=CT3[:, g, cs_], rhs=sb_.ap, start=True, stop=True, reads=[CT.res, sb_.res], writes=[C.pres[bo]])
        op("dve", "tensor_tensor", out=yo.ap.rearrange("p (h d) -> p h d", d=64), in0=C.ps[bo].rearrange("p (h d) -> p h d", d=64),
           in1=v3(ea)[:, c, hs].unsqueeze(2).broadcast_to([128, 8, 64]), op=ALU.mult, reads=[C.pres[bo], ea.res], writes=[yo.res])
        bd = rot()
        for h in range(8):
            op("pe", "matmul", C.ps[bd][:, h * 64:(h + 1) * 64], lhsT=Mb3[:, h, :], rhs=xdt.ap[:, h * 64:(h + 1) * 64], start=True,
               stop=True, reads=[Mb.res, xdt.res], writes=[C.pres[bd]])
        op("dve", "tensor_tensor", out=yy.ap, in0=C.ps[bd], in1=yo.ap, op=ALU.add, reads=[C.pres[bd], yo.res], writes=[yy.res])
        op("dve", "tensor_tensor", out=xd.ap.rearrange("p (h d) -> p h d", d=64), in0=xv,
           in1=dbc.ap[:, hs].unsqueeze(2).broadcast_to([128, 8, 64]), op=ALU.mult, reads=[xtok.res, dbc.res], writes=[xd.res])
        op("dve", "tensor_tensor", out=yy.ap, in0=yy.ap, in1=xd.ap, op=ALU.add, reads=[yy.res, xd.res], writes=[yy.res])
        op("dve", "tensor_tensor", out=yy.ap, in0=yy.ap, in1=zs3[:, c, :], op=ALU.mult, reads=[yy.res, zs.res], writes=[yy.res])
        op("act", "activation", out=sq.ap, in_=yy.ap, func=AF.Square, accum_out=rs.ap[:, 0:1], reads=[yy.res], writes=[sq.res, rs.res])
        op("act", "activation", out=rs.ap[:, 1:2], in_=rs.ap[:, 0:1], func=AF.Sqrt, scale=1.0 / 512, bias=C.eps.ap[:, 0:1],
           reads=[rs.res], writes=[rs.res])
        op("dve", "reciprocal", out=rs.ap[:, 2:3], in_=rs.ap[:, 1:2], reads=[rs.res], writes=[rs.res])
        op("dve", "tensor_scalar", out=ynb.ap, in0=yy.ap, scalar1=rs.ap[:, 2:3], scalar2=None, op0=ALU.mult, reads=[yy.res, rs.res],
           writes=[ynb.res])
        bk = rot()
        for j in range(4):
            op("pe", "transpose", out=C.psb[bk][:, j * 128:(j + 1) * 128], in_=ynb.ap[:, j * 128:(j + 1) * 128], identity=C.ident.ap,
               reads=[ynb.res, C.ident.res], writes=[C.pres[bk]])
        for j in range(4):
            op("act", "activation", out=yT3[:, g * 4 + j, cs_], in_=C.psb[bk][:, j * 128:(j + 1) * 128], func=AF.Copy,
               scale=ng.ap[:, g * 4 + j:g * 4 + j + 1], reads=[C.pres[bk], ng.res], writes=[(yT.res, g)])
        bs = rot()
        op("pe", "matmul", C.ps[bs], lhsT=Btok4[:, c, g, :], rhs=xw.ap, start=True, stop=True, reads=[Btok.res, xw.res], writes=[C.pres[bs]])
        st3 = state3[:, g, :].rearrange("p (h d) -> p h d", d=64)
        op("dve", "tensor_tensor", out=st3, in0=st3, in1=v3(cdec)[:, c, hs].unsqueeze(2).broadcast_to([128, 8, 64]), op=ALU.mult,
           reads=[(state.res, g), cdec.res], writes=[(state.res, g)])
        op("dve", "tensor_tensor", out=state3[:, g, :], in0=state3[:, g, :], in1=C.ps[bs], op=ALU.add,
           reads=[(state.res, g), C.pres[bs]], writes=[(state.res, g)])

    PW = {}

    def proj_quarter(g, q):
        gb_ = GB[g % 2]
        xTg, xTg3, xtok, xtok3, zs, zs3 = gb_["xTg"], gb_["xTg3"], gb_["xtok"], gb_["xtok3"], gb_["zs"], gb_["zs3"]
        if q == 0:
            PW[g] = (load_w(w_in[:, :, 4096 + g * 512:4096 + (g + 1) * 512], NKC), load_w(w_in[:, :, g * 512:(g + 1) * 512], NKC))
        (wtx, wvx), (wtz, wvz) = PW[g]
        j = q
        bk = rot()
        for kc in range(NKC):
            op("pe", "matmul", C.ps[bk], lhsT=wvx[:, kc, j * 128:(j + 1) * 128], rhs=hnT3[:, kc, :], start=(kc == 0),
               stop=(kc == NKC - 1), reads=[hnT.res, wtx.res], writes=[C.pres[bk]])
        conv_chunk(bk, g * 4 + j, xTg3[:, j, :], xTg.res)
        c = q
        bk = rot()
        for kc in range(NKC):
            op("pe", "matmul", C.ps[bk], lhsT=hnT3[:, kc, c * 128:(c + 1) * 128], rhs=wvz[:, kc, :], start=(kc == 0),
               stop=(kc == NKC - 1), reads=[hnT.res, wtz.res], writes=[C.pres[bk]])
        op("act", "activation", out=zs3[:, c, :], in_=C.ps[bk], func=AF.Silu, reads=[C.pres[bk]], writes=[zs.res])
        if q == 3:
            for c in range(4):
                bk = rot()
                for j in range(4):
                    op("pe", "transpose", out=C.psb[bk][:, j * 128:(j + 1) * 128], in_=xTg3[:, j, c * 128:(c + 1) * 128],
                       identity=C.ident.ap, reads=[xTg.res, C.ident.res], writes=[C.pres[bk]])
                op("act", "activation", out=xtok3[:, c, :], in_=C.psb[bk][:, 0:512], func=AF.Copy, reads=[C.pres[bk]], writes=[xtok.res])

    for tt in range(T // TT):
        rows = slice(tt * TT, (tt + 1) * TT)
        norm_tile(C, src[rows, :], gbc, hnT, ("mixx", tt))
        for piece in range(4):
            wt, wv = load_w(w_in[:, :, 8192 + piece * 512:8192 + (piece + 1) * 512], NKC)
            for j in range(4):
                fcl = piece * 4 + j
                bk = rot()
                for kc in range(NKC):
                    op("pe", "matmul", C.ps[bk], lhsT=wv[:, kc, j * 128:(j + 1) * 128], rhs=hnT3[:, kc, :], start=(kc == 0),
                       stop=(kc == NKC - 1), reads=[hnT.res, wt.res], writes=[C.pres[bk]])
                if fcl < 8:
                    conv_chunk(bk, 32 + fcl, BT3[:, fcl, :], BT.res)
                else:
                    conv_chunk(bk, 32 + fcl, CT3[:, fcl - 8, :], CT.res)
        for c in range(4):
            bk = rot()
            for g in range(8):
                op("pe", "transpose", out=C.psb[bk][:, g * 128:(g + 1) * 128], in_=BT3[:, g, c * 128:(c + 1) * 128],
                   identity=C.ident.ap, reads=[BT.res, C.ident.res], writes=[C.pres[bk]])
            op("act", "activation", out=Btok4[:, c], in_=C.psb[bk].rearrange("p (g n) -> p g n", g=8), func=AF.Copy,
               reads=[C.pres[bk]], writes=[Btok.res])
        wt, wv = load_w(w_in[:, :, 10240:10304], NKC, 64)
        bk = rot()
        for c in range(4):
            for kc in range(NKC):
                op("pe", "matmul", C.ps[bk][:, c * 64:(c + 1) * 64], lhsT=hnT3[:, kc, c * 128:(c + 1) * 128], rhs=wv[:, kc, :],
                   start=(kc == 0), stop=(kc == NKC - 1), reads=[hnT.res, wt.res], writes=[C.pres[bk]])
        op("dve", "tensor_tensor", out=v3(dtt), in0=C.ps[bk][:, 0:256].rearrange("p (c h) -> p c h", c=4),
           in1=dtb.ap.unsqueeze(1).broadcast_to([128, 4, 64]), op=ALU.add, reads=[C.pres[bk], dtb.res], writes=[dtt.res])
        op("act", "activation", out=dtt.ap, in_=dtt.ap, func=AF.Exp, reads=[dtt.res], writes=[dtt.res])
        op("dve", "tensor_scalar", out=dtt.ap, in0=dtt.ap, scalar1=1.0, scalar2=None, op0=ALU.add, reads=[dtt.res], writes=[dtt.res])
        op("act", "activation", out=dtt.ap, in_=dtt.ap, func=AF.Ln, reads=[dtt.res], writes=[dtt.res])
        op("dve", "tensor_tensor", out=v3(adt), in0=v3(dtt), in1=abc.ap.unsqueeze(1).broadcast_to([128, 4, 64]), op=ALU.mult,
           reads=[dtt.res, abc.res], writes=[adt.res])
        bk = rot()
        op("pe", "matmul", C.ps[bk][:, 0:256], lhsT=trif, rhs=adt.ap, start=True, stop=True, reads=[cf.res, adt.res], writes=[C.pres[bk]])
        op("pe", "matmul", C.ps[bk][:, 256:512], lhsT=onesf, rhs=adt.ap, start=True, stop=True, reads=[cf.res, adt.res], writes=[C.pres[bk]])
        op("act", "activation", out=acum.ap, in_=C.ps[bk][:, 0:256], func=AF.Copy, reads=[C.pres[bk]], writes=[acum.res])
        op("act", "activation", out=asum.ap, in_=C.ps[bk][:, 256:512], func=AF.Copy, reads=[C.pres[bk]], writes=[asum.res])
        op("act", "activation", out=ea.ap, in_=acum.ap, func=AF.Exp, reads=[acum.res], writes=[ea.res])
        op("act", "activation", out=cdec.ap, in_=asum.ap, func=AF.Exp, reads=[asum.res], writes=[cdec.res])
        op("dve", "tensor_tensor", out=wdec.ap, in0=asum.ap, in1=acum.ap, op=ALU.subtract, reads=[asum.res, acum.res], writes=[wdec.res])
        op("act", "activation", out=wdec.ap, in_=wdec.ap, func=AF.Exp, reads=[wdec.res], writes=[wdec.res])
        op("dve", "tensor_tensor", out=wdec.ap, in0=wdec.ap, in1=dtt.ap, op=ALU.mult, reads=[wdec.res, dtt.res], writes=[wdec.res])
        items = [(g_, c_) for g_ in range(8) for c_ in range(4)]
        for q_ in range(4):
            proj_quarter(0, q_)
        build_lm(0, 0, LM[0])
        for idx, (g, c) in enumerate(items):
            if idx + 1 < len(items):
                build_lm(items[idx + 1][0], items[idx + 1][1], LM[(idx + 1) % 2])
            y_phase(g, c, LM[idx % 2])
            if g + 1 < 8:
                proj_quarter(g + 1, c)
        wout = W["m_w_out"][o].rearrange("(kc p) f -> p kc f", p=128)
        yrd = [(yT.res, g_) for g_ in range(8)]
        for ds in range(4):
            w0t, w0v = load_w(wout[:, 0:16, ds * 512:(ds + 1) * 512], NKC)
            w1t, w1v = load_w(wout[:, 16:32, ds * 512:(ds + 1) * 512], NKC)
            for tq in range(4):
                bk = rot()
                for kc in range(32):
                    wv_ = w0v if kc < 16 else w1v
                    wt_ = w0t if kc < 16 else w1t
                    op("pe", "matmul", C.ps[bk], lhsT=yT3[:, kc, tq * 128:(tq + 1) * 128], rhs=wv_[:, kc % 16, :], start=(kc == 0), stop=(kc == 31),
                       reads=[wt_.res] + yrd, writes=[C.pres[bk]])
                xb = xs[nxs[0] % 4]
                nxs[0] += 1
                r0 = tt * TT + tq * 128
                dma("sp", xb.ap, src[r0:r0 + 128, ds * 512:(ds + 1) * 512], writes=[xb.res])
                op("dve", "tensor_tensor", out=xb.ap, in0=C.ps[bk], in1=xb.ap, op=ALU.add, reads=[C.pres[bk], xb.res], writes=[xb.res])
                dma("sp", dst[r0:r0 + 128, ds * 512:(ds + 1) * 512], xb.ap, reads=[xb.res])


WNAMES = ["norm_ffn1", "ffn1_gate", "ffn1_up", "ffn1_down", "norm_mix", "norm_ffn2", "ffn2_gate", "ffn2_up",
          "ffn2_down", "ev_w_in", "ev_sinks", "s5_a_re", "s5_a_im", "s5_log_dt", "s5_b_re", "s5_b_im",
          "s5_c_re", "s5_c_im", "s5_d", "s5_w_glu", "s5_b_glu", "ev_w_out", "m_w_in", "m_conv_w", "m_conv_b",
          "m_dt_bias", "m_a_log", "m_d", "m_norm", "m_w_out", "final_norm"]


def build_program(T, shapes, stages, stop_at=""):
    nc = bass.Bass("TRN2", target_bir_lowering=False)
    C = Ctx()
    C.stop_at = stop_at
    C.nc = nc
    C.T = T
    W = {n: nc.dram_tensor(n, list(shapes[n]), F32, kind="ExternalInput").ap() for n in WNAMES}
    xin = nc.dram_tensor("x", [T, D], F32, kind="ExternalInput").ap()
    pos = nc.dram_tensor("positions", [T], I32, kind="ExternalInput").ap()
    cst = nc.dram_tensor("cst", [128, 1024], F32, kind="ExternalInput").ap()
    y = nc.dram_tensor("y", [T, D], F32, kind="ExternalOutput").ap()
    xres = nc.dram_tensor("xres", [T, D], F32, kind="Internal").ap()
    C.W, C.pos, C.cst = W, pos, cst
    C.tabs = nc.dram_tensor("tabs", [32, 2, 128, 512], F32, kind="Internal").ap()
    C.lhsd = nc.dram_tensor("lhsd", [128, 4 * 32 * 128], BF16, kind="Internal").ap()
    C.smalld = nc.dram_tensor("smalld", [128, 64], F32, kind="Internal").ap()
    S = Sched(nc)
    C.S = S
    with contextlib.ExitStack() as st:
        arena_t = st.enter_context(nc.sbuf_tensor("arena", [128, 206 * 1024], U8))
        C.A = Arena(arena_t[:], 206 * 1024)
        pst = [st.enter_context(nc.psum_tensor(f"ps{i}", [128, 512], F32)) for i in range(8)]
        C.ps = [p[:] for p in pst]
        C.psb = [p[:].bitcast(BF16) for p in pst]
        C.pres = [f"psum{i}" for i in range(8)]
        C.bank_i = 0

        def next_bank():
            b = C.bank_i % 8
            C.bank_i += 1
            return b
        C.next_bank = next_bank
        C.cnt_norm = 0
        A = C.A
        cf = A.alloc("cstf", 1024, F32)
        S.dma("sp", cf.ap, cst, writes=[cf.res])
        C.cstf = cf
        C.ident = A.alloc("ident", 128, BF16)
        S.op("dve", "tensor_copy", out=C.ident.ap, in_=cf.ap[:, 0:128], reads=[cf.res], writes=[C.ident.res])
        C.eps = A.alloc("eps", 1, F32)
        S.op("dve", "tensor_copy", out=C.eps.ap, in_=cf.ap[:, 128:129], reads=[cf.res], writes=[C.eps.res])
        A.set_floor()
        C.last_out = None
        finals = []
        cur_src = xin
        for stg in stages:
            kind = stg[0]
            if kind == "ffn":
                l, which = stg[1], stg[2]
                ffn_stage(C, cur_src, xres, W[f"norm_ffn{which}"][l], W[f"ffn{which}_gate"][l],
                          W[f"ffn{which}_up"][l], W[f"ffn{which}_down"][l], tag=("ffn", l, which))
                cur_src = xres
            elif kind == "mix":
                l = stg[1]
                if l % 2 == 0:
                    even_stage(C, l, cur_src, xres)
                else:
                    odd_stage(C, l, cur_src, xres)
                cur_src = xres
            elif kind == "final":
                finals = final_norm_stage(C, cur_src, y, W["final_norm"])
            elif kind == "copyout":
                S.barrier()
                finals = [S.dma("sp", y, cur_src)]
        S.emit(final_waits=finals)
    return nc


def make_consts():
    c = np.zeros((128, 1024), np.float32)
    c[:, 0:128] = np.eye(128, dtype=np.float32)
    c[:, 128] = EPS
    p = np.arange(128)
    for g2 in range(2):
        c[:, 130 + g2] = (p // 64 == g2)
        c[:, 136 + g2] = ((p // 16) % 2 == g2)
    for jj in range(4):
        c[:, 132 + jj] = (p // 32 == jj)
    c[:, 140:148] = np.exp(-math.log(500000.0) * np.arange(8, dtype=np.float32) * (2.0 / 16)).astype(np.float32)[None, :]
    kj = p[:, None]
    qi = p[None, :]
    c[:, 256:384] = (kj <= qi)
    c[:, 384:512] = (kj > qi)
    c[:, 512:640] = np.where(kj > qi, -30000.0, 0.0)
    c[:, 640:768] = 1.0
    return c


def kernel(**inputs):
    x = np.ascontiguousarray(inputs["x"], dtype=np.float32)
    B, Sq, _ = x.shape
    MIXERS_READY = True
    stages = []
    for l in range(DEPTH):
        stages.append(("ffn", l, 1))
        if MIXERS_READY:
            stages.append(("mix", l))
        stages.append(("ffn", l, 2))
    stages.append(("final",))
    shapes = {n: inputs[n].shape for n in WNAMES}
    nc = build_program(Sq, shapes, stages)
    cst = make_consts()
    wmap = {n: np.ascontiguousarray(inputs[n], dtype=np.float32) for n in WNAMES}
    in_maps = []
    for b in range(B):
        m = dict(wmap)
        m["x"] = x[b]
        m["positions"] = np.ascontiguousarray(inputs["positions"][b], dtype=np.int32)
        m["cst"] = cst
        in_maps.append(m)
    res = run_bass_kernel_spmd(nc, in_maps, core_ids=list(range(B)))
    return np.stack([np.asarray(r["y"]) for r in res.results], axis=0).astype(np.float32)
```
